# Optimizing a Trainium2 kernel written in Bass

```python
import math
import jax
import jax.numpy as jnp
from jax import lax
import numpy as np

D_MODEL = 1024
BATCH = 2
SEQ = 16384
DEPTH = 4

DN_HEADS = 4
DN_HEAD_DIM = 128
DN_WIDTH = DN_HEADS * DN_HEAD_DIM
DN_CONV = 4
DN_CHUNK = 64
S5_WIDTH = 512
S5_GROUP = 16
S5_GROUPS = S5_WIDTH // S5_GROUP
S5_STATE = 64
LRU_WIDTH = 512
LRU_BLOCKS = 8
LRU_BLOCK = LRU_WIDTH // LRU_BLOCKS
LRU_CONV = 4
LRU_C = 8.0
N_BRANCH = 3
BRANCH_WIDTH = 512
D_FF = 3 * D_MODEL
FFN_CONV = 3
DEEPNORM_ALPHA = (2.0 * DEPTH) ** 0.25
DEEPNORM_BETA = (8.0 * DEPTH) ** -0.25
LN_EPS = 1e-5
RMS_EPS = 1e-6
L2_EPS = 1e-6

IN_SIZES = (DN_WIDTH, DN_WIDTH, DN_WIDTH, DN_WIDTH, DN_HEADS, DN_HEADS, S5_WIDTH, LRU_WIDTH, LRU_WIDTH, N_BRANCH * D_MODEL)
D_IN = sum(IN_SIZES)

kernel_name = 'hybrid_deltanet_s5_rglru_deepnorm'


def causal_dwconv(x, w, b=None):
    k, c = w.shape
    y = lax.conv_general_dilated(x, w[:, None, :].astype(x.dtype), window_strides=(1,), padding=[(k - 1, 0)], dimension_numbers=('NWC', 'WIO', 'NWC'), feature_group_count=c)
    if b is not None:
        y = y + b.astype(x.dtype)
    return y


def layer_norm(x, g, b):
    xf = x.astype(jnp.float32)
    mu = jnp.mean(xf, -1, keepdims=True)
    var = jnp.mean(jnp.square(xf - mu), -1, keepdims=True)
    return ((xf - mu) * lax.rsqrt(var + LN_EPS) * g.astype(jnp.float32) + b.astype(jnp.float32)).astype(x.dtype)


def _affine_combine(left, right):
    a_l, b_l = left
    a_r, b_r = right
    return a_r * a_l, a_r * b_l + b_r


def linear_scan(a, b):
    return lax.associative_scan(_affine_combine, (a, b), axis=1)[1]


def l2norm(t):
    return t * lax.rsqrt(jnp.sum(jnp.square(t), -1, keepdims=True) + L2_EPS)


def gated_deltanet(q, k, v, z, beta_logit, decay_logit, conv_w, a_log, dt_bias, norm_w):
    out_dtype = z.dtype
    bsz, seq, _ = q.shape
    h, dh, c = DN_HEADS, DN_HEAD_DIM, DN_CHUNK
    n = seq // c
    f32 = jnp.float32
    qkv = jax.nn.silu(causal_dwconv(jnp.concatenate([q, k, v], axis=-1), conv_w))
    qkv = qkv.astype(f32).reshape(bsz, seq, 3, h, dh)
    q = l2norm(qkv[:, :, 0]) * (dh ** -0.5)
    k = l2norm(qkv[:, :, 1])
    v = qkv[:, :, 2]
    beta = jax.nn.sigmoid(beta_logit.astype(f32))
    g = -jnp.exp(a_log.astype(f32)) * jax.nn.softplus(decay_logit.astype(f32) + dt_bias.astype(f32))

    def to_chunks(t):
        return t.reshape(bsz, n, c, h, -1).transpose(0, 3, 1, 2, 4)

    qc, kc, vc = to_chunks(q), to_chunks(k), to_chunks(v)
    bc = to_chunks(beta[..., None])[..., 0]
    gc = jnp.cumsum(to_chunks(g[..., None])[..., 0], axis=-1)
    causal = jnp.tril(jnp.ones((c, c), dtype=bool))
    strict = jnp.tril(jnp.ones((c, c), dtype=bool), -1)
    decay = jnp.exp(jnp.where(causal, gc[..., :, None] - gc[..., None, :], -jnp.inf))
    kb = kc * bc[..., None]
    lower = jnp.where(strict, jnp.einsum('bhncd,bhnjd->bhncj', kb, kc) * decay, 0.0)
    tmat = lower + jnp.eye(c, dtype=f32)
    rhs = jnp.concatenate([vc * bc[..., None], kb * jnp.exp(gc)[..., None]], axis=-1)
    sol = lax.linalg.triangular_solve(tmat, rhs, left_side=True, lower=True, unit_diagonal=True)
    u_c, w_c = sol[..., :dh], sol[..., dh:]
    attn = jnp.einsum('bhncd,bhnjd->bhncj', qc, kc) * decay
    q_dec = qc * jnp.exp(gc)[..., None]
    k_tail = kc * jnp.exp(gc[..., -1:] - gc)[..., None]
    chunk_decay = jnp.exp(gc[..., -1])

    def step(state, inp):
        u_i, w_i, qd_i, a_i, kt_i, cd_i = inp
        v_new = u_i - jnp.einsum('bhcd,bhde->bhce', w_i, state)
        o_i = jnp.einsum('bhcd,bhde->bhce', qd_i, state) + jnp.einsum('bhcj,bhje->bhce', a_i, v_new)
        state = state * cd_i[..., None, None] + jnp.einsum('bhcd,bhce->bhde', kt_i, v_new)
        return state, o_i

    xs = tuple(jnp.moveaxis(t, 2, 0) for t in (u_c, w_c, q_dec, attn, k_tail, chunk_decay))
    _, o = lax.scan(step, jnp.zeros((bsz, h, dh, dh), f32), xs)
    o = o.transpose(1, 0, 3, 2, 4).reshape(bsz, seq, h, dh)
    o = o * lax.rsqrt(jnp.mean(jnp.square(o), -1, keepdims=True) + RMS_EPS) * norm_w.astype(f32)
    o = o * jax.nn.silu(z.astype(f32).reshape(bsz, seq, h, dh))
    return o.reshape(bsz, seq, DN_WIDTH).astype(out_dtype)


def s5_branch(u, lam_re, lam_im, log_step, b_re, b_im, c_re, c_im, d_skip, w_glu, b_glu):
    bsz, seq, _ = u.shape
    f32 = jnp.float32
    uf = u.astype(f32).reshape(bsz, seq, S5_GROUPS, S5_GROUP)
    lam = lax.complex(lam_re.astype(f32), lam_im.astype(f32))
    delta = jnp.exp(log_step.astype(f32))[:, None]
    lam_bar = jnp.exp(lam * delta)
    b_bar = ((lam_bar - 1.0) / lam)[..., None] * lax.complex(b_re.astype(f32), b_im.astype(f32))
    bu = jnp.einsum('gpc,bsgc->bsgp', b_bar, uf.astype(jnp.complex64))
    states = linear_scan(jnp.broadcast_to(lam_bar, bu.shape), bu)
    cmat = lax.complex(c_re.astype(f32), c_im.astype(f32))
    y = jnp.einsum('gcp,bsgp->bsgc', cmat, states).real + d_skip.astype(f32) * uf
    y = jax.nn.gelu(y.reshape(bsz, seq, S5_WIDTH))
    y = y * jax.nn.sigmoid(y @ w_glu.astype(f32) + b_glu.astype(f32))
    return y.astype(u.dtype)


def rglru_branch(xb, yb, conv_w, conv_b, w_a, b_a, w_x, b_x, lam):
    bsz, seq, _ = xb.shape
    f32 = jnp.float32
    xc = causal_dwconv(xb, conv_w, conv_b).astype(f32).reshape(bsz, seq, LRU_BLOCKS, LRU_BLOCK)
    r = jax.nn.sigmoid(jnp.einsum('bsnc,ncd->bsnd', xc, w_a.astype(f32)) + b_a.astype(f32))
    i = jax.nn.sigmoid(jnp.einsum('bsnc,ncd->bsnd', xc, w_x.astype(f32)) + b_x.astype(f32))
    log_a = LRU_C * r * jax.nn.log_sigmoid(lam.astype(f32).reshape(LRU_BLOCKS, LRU_BLOCK))
    a = jnp.exp(log_a)
    mult = jnp.sqrt(-jnp.expm1(2.0 * log_a))
    mult = jnp.where((jnp.arange(seq) == 0)[None, :, None, None], 1.0, mult)
    hseq = linear_scan(a, mult * i * xc).reshape(bsz, seq, LRU_WIDTH)
    return (hseq * jax.nn.gelu(yb.astype(f32))).astype(xb.dtype)


def conv_geglu_ffn(x, w_up, conv_w, conv_b, w_down):
    hid = causal_dwconv(x @ w_up, conv_w, conv_b)
    gate, val = jnp.split(hid, 2, axis=-1)
    return (jax.nn.gelu(gate) * val) @ w_down


def setup_inputs(seed: int = 0):
    key = jax.random.key(seed)
    ks = iter(jax.random.split(key, 40))
    f32 = jnp.float32
    L = DEPTH

    def nrm(shape, scale):
        return jax.random.normal(next(ks), shape, f32) * scale

    def unif(shape, lo, hi):
        return jax.random.uniform(next(ks), shape, f32, lo, hi)

    x = nrm((BATCH, SEQ, D_MODEL), 1.0)
    w_in = nrm((L, D_MODEL, D_IN), D_MODEL ** -0.5)
    dn_conv_w = nrm((L, DN_CONV, 3 * DN_WIDTH), DN_CONV ** -0.5)
    dn_a_log = jnp.log(unif((L, DN_HEADS), 1.0, 16.0))
    dt = jnp.exp(unif((L, DN_HEADS), math.log(1e-3), math.log(1e-1)))
    dn_dt_bias = dt + jnp.log(-jnp.expm1(-dt))
    dn_norm_w = 1.0 + nrm((L, DN_HEAD_DIM), 0.02)
    s5_lam_re = -0.5 + nrm((L, S5_GROUPS, S5_STATE), 0.01)
    s5_lam_im = jnp.tile(jnp.pi * jnp.arange(S5_STATE, dtype=f32), (L, S5_GROUPS, 1))
    s5_log_step = unif((L, S5_GROUPS), math.log(1e-3), math.log(1e-1))
    s5_b_re = nrm((L, S5_GROUPS, S5_STATE, S5_GROUP), (2.0 * S5_GROUP) ** -0.5)
    s5_b_im = nrm((L, S5_GROUPS, S5_STATE, S5_GROUP), (2.0 * S5_GROUP) ** -0.5)
    s5_c_re = nrm((L, S5_GROUPS, S5_GROUP, S5_STATE), (2.0 * S5_STATE) ** -0.5)
    s5_c_im = nrm((L, S5_GROUPS, S5_GROUP, S5_STATE), (2.0 * S5_STATE) ** -0.5)
    s5_d = nrm((L, S5_GROUPS, S5_GROUP), 1.0)
    s5_w_glu = nrm((L, S5_WIDTH, S5_WIDTH), S5_WIDTH ** -0.5)
    s5_b_glu = nrm((L, S5_WIDTH), 0.01)
    lru_conv_w = nrm((L, LRU_CONV, LRU_WIDTH), LRU_CONV ** -0.5)
    lru_conv_b = nrm((L, LRU_WIDTH), 0.01)
    lru_w_a = nrm((L, LRU_BLOCKS, LRU_BLOCK, LRU_BLOCK), LRU_BLOCK ** -0.5)
    lru_b_a = nrm((L, LRU_BLOCKS, LRU_BLOCK), 0.01)
    lru_w_x = nrm((L, LRU_BLOCKS, LRU_BLOCK, LRU_BLOCK), LRU_BLOCK ** -0.5)
    lru_b_x = nrm((L, LRU_BLOCKS, LRU_BLOCK), 0.01)
    a0 = unif((L, LRU_WIDTH), 0.9, 0.999) ** (1.0 / LRU_C)
    lru_lam = jnp.log(a0) - jnp.log1p(-a0)
    w_branch = nrm((L, N_BRANCH, BRANCH_WIDTH, D_MODEL), BRANCH_WIDTH ** -0.5)
    b_gate = nrm((L, N_BRANCH, D_MODEL), 0.01)
    w_out = nrm((L, D_MODEL, D_MODEL), D_MODEL ** -0.5 * DEEPNORM_BETA)
    ln1_g = 1.0 + nrm((L, D_MODEL), 0.02)
    ln1_b = nrm((L, D_MODEL), 0.01)
    ffn_w_up = nrm((L, D_MODEL, 2 * D_FF), D_MODEL ** -0.5)
    ffn_conv_w = nrm((L, FFN_CONV, 2 * D_FF), FFN_CONV ** -0.5)
    ffn_conv_b = nrm((L, 2 * D_FF), 0.01)
    ffn_w_down = nrm((L, D_FF, D_MODEL), D_FF ** -0.5 * DEEPNORM_BETA)
    ln2_g = 1.0 + nrm((L, D_MODEL), 0.02)
    ln2_b = nrm((L, D_MODEL), 0.01)
    return {'x': x, 'w_in': w_in, 'dn_conv_w': dn_conv_w, 'dn_a_log': dn_a_log, 'dn_dt_bias': dn_dt_bias, 'dn_norm_w': dn_norm_w, 's5_lam_re': s5_lam_re, 's5_lam_im': s5_lam_im, 's5_log_step': s5_log_step, 's5_b_re': s5_b_re, 's5_b_im': s5_b_im, 's5_c_re': s5_c_re, 's5_c_im': s5_c_im, 's5_d': s5_d, 's5_w_glu': s5_w_glu, 's5_b_glu': s5_b_glu, 'lru_conv_w': lru_conv_w, 'lru_conv_b': lru_conv_b, 'lru_w_a': lru_w_a, 'lru_b_a': lru_b_a, 'lru_w_x': lru_w_x, 'lru_b_x': lru_b_x, 'lru_lam': lru_lam, 'w_branch': w_branch, 'b_gate': b_gate, 'w_out': w_out, 'ln1_g': ln1_g, 'ln1_b': ln1_b, 'ffn_w_up': ffn_w_up, 'ffn_conv_w': ffn_conv_w, 'ffn_conv_b': ffn_conv_b, 'ffn_w_down': ffn_w_down, 'ln2_g': ln2_g, 'ln2_b': ln2_b}


def reference(x, w_in, dn_conv_w, dn_a_log, dn_dt_bias, dn_norm_w, s5_lam_re, s5_lam_im, s5_log_step, s5_b_re, s5_b_im, s5_c_re, s5_c_im, s5_d, s5_w_glu, s5_b_glu, lru_conv_w, lru_conv_b, lru_w_a, lru_b_a, lru_w_x, lru_b_x, lru_lam, w_branch, b_gate, w_out, ln1_g, ln1_b, ffn_w_up, ffn_conv_w, ffn_conv_b, ffn_w_down, ln2_g, ln2_b):
    bsz, seq, _ = x.shape
    offsets = np.cumsum(IN_SIZES)[:-1].tolist()
    for l in range(DEPTH):
        proj = x @ w_in[l]
        dq, dk, dv, dz, dbeta, ddecay, s5_u, lru_x, lru_y, gate_logits = jnp.split(proj, offsets, axis=-1)
        o_dn = gated_deltanet(dq, dk, dv, dz, dbeta, ddecay, dn_conv_w[l], dn_a_log[l], dn_dt_bias[l], dn_norm_w[l])
        o_s5 = s5_branch(s5_u, s5_lam_re[l], s5_lam_im[l], s5_log_step[l], s5_b_re[l], s5_b_im[l], s5_c_re[l], s5_c_im[l], s5_d[l], s5_w_glu[l], s5_b_glu[l])
        o_lru = rglru_branch(lru_x, lru_y, lru_conv_w[l], lru_conv_b[l], lru_w_a[l], lru_b_a[l], lru_w_x[l], lru_b_x[l], lru_lam[l])
        branches = jnp.stack([o_dn, o_s5, o_lru], axis=2)
        branch_d = jnp.einsum('bsnc,ncd->bsnd', branches, w_branch[l])
        gates = jax.nn.sigmoid(gate_logits.reshape(bsz, seq, N_BRANCH, D_MODEL) + b_gate[l])
        mixed = jnp.sum(gates * branch_d, axis=2) @ w_out[l]
        x = layer_norm(DEEPNORM_ALPHA * x + mixed, ln1_g[l], ln1_b[l])
        f = conv_geglu_ffn(x, ffn_w_up[l], ffn_conv_w[l], ffn_conv_b[l], ffn_w_down[l])
        x = layer_norm(DEEPNORM_ALPHA * x + f, ln2_g[l], ln2_b[l])
    return x
```

```python
import contextlib
import numpy as np
import concourse.bass as bass
import concourse.mybir as mybir
from concourse.bass_utils import run_bass_kernel_spmd

F32 = mybir.dt.float32
BF16 = mybir.dt.bfloat16
AF = mybir.ActivationFunctionType
OP = mybir.AluOpType

D = 1024
SEQ = 16384
BATCH = 2
DEPTH = 4
NCORE = 8
ALPHA = (2.0 * DEPTH) ** 0.25
NMIX = 3592


class Sched:
    ENGS = ("pe", "dve", "act", "pool", "sp")
    NLANE = 8

    def __init__(self, nc, stack, same_sync=("dve", "act", "pool")):
        self.nc = nc
        self.ops = {e: [] for e in self.ENGS}
        self.last_w = {}
        self.readers = {}
        self.same_sync = same_sync
        self.sem = {e: stack.enter_context(nc.semaphore("s_" + e)) for e in ("pe", "dve", "act", "pool")}
        self.lanes = {}
        for q in ("sp", "pool", "act"):
            self.lanes[q] = [stack.enter_context(nc.semaphore("d_%s%d" % (q, i))) for i in range(self.NLANE)]
        self.lane_cnt = {q: [0] * self.NLANE for q in self.lanes}
        self.lane_rr = {q: 0 for q in self.lanes}
        self.ccount = {e: 0 for e in self.ENGS}

    def op(self, eng, fn, reads=(), writes=(), dma=False):
        deps = set()
        for k in reads:
            if k in self.last_w:
                deps.add(self.last_w[k])
        for k in writes:
            if k in self.last_w:
                deps.add(self.last_w[k])
            for r in self.readers.get(k, ()):
                deps.add(r)
        if dma:
            q = eng
            ln = self.lane_rr[q]
            self.lane_rr[q] = (ln + 1) % self.NLANE
            self.lane_cnt[q][ln] += 16
            me = ("dma", q, ln, self.lane_cnt[q][ln])
        else:
            self.ccount[eng] += 1
            me = ("c", eng, self.ccount[eng])
        deps.discard(me)
        for k in writes:
            self.last_w[k] = me
            self.readers[k] = []
        for k in reads:
            self.readers.setdefault(k, []).append(me)
        self.ops[eng].append((fn, deps, me))

    def emit(self):
        nc = self.nc
        sched = self

        def run(eng, e):
            waited = {}
            for (fn, deps, me) in sched.ops[eng]:
                need = {}
                for d in deps:
                    if d[0] == "dma":
                        s, v = sched.lanes[d[1]][d[2]], d[3]
                    else:
                        if d[1] == eng and me[0] == "c" and eng not in sched.same_sync:
                            continue
                        s, v = sched.sem[d[1]], d[2]
                    key = id(s)
                    if need.get(key, (None, 0))[1] < v:
                        need[key] = (s, v)
                for key, (s, v) in need.items():
                    if waited.get(key, 0) < v:
                        e.wait_ge(s, v)
                        waited[key] = v
                ins = fn(e)
                if me[0] == "dma":
                    ins.then_inc(sched.lanes[me[1]][me[2]], 16)
                else:
                    ins.then_inc(sched.sem[eng], 1)
            if eng in sched.lanes:
                for ln, s in enumerate(sched.lanes[eng]):
                    if sched.lane_cnt[eng][ln] > 0:
                        e.wait_ge(s, sched.lane_cnt[eng][ln])

        with nc.Block() as block:
            @block.tensor
            def _(e):
                run("pe", e)

            @block.vector
            def _(e):
                run("dve", e)

            @block.scalar
            def _(e):
                run("act", e)

            @block.gpsimd
            def _(e):
                run("pool", e)

            @block.sync
            def _(e):
                run("sp", e)


class Ctx:
    def __init__(self, name="k"):
        self.nc = bass.Bass("TRN2", target_bir_lowering=False)
        self.stack = contextlib.ExitStack()
        self.s = Sched(self.nc, self.stack)
        self.nps = 0
        self.uid = 0
        self.banks = [self.stack.enter_context(self.nc.psum_tensor("ps%d" % i, [128, 512], F32)) for i in range(8)]
        self.bank_rr = 0

    def sb(self, name, shape, dt=F32):
        return self.stack.enter_context(self.nc.sbuf_tensor("sb_" + name, list(shape), dt))

    def din(self, name, shape, dt=F32):
        return self.nc.dram_tensor(name, list(shape), dt, kind="ExternalInput").ap()

    def dout(self, name, shape, dt=F32):
        return self.nc.dram_tensor(name, list(shape), dt, kind="ExternalOutput").ap()

    def bank(self, lo=0, hi=6):
        b = lo + (self.bank_rr % (hi - lo))
        self.bank_rr += 1
        return b

    def finish(self):
        self.s.emit()
        self.stack.close()
        return self.nc


def _rr(seq, state=[0]):
    state[0] += 1
    return seq[state[0] % len(seq)]


def stream_linear(c, W, K, cols, rhs, ntok, epi, wbufs, cast_engs=("act", "pool"), banks=(0, 6)):
    s = c.s
    KT = K // 128
    KTg = min(KT, 8)
    KG = KT // KTg
    Wv = W.rearrange("(kt p) n -> p kt n", p=128)
    i = 0
    while i < len(cols):
        blk = [cols[i]]
        for cc in cols[i + 1:i + 4]:
            if cc[0] == blk[-1][0] + blk[-1][1]:
                blk.append(cc)
            else:
                break
        c0 = blk[0][0]
        ncol = sum(b_[1] for b_ in blk)
        bks = [c.bank(*banks) for _ in blk]
        for kg in range(KG):
            bi = wbufs["rr"] % 2
            wbufs["rr"] += 1
            st, bf = wbufs["st"][bi], wbufs["bf"][bi]
            stv = st[:, 0:KTg * ncol].rearrange("p (kt n) -> p kt n", kt=KTg)
            bfv = bf[:, 0:KTg * ncol].rearrange("p (kt n) -> p kt n", kt=KTg)
            s.op("sp", lambda e, o=stv, i_=Wv[:, kg * KTg:(kg + 1) * KTg, c0:c0 + ncol]: e.dma_start(out=o, in_=i_),
                 writes=[("wst", bi)], dma=True)
            ce = _rr(cast_engs)
            o_cp(c, ce, bf[:, 0:KTg * ncol], st[:, 0:KTg * ncol], [("wst", bi)], [("wbf", bi)])
            off = 0
            for j, (col0, nc_) in enumerate(blk):
                ps = c.banks[bks[j]][0:nc_, 0:ntok]
                for k in range(KTg):
                    kt = kg * KTg + k
                    r_ap, r_key = rhs(kt)
                    s.op("pe", lambda e, o=ps, l=bfv[:, k, off:off + nc_], r=r_ap, a=(kt == 0), z=(kt == KT - 1):
                         e.matmul(o, l, r, start=a, stop=z),
                         reads=[("wbf", bi), r_key], writes=[("ps", bks[j])])
                off += nc_
        for j, (col0, nc_) in enumerate(blk):
            epi(i + j, c.banks[bks[j]][0:nc_, 0:ntok], ("ps", bks[j]))
        i += len(blk)


def make_wbufs(c):
    return {"st": [c.sb("wst%d" % i, [128, 4096], F32) for i in range(2)],
            "bf": [c.sb("wbf%d" % i, [128, 4096], BF16) for i in range(2)], "rr": 0}


def build_KA(TOK=4096, NW=NMIX):
    c = Ctx()
    s = c.s
    xT = c.din("xT", [D, TOK])
    w = c.din("w", [D, NW])
    out = c.dout("pT", [NW, TOK])
    xbf = c.sb("xbf", [128, 8, TOK], BF16)
    xst = [c.sb("xst%d" % i, [128, 1024], F32) for i in range(2)]
    ost = [c.sb("ost%d" % i, [128, 512], F32) for i in range(4)]
    wb = make_wbufs(c)
    xv = xT.rearrange("(kt p) t -> p kt t", p=128)
    n = 0
    for kt in range(8):
        for t0 in range(0, TOK, 1024):
            bi = n % 2
            s.op("sp", lambda e, o=xst[bi][:, :], i_=xv[:, kt, t0:t0 + 1024]: e.dma_start(out=o, in_=i_),
                 writes=[("xst", bi)], dma=True)
            s.op("dve", lambda e, o=xbf[:, kt, t0:t0 + 1024], i_=xst[bi][:, :]: e.tensor_copy(o, i_),
                 reads=[("xst", bi)], writes=[("xbf", kt, t0 // 512), ("xbf", kt, t0 // 512 + 1)])
            n += 1
    cols = [(c0, min(128, NW - c0)) for c0 in range(0, NW, 128)]
    cnt = [0]
    for tt in range(TOK // 512):
        def rhs(kt, tt=tt):
            return xbf[:, kt, tt * 512:(tt + 1) * 512], ("xbf", kt, tt)

        def epi(i, ps, pkey, tt=tt):
            k = cnt[0] % 4
            cnt[0] += 1
            nr = cols[i][1]
            eng = "act" if cnt[0] % 2 else "dve"
            if eng == "act":
                s.op("act", lambda e, o=ost[k][0:nr, :], i_=ps: e.copy(o, i_), reads=[pkey], writes=[("ost", k)])
            else:
                s.op("dve", lambda e, o=ost[k][0:nr, :], i_=ps: e.tensor_copy(o, i_), reads=[pkey], writes=[("ost", k)])
            s.op("sp", lambda e, o=out[cols[i][0]:cols[i][0] + nr, tt * 512:(tt + 1) * 512], i_=ost[k][0:nr, :]:
                 e.dma_start(out=o, in_=i_), reads=[("ost", k)], dma=True)
        stream_linear(c, w, D, cols, rhs, 512, epi, wb)
    return c.finish()


def o_tt(c, eng, out, a, b, op, r, w):
    c.s.op(eng, lambda e: e.tensor_tensor(out, a, b, op), reads=r, writes=w)


def o_ts(c, eng, out, a, s1, s2, op0, op1, r, w):
    if s2 is None:
        c.s.op(eng, lambda e: e.tensor_scalar(out, a, s1, None, op0), reads=r, writes=w)
    else:
        c.s.op(eng, lambda e: e.tensor_scalar(out, a, s1, s2, op0, op1), reads=r, writes=w)


def o_stt(c, eng, out, a, sc, b, op0, op1, r, w):
    eng = "dve"
    c.s.op(eng, lambda e: e.scalar_tensor_tensor(out, a, sc, b, op0, op1), reads=r, writes=w)


def o_act(c, out, a, func, r, w, bias=None, scale=None):
    kw = {}
    if bias is not None:
        kw["bias"] = bias
    if scale is not None:
        kw["scale"] = scale
    c.s.op("act", lambda e: e.activation(out, a, func, **kw), reads=r, writes=w)


def o_cp(c, eng, out, a, r, w):
    if eng == "act":
        c.s.op("act", lambda e: e.copy(out, a), reads=r, writes=w)
    else:
        c.s.op(eng, lambda e: e.tensor_copy(out, a), reads=r, writes=w)


def o_mm(c, out, l, rr, r, w, start=True, stop=True):
    c.s.op("pe", lambda e: e.matmul(out, l, rr, start=start, stop=stop), reads=r, writes=w)


def o_tr(c, out, a, ident, r, w):
    c.s.op("pe", lambda e: e.transpose(out, a, ident), reads=r, writes=w)


def o_dma(c, out, a, r=(), w=(), q="sp"):
    c.s.op(q, lambda e: e.dma_start(out=out, in_=a), reads=r, writes=w, dma=True)


def o_scan(c, out, d0, d1, init, r, w):
    c.s.op("dve", lambda e: e.tensor_tensor_scan(out, d0, d1, init, OP.mult, OP.add), reads=r, writes=w)


def o_memset(c, eng, out, v, w):
    c.s.op(eng, lambda e: e.memset(out, v), writes=w)


TWO_PI = float(2 * np.pi)


def sin_cos(c, th_ap, r, temps, ki, kq):
    outs = []
    for idx, shift in ((0, 0.0), (1, float(np.pi / 2))):
        (a, ka), (kf, kk), (res, kr) = temps[3 * idx: 3 * idx + 3]
        o_ts(c, "dve", a, th_ap, shift, None, OP.add, None, r, [ka])
        o_ts(c, "dve", kf, a, 1.0 / TWO_PI, None, OP.mult, None, [ka], [kk])
        o_cp(c, "dve", ki, kf, [kk], [kq])
        o_cp(c, "dve", kf, ki, [kq], [kk])
        o_stt(c, "dve", a, kf, -TWO_PI, a, OP.mult, OP.add, [kk, ka], [ka])
        o_ts(c, "dve", a, a, float(np.pi), float(-np.pi), OP.min, OP.max, [ka], [ka])
        o_act(c, res, a, AF.Sin, [ka], [kr])
        outs.append((res, kr))
    return outs[0], outs[1]


TC = 512
NPC = 36
S5POOL = "pool"


def build_KB(S=SEQ, do=("lru", "s5", "dn"), dbg=False):
    c = Ctx()
    s = c.s
    NSC = S // TC
    qkvz = c.din("qkvz", [4, 128, S])
    bd = c.din("bd", [2, S])
    s5u = c.din("s5u", [128, S])
    lrx = c.din("lrx", [128, S])
    lry = c.din("lry", [128, S])
    pcol_d = c.din("pcol", [128, NPC])
    lruw_d = c.din("lruw", [128, 2, 128])
    s5row_d = c.din("s5row", [128, 3, 512])
    s5b_d = c.din("s5b", [128, 2, 4, 128])
    s5c_d = c.din("s5c", [128, 2, 4, 128])
    ident_d = c.din("identd", [128, 128])
    cmask_d = c.din("cmask", [128, TC])
    c64_d = c.din("c64", [64, 3, 512])
    sel_d = c.din("sel", [2, 4])
    out = c.dout("o", [3, 128, S])

    pcol = c.sb("pcol", [128, NPC])
    o_dma(c, pcol[:], pcol_d, w=["pcol"])
    ident = c.sb("ident", [128, 128])
    o_dma(c, ident[:], ident_d, w=["ident"])
    ones = c.sb("ones", [128, 128])
    o_memset(c, "pool", ones[:], 1.0, ["ones"])

    def col(i):
        return pcol[:, i:i + 1]

    if "lru" in do:
        lruw = c.sb("lruw", [128, 2, 128])
        o_dma(c, lruw[:], lruw_d, w=["lruw"])
        c8 = c.sb("c8", [128, 1])
        o_act(c, c8[:], col(22), AF.Exp, ["pcol"], ["c8"], scale=-1.0)
        o_act(c, c8[:], c8[:], AF.Ln, ["c8"], ["c8"], bias=1.0)
        o_ts(c, "dve", c8[:], c8[:], -8.0, None, OP.mult, None, ["c8"], ["c8"])
        lru_h = c.sb("lru_h", [128, 1])
        o_memset(c, "dve", lru_h[:], 0.0, ["lru_h"])
        L_xin = [c.sb("L_xin%d" % i, [128, 3 + TC]) for i in range(2)]
        L_yin = [c.sb("L_yin%d" % i, [128, TC]) for i in range(2)]
        L_t = [c.sb("L_t%d" % i, [128, TC]) for i in range(6)]

    def lru_chunk(sc):
        t0 = sc * TC
        bi = sc % 2
        xin, yin = L_xin[bi], L_yin[bi]
        kx, ky = ("L_xin", bi), ("L_yin", bi)
        if sc == 0:
            o_memset(c, "pool", xin[:, 0:3], 0.0, [kx])
            o_dma(c, xin[:, 3:3 + TC], lrx[:, 0:TC], w=[kx])
        else:
            o_dma(c, xin[:, :], lrx[:, t0 - 3:t0 + TC], w=[kx])
        o_dma(c, yin[:, :], lry[:, t0:t0 + TC], w=[ky])
        xc, r_, i_, a_, m_, h_ = [t[:, :] for t in L_t]
        K = ["L_t%d" % i for i in range(6)]
        o_ts(c, "pool", xc, xin[:, 0:TC], col(15), col(19), OP.mult, OP.add, [kx, "pcol"], [K[0]])
        for k in (1, 2, 3):
            o_stt(c, "pool", xc, xin[:, k:k + TC], col(15 + k), xc, OP.mult, OP.add, [kx, "pcol", K[0]], [K[0]])
        b1, b2 = c.bank(), c.bank()
        o_mm(c, c.banks[b1][:, 0:TC], lruw[:, 0, :], xc, ["lruw", K[0]], [("ps", b1)])
        o_mm(c, c.banks[b2][:, 0:TC], lruw[:, 1, :], xc, ["lruw", K[0]], [("ps", b2)])
        o_act(c, r_, c.banks[b1][:, 0:TC], AF.Sigmoid, [("ps", b1), "pcol"], [K[1]], bias=col(20))
        o_act(c, i_, c.banks[b2][:, 0:TC], AF.Sigmoid, [("ps", b2), "pcol"], [K[2]], bias=col(21))
        o_act(c, a_, r_, AF.Exp, [K[1], "c8"], [K[3]], scale=c8[:, 0:1])
        o_tt(c, "pool", m_, a_, a_, OP.mult, [K[3]], [K[4]])
        o_act(c, m_, m_, AF.Sqrt, [K[4]], [K[4]], bias=1.0, scale=-1.0)
        if sc == 0:
            o_memset(c, "pool", m_[:, 0:1], 1.0, [K[4]])
        o_tt(c, "pool", m_, m_, i_, OP.mult, [K[4], K[2]], [K[4]])
        o_tt(c, "pool", m_, m_, xc, OP.mult, [K[4], K[0]], [K[4]])
        o_scan(c, h_, a_, m_, lru_h[:, 0:1], [K[3], K[4], "lru_h"], [K[5]])
        o_cp(c, "dve", lru_h[:, 0:1], h_[:, TC - 1:TC], [K[5]], ["lru_h"])
        o_act(c, r_, yin[:, :], AF.Gelu_apprx_tanh, [ky], [K[1]])
        o_tt(c, "pool", r_, r_, h_, OP.mult, [K[1], K[5]], [K[1]])
        o_dma(c, out[2, :, t0:t0 + TC], r_, r=[K[1]])

    if "s5" in do:
        dl = c.sb("s5_dl", [128, 4])
        o_act(c, dl[:], pcol[:, 31:35], AF.Exp, ["pcol"], ["s5_dl"])
        rr = c.sb("s5_r", [128, 4])
        o_tt(c, "dve", rr[:], pcol[:, 23:27], dl[:], OP.mult, ["pcol", "s5_dl"], ["s5_r"])
        o_act(c, rr[:], rr[:], AF.Exp, ["s5_r"], ["s5_r"])
        th = c.sb("s5_th", [128, 4])
        o_tt(c, "dve", th[:], pcol[:, 27:31], dl[:], OP.mult, ["pcol", "s5_dl"], ["s5_th"])
        sct = [c.sb("s5ct%d" % i, [128, 4]) for i in range(6)]
        scki = c.sb("s5cki", [128, 4], mybir.dt.int32)
        (sn, ksn), (cs, kcs) = sin_cos(c, th[:], ["s5_th"], [(t_[:], "s5ct%d" % i) for i, t_ in enumerate(sct)], scki[:], "s5cki")
        Ct = c.sb("s5_Ct", [128, 4, TC])
        St = c.sb("s5_St", [128, 4, TC])
        o_memset(c, "dve", Ct[:, :, 0:1], 1.0, ["s5_Ct"])
        o_memset(c, "dve", St[:, :, 0:1], 0.0, ["s5_St"])
        cw = c.sb("s5_cw", [128, 4])
        sw = c.sb("s5_sw", [128, 4])
        nsw = c.sb("s5_nsw", [128, 4])
        tq = c.sb("s5_tq", [128, 4])
        o_cp(c, "dve", cw[:], cs, [kcs], ["s5_cw"])
        o_cp(c, "dve", sw[:], sn, [ksn], ["s5_sw"])
        tmpT = c.sb("s5_tmpT", [128, TC // 2])
        w_ = 1
        while w_ < TC:
            o_ts(c, "dve", nsw[:], sw[:], -1.0, None, OP.mult, None, ["s5_sw"], ["s5_nsw"])
            for m in range(4):
                o_ts(c, "dve", tmpT[:, 0:w_], Ct[:, m, 0:w_], cw[:, m:m + 1], None, OP.mult, None, ["s5_Ct", "s5_cw"], ["s5_tmpT"])
                o_stt(c, "dve", Ct[:, m, w_:2 * w_], St[:, m, 0:w_], nsw[:, m:m + 1], tmpT[:, 0:w_], OP.mult, OP.add,
                      ["s5_St", "s5_nsw", "s5_tmpT"], ["s5_Ct"])
                o_ts(c, "dve", tmpT[:, 0:w_], St[:, m, 0:w_], cw[:, m:m + 1], None, OP.mult, None, ["s5_St", "s5_cw"], ["s5_tmpT"])
                o_stt(c, "dve", St[:, m, w_:2 * w_], Ct[:, m, 0:w_], sw[:, m:m + 1], tmpT[:, 0:w_], OP.mult, OP.add,
                      ["s5_Ct", "s5_sw", "s5_tmpT"], ["s5_St"])
            o_tt(c, "dve", tq[:], sw[:], sw[:], OP.mult, ["s5_sw"], ["s5_tq"])
            o_tt(c, "dve", sw[:], cw[:], sw[:], OP.mult, ["s5_cw", "s5_sw"], ["s5_sw"])
            o_ts(c, "dve", sw[:], sw[:], 2.0, None, OP.mult, None, ["s5_sw"], ["s5_sw"])
            o_tt(c, "dve", cw[:], cw[:], cw[:], OP.mult, ["s5_cw"], ["s5_cw"])
            o_tt(c, "dve", cw[:], cw[:], tq[:], OP.subtract, ["s5_cw", "s5_tq"], ["s5_cw"])
            w_ *= 2
        o_ts(c, "dve", nsw[:], sw[:], -1.0, None, OP.mult, None, ["s5_sw"], ["s5_nsw"])
        row = c.sb("s5row", [128, 3, 512])
        o_dma(c, row[:], s5row_d, w=["s5row"])
        S_w = [[c.sb("S_w%d_%d" % (p, i), [128, TC]) for i in range(8)] for p in range(2)]

        def SW(p, i):
            return S_w[p][i][:, :], ("S_w", p, i)
        (dlr, kdlr), (er, ker), (thr, kthr), (nr, knr), (ni, kni), (den, kden), (t2, kt2), (fr, kfr) = [SW(0, i) for i in range(8)]
        (fi, kfi) = SW(1, 0)
        rki = c.sb("s5rki", [128, 512], mybir.dt.int32)
        o_act(c, dlr, row[:, 2, :], AF.Exp, ["s5row"], [kdlr])
        o_tt(c, "dve", er, row[:, 0, :], dlr, OP.mult, ["s5row", kdlr], [ker])
        o_act(c, er, er, AF.Exp, [ker], [ker])
        o_tt(c, "dve", thr, row[:, 1, :], dlr, OP.mult, ["s5row", kdlr], [kthr])
        (snr, ksnr), (csr, kcsr) = sin_cos(c, thr, [kthr], [SW(1, i) for i in range(1, 7)], rki[:], "s5rki")
        o_tt(c, "dve", nr, er, csr, OP.mult, [ker, kcsr], [knr])
        o_ts(c, "dve", nr, nr, -1.0, None, OP.add, None, [knr], [knr])
        o_tt(c, "dve", ni, er, snr, OP.mult, [ker, ksnr], [kni])
        o_tt(c, "dve", den, row[:, 0, :], row[:, 0, :], OP.mult, ["s5row"], [kden])
        o_tt(c, "dve", t2, row[:, 1, :], row[:, 1, :], OP.mult, ["s5row"], [kt2])
        o_tt(c, "dve", t2, den, t2, OP.add, [kden, kt2], [kt2])
        c.s.op("dve", lambda e: e.reciprocal(den, t2), reads=[kt2], writes=[kden])
        o_tt(c, "dve", fr, nr, row[:, 0, :], OP.mult, [knr, "s5row"], [kfr])
        o_tt(c, "dve", t2, ni, row[:, 1, :], OP.mult, [kni, "s5row"], [kt2])
        o_tt(c, "dve", fr, fr, t2, OP.add, [kfr, kt2], [kfr])
        o_tt(c, "dve", fr, fr, den, OP.mult, [kfr, kden], [kfr])
        o_tt(c, "dve", fi, ni, row[:, 0, :], OP.mult, [kni, "s5row"], [kfi])
        o_tt(c, "dve", t2, nr, row[:, 1, :], OP.mult, [knr, "s5row"], [kt2])
        o_tt(c, "dve", fi, fi, t2, OP.subtract, [kfi, kt2], [kfi])
        o_tt(c, "dve", fi, fi, den, OP.mult, [kfi, kden], [kfi])
        Bin = c.sb("s5_Bin", [128, 2, 512])
        o_dma(c, Bin[:], s5b_d.rearrange("p a m n -> p a (m n)"), w=["s5_Bin"])
        Bb = c.sb("s5_Bb", [128, 2, 512])
        o_tt(c, "dve", Bb[:, 0, :], fr, Bin[:, 0, :], OP.mult, [kfr, "s5_Bin"], ["s5_Bb"])
        o_tt(c, "dve", t2, fi, Bin[:, 1, :], OP.mult, [kfi, "s5_Bin"], [kt2])
        o_tt(c, "dve", Bb[:, 0, :], Bb[:, 0, :], t2, OP.subtract, ["s5_Bb", kt2], ["s5_Bb"])
        o_tt(c, "dve", Bb[:, 1, :], fr, Bin[:, 1, :], OP.mult, [kfr, "s5_Bin"], ["s5_Bb"])
        o_tt(c, "dve", t2, fi, Bin[:, 0, :], OP.mult, [kfi, "s5_Bin"], [kt2])
        o_tt(c, "dve", Bb[:, 1, :], Bb[:, 1, :], t2, OP.add, ["s5_Bb", kt2], ["s5_Bb"])
        Cm = c.sb("s5_Cm", [128, 2, 512])
        o_dma(c, Cm[:], s5c_d.rearrange("p a m n -> p a (m n)"), w=["s5_Cm"])
        o_ts(c, "dve", Cm[:, 1, :], Cm[:, 1, :], -1.0, None, OP.mult, None, ["s5_Cm"], ["s5_Cm"])
        s5_init = c.sb("s5_init", [128, 2, 4])
        o_memset(c, "dve", s5_init[:], 0.0, ["s5_init"])
        s5_tc = c.sb("s5_tc", [128, 4])
        if dbg:
            for nm, t_, shp, keys in (("Ct", Ct, [128, 4, TC], ["s5_Ct"]), ("St", St, [128, 4, TC], ["s5_St"]),
                                      ("Bb", Bb, [128, 2, 512], ["s5_Bb"]), ("rr", rr, [128, 4], ["s5_r"]),
                                      ("cw", cw, [128, 4], ["s5_cw"]), ("sw", sw, [128, 4], ["s5_sw"]),
                                      ("Cm", Cm, [128, 2, 512], ["s5_Cm"])):
                o_dma(c, c.dout("dbg_" + nm, shp), t_[:], r=keys)
        S_uin = [c.sb("S_uin%d" % i, [128, TC]) for i in range(2)]
        S_y = c.sb("S_y", [128, TC])

    def s5_chunk(sc):
        t0 = sc * TC
        bi = sc % 2
        uin = S_uin[bi]
        ku = ("S_uin", bi)
        o_dma(c, uin[:, :], s5u[:, t0:t0 + TC], w=[ku])
        by = 7
        for m in range(4):
            W = S_w[m % 2]
            KW = [("S_w", m % 2, i) for i in range(8)]
            b1, b2 = c.bank(), c.bank()
            pr, pi = c.banks[b1][:, 0:TC], c.banks[b2][:, 0:TC]
            o_mm(c, pr, Bb[:, 0, m * 128:(m + 1) * 128], uin[:, :], ["s5_Bb", ku], [("ps", b1)])
            o_mm(c, pi, Bb[:, 1, m * 128:(m + 1) * 128], uin[:, :], ["s5_Bb", ku], [("ps", b2)])
            Cj, Sj = Ct[:, m, :], St[:, m, :]
            t1, t2_, t3, t4, Xr, Xi, hr, hi = [t[:, :] for t in W]
            o_tt(c, "dve", t1, pr, Cj, OP.mult, [("ps", b1), "s5_Ct"], [KW[0]])
            o_tt(c, "dve", t2_, pi, Sj, OP.mult, [("ps", b2), "s5_St"], [KW[1]])
            o_tt(c, "dve", t3, pi, Cj, OP.mult, [("ps", b2), "s5_Ct"], [KW[2]])
            o_tt(c, "dve", t4, pr, Sj, OP.mult, [("ps", b1), "s5_St"], [KW[3]])
            o_tt(c, S5POOL, Xr, t1, t2_, OP.add, [KW[0], KW[1]], [KW[4]])
            o_tt(c, S5POOL, Xi, t3, t4, OP.subtract, [KW[2], KW[3]], [KW[5]])
            rb = rr[:, m:m + 1].to_broadcast([128, TC])
            o_scan(c, t1, rb, Xr, s5_init[:, 0, m:m + 1], ["s5_r", KW[4], "s5_init"], [KW[0]])
            o_scan(c, t3, rb, Xi, s5_init[:, 1, m:m + 1], ["s5_r", KW[5], "s5_init"], [KW[2]])
            o_ts(c, "dve", s5_tc[:, m:m + 1], t1[:, TC - 1:TC], cw[:, m:m + 1], None, OP.mult, None, [KW[0], "s5_cw"], ["s5_tc"])
            o_stt(c, "dve", s5_init[:, 0, m:m + 1], t3[:, TC - 1:TC], nsw[:, m:m + 1], s5_tc[:, m:m + 1], OP.mult, OP.add,
                  [KW[2], "s5_nsw", "s5_tc"], ["s5_init"])
            o_ts(c, "dve", s5_tc[:, m:m + 1], t3[:, TC - 1:TC], cw[:, m:m + 1], None, OP.mult, None, [KW[2], "s5_cw"], ["s5_tc"])
            o_stt(c, "dve", s5_init[:, 1, m:m + 1], t1[:, TC - 1:TC], sw[:, m:m + 1], s5_tc[:, m:m + 1], OP.mult, OP.add,
                  [KW[0], "s5_sw", "s5_tc"], ["s5_init"])
            o_tt(c, S5POOL, t2_, t1, Cj, OP.mult, [KW[0], "s5_Ct"], [KW[1]])
            o_tt(c, S5POOL, t4, t3, Sj, OP.mult, [KW[2], "s5_St"], [KW[3]])
            o_tt(c, "dve", hr, t2_, t4, OP.subtract, [KW[1], KW[3]], [KW[6]])
            o_tt(c, S5POOL, Xr, t1, Sj, OP.mult, [KW[0], "s5_St"], [KW[4]])
            o_tt(c, S5POOL, Xi, t3, Cj, OP.mult, [KW[2], "s5_Ct"], [KW[5]])
            o_tt(c, "dve", hi, Xr, Xi, OP.add, [KW[4], KW[5]], [KW[7]])
            o_mm(c, c.banks[by][:, 0:TC], Cm[:, 0, m * 128:(m + 1) * 128], hr, ["s5_Cm", KW[6]], [("ps", by)],
                 start=(m == 0), stop=False)
            o_mm(c, c.banks[by][:, 0:TC], Cm[:, 1, m * 128:(m + 1) * 128], hi, ["s5_Cm", KW[7]], [("ps", by)],
                 start=False, stop=(m == 3))
        o_stt(c, "dve", S_y[:, :], uin[:, :], col(35), c.banks[by][:, 0:TC], OP.mult, OP.add, [ku, "pcol", ("ps", by)], ["S_y"])
        o_act(c, S_y[:, :], S_y[:, :], AF.Gelu_apprx_tanh, ["S_y"], ["S_y"])
        o_dma(c, out[1, :, t0:t0 + TC], S_y[:, :], r=["S_y"])

    dn_chunk = build_dn(c, do, S, qkvz, bd, out, pcol, ident, ones, cmask_d, c64_d, sel_d)
    for sc in range(NSC):
        if "lru" in do:
            lru_chunk(sc)
        if "s5" in do:
            s5_chunk(sc)
        if "dn" in do:
            dn_chunk(sc)
    return c.finish()


def build_dn(c, do, S, qkvz, bd, out, pcol, ident, ones, cmask_d, c64_d, sel_d):
    if "dn" not in do:
        return None
    NCH = TC // 64

    def col(i):
        return pcol[:, i:i + 1]

    cmask = c.sb("cmask", [128, TC])
    o_dma(c, cmask[:], cmask_d, w=["cmask"])
    c64 = c.sb("c64", [64, 3, 512])
    o_dma(c, c64[:], c64_d, w=["c64"])
    sel = c.sb("sel", [2, 4])
    o_dma(c, sel[:], sel_d, w=["sel"])
    negmask, strict, ident8 = c64[:, 0, :], c64[:, 1, :], c64[:, 2, :]
    negA = c.sb("negA", [128, 1])
    o_act(c, negA[:], col(12), AF.Exp, ["pcol"], ["negA"])
    o_ts(c, "dve", negA[:], negA[:], -1.0, None, OP.mult, None, ["negA"], ["negA"])
    Sst = [c.sb("dnS%d" % i, [128, 128]) for i in range(2)]
    o_memset(c, "dve", Sst[0][:], 0.0, [("dnS", 0)])
    Xin = [[c.sb("D_in%d_%d" % (w, i), [128, 3 + TC]) for i in range(2)] for w in range(3)]
    Zin = [c.sb("D_z%d" % i, [128, TC]) for i in range(2)]
    Brow = [c.sb("D_br%d" % i, [1, TC]) for i in range(2)]
    Drow = [c.sb("D_dr%d" % i, [1, TC]) for i in range(2)]
    names = ["qc", "kc", "vc", "sq", "rs", "beta", "gc", "eg", "egl", "kb", "kbg", "vb", "kt", "qd", "oT", "wT", "zs", "tmp"]
    T = {n: c.sb("D_" + n, [128, TC]) for n in names}
    T64 = {n: c.sb("D64_" + n, [64, 512]) for n in ["dm", "AT", "U0", "U1", "L0", "L1", "X"]}
    TM = {n: c.sb("Dtm_" + n, [64, NCH, 128]) for n in ["kbg", "vb", "kt", "u"]}
    Vn = [c.sb("D_vn%d" % i, [64, 128]) for i in range(2)]
    G1 = c.sb("D_G1", [2, TC])
    G2 = c.sb("D_G2", [2, TC])

    def A(n):
        return T[n][:, :]

    def K(n):
        return "D_" + n

    def A64(n):
        return T64[n][:, :]

    def K64(n):
        return "D64_" + n

    def dn_chunk(sc):
        t0 = sc * TC
        bi = sc % 2
        for w in range(3):
            xin, kx = Xin[w][bi], ("D_in", w, bi)
            if sc == 0:
                o_memset(c, "pool", xin[:, 0:3], 0.0, [kx])
                o_dma(c, xin[:, 3:3 + TC], qkvz[w, :, 0:TC], w=[kx])
            else:
                o_dma(c, xin[:, :], qkvz[w, :, t0 - 3:t0 + TC], w=[kx])
        zin, kz = Zin[bi], ("D_z", bi)
        o_dma(c, zin[:, :], qkvz[3, :, t0:t0 + TC], w=[kz])
        brow, drow = Brow[bi], Drow[bi]
        kbr, kdr = ("D_br", bi), ("D_dr", bi)
        o_dma(c, brow[:, :], bd[0:1, t0:t0 + TC], w=[kbr])
        o_dma(c, drow[:, :], bd[1:2, t0:t0 + TC], w=[kdr])
        for w, nm in enumerate(["qc", "kc", "vc"]):
            xin, kx = Xin[w][bi], ("D_in", w, bi)
            o_ts(c, "pool", A(nm), xin[:, 0:TC], col(w * 4 + 0), None, OP.mult, None, [kx, "pcol"], [K(nm)])
            for k in (1, 2, 3):
                o_stt(c, "dve", A(nm), xin[:, k:k + TC], col(w * 4 + k), A(nm), OP.mult, OP.add, [kx, "pcol", K(nm)], [K(nm)])
            o_act(c, A(nm), A(nm), AF.Silu, [K(nm)], [K(nm)])
        for nm, scale in (("qc", 128.0 ** -0.5), ("kc", 1.0)):
            o_act(c, A("sq"), A(nm), AF.Square, [K(nm)], [K("sq")])
            b = c.bank()
            o_mm(c, c.banks[b][:, 0:TC], ones[:, :], A("sq"), ["ones", K("sq")], [("ps", b)])
            o_act(c, A("rs"), c.banks[b][:, 0:TC], AF.Sqrt, [("ps", b)], [K("rs")], bias=1e-6)
            c.s.op("dve", lambda e: e.reciprocal(A("rs"), A("rs")), reads=[K("rs")], writes=[K("rs")])
            o_stt(c, "dve", A(nm), A(nm), scale, A("rs"), OP.mult, OP.mult, [K(nm), K("rs")], [K(nm)])
        b = c.bank()
        o_mm(c, c.banks[b][:, 0:TC], ones[0:1, :], brow[:, :], ["ones", kbr], [("ps", b)])
        o_act(c, A("beta"), c.banks[b][:, 0:TC], AF.Sigmoid, [("ps", b)], [K("beta")])
        b = c.bank()
        o_mm(c, c.banks[b][:, 0:TC], ones[0:1, :], drow[:, :], ["ones", kdr], [("ps", b)])
        o_act(c, A("tmp"), c.banks[b][:, 0:TC], AF.Exp, [("ps", b), "pcol"], [K("tmp")], bias=col(13))
        o_act(c, A("tmp"), A("tmp"), AF.Ln, [K("tmp")], [K("tmp")], bias=1.0)
        o_ts(c, "dve", A("tmp"), A("tmp"), negA[:, 0:1], None, OP.mult, None, [K("tmp"), "negA"], [K("tmp")])
        o_scan(c, A("gc"), cmask[:, :], A("tmp"), 0.0, ["cmask", K("tmp")], [K("gc")])
        o_act(c, A("eg"), A("gc"), AF.Exp, [K("gc")], [K("eg")])
        gc3 = A("gc").rearrange("p (n c) -> p n c", c=64)
        o_tt(c, "dve", A("egl").rearrange("p (n c) -> p n c", c=64), gc3[:, :, 63:64].to_broadcast([128, NCH, 64]), gc3,
             OP.subtract, [K("gc")], [K("egl")])
        o_act(c, A("egl"), A("egl"), AF.Exp, [K("egl")], [K("egl")])
        o_tt(c, "dve", A("kb"), A("kc"), A("beta"), OP.mult, [K("kc"), K("beta")], [K("kb")])
        o_tt(c, "pool", A("kbg"), A("kb"), A("eg"), OP.mult, [K("kb"), K("eg")], [K("kbg")])
        o_tt(c, "pool", A("vb"), A("vc"), A("beta"), OP.mult, [K("vc"), K("beta")], [K("vb")])
        o_tt(c, "pool", A("kt"), A("kc"), A("egl"), OP.mult, [K("kc"), K("egl")], [K("kt")])
        o_tt(c, "pool", A("qd"), A("qc"), A("eg"), OP.mult, [K("qc"), K("eg")], [K("qd")])
        o_ts(c, "dve", G1[:, :], A("gc")[0:2, :], sel[:, 0:1], sel[:, 1:2], OP.mult, OP.add, [K("gc"), "sel"], ["D_G1"])
        o_ts(c, "dve", G2[:, :], A("gc")[0:2, :], sel[:, 2:3], sel[:, 3:4], OP.mult, OP.add, [K("gc"), "sel"], ["D_G2"])
        b = c.bank()
        for n in range(NCH):
            cs_ = slice(n * 64, (n + 1) * 64)
            o_mm(c, c.banks[b][0:64, cs_], G1[:, cs_], G2[:, cs_], ["D_G1", "D_G2"], [("ps", b)])
        o_stt(c, "dve", A64("dm"), c.banks[b][0:64, 0:512], 0.0, negmask, OP.min, OP.add, [("ps", b), "c64"], [K64("dm")])
        o_act(c, A64("dm"), A64("dm"), AF.Exp, [K64("dm")], [K64("dm")])
        b = c.bank()
        for n in range(NCH):
            cs_ = slice(n * 64, (n + 1) * 64)
            o_mm(c, c.banks[b][0:64, cs_], A("kc")[:, cs_], A("qc")[:, cs_], [K("kc"), K("qc")], [("ps", b)])
        o_tt(c, "dve", A64("AT"), c.banks[b][0:64, 0:512], A64("dm"), OP.mult, [("ps", b), K64("dm")], [K64("AT")])
        b = c.bank()
        for n in range(NCH):
            cs_ = slice(n * 64, (n + 1) * 64)
            o_mm(c, c.banks[b][0:64, cs_], A("kc")[:, cs_], A("kb")[:, cs_], [K("kc"), K("kb")], [("ps", b)])
        o_tt(c, "dve", A64("U0"), c.banks[b][0:64, 0:512], A64("dm"), OP.mult, [("ps", b), K64("dm")], [K64("U0")])
        o_tt(c, "pool", A64("U0"), A64("U0"), strict, OP.mult, [K64("U0"), "c64"], [K64("U0")])
        b = c.bank()
        for n in range(NCH):
            cs_ = slice(n * 64, (n + 1) * 64)
            o_tr(c, c.banks[b][0:64, cs_], A64("U0")[:, cs_], ident[0:64, 0:64], [K64("U0"), "ident"], [("ps", b)])
        o_cp(c, "act", A64("L0"), c.banks[b][0:64, 0:512], [("ps", b)], [K64("L0")])
        o_tt(c, "dve", A64("X"), ident8, A64("U0"), OP.subtract, ["c64", K64("U0")], [K64("X")])
        cu, cl = "U0", "L0"
        for lvl in range(5):
            nu, nl = ("U1", "L1") if cu == "U0" else ("U0", "L0")
            b = c.bank()
            for n in range(NCH):
                cs_ = slice(n * 64, (n + 1) * 64)
                o_mm(c, c.banks[b][0:64, cs_], A64(cl)[:, cs_], A64(cu)[:, cs_], [K64(cl), K64(cu)], [("ps", b)])
            o_cp(c, "act", A64(nu), c.banks[b][0:64, 0:512], [("ps", b)], [K64(nu)])
            b = c.bank()
            for n in range(NCH):
                cs_ = slice(n * 64, (n + 1) * 64)
                o_mm(c, c.banks[b][0:64, cs_], A64(cu)[:, cs_], A64(cl)[:, cs_], [K64(cl), K64(cu)], [("ps", b)])
            o_cp(c, "dve", A64(nl), c.banks[b][0:64, 0:512], [("ps", b)], [K64(nl)])
            b = c.bank()
            for n in range(NCH):
                cs_ = slice(n * 64, (n + 1) * 64)
                o_mm(c, c.banks[b][0:64, cs_], A64(nl)[:, cs_], A64("X")[:, cs_], [K64(nl), K64("X")], [("ps", b)])
            o_tt(c, "dve", A64("X"), A64("X"), c.banks[b][0:64, 0:512], OP.add, [K64("X"), ("ps", b)], [K64("X")])
            cu, cl = nu, nl
        ei = 0
        for nm in ("kbg", "vb", "kt"):
            for half in range(2):
                b = c.bank()
                for q in range(4):
                    n = half * 4 + q
                    o_tr(c, c.banks[b][0:64, q * 128:(q + 1) * 128], A(nm)[:, n * 64:(n + 1) * 64], ident[:, :],
                         [K(nm), "ident"], [("ps", b)])
                o_cp(c, "act" if ei % 2 == 0 else "dve", TM[nm][:, half * 4:(half + 1) * 4, :].rearrange("p a b -> p (a b)"),
                     c.banks[b][0:64, 0:512], [("ps", b)], [("Dtm", nm)])
                ei += 1
        for half in range(2):
            b = c.bank()
            for q in range(4):
                n = half * 4 + q
                o_mm(c, c.banks[b][0:64, q * 128:(q + 1) * 128], A64("X")[:, n * 64:(n + 1) * 64], TM["vb"][:, n, :],
                     [K64("X"), ("Dtm", "vb")], [("ps", b)])
            o_cp(c, "act", TM["u"][:, half * 4:(half + 1) * 4, :].rearrange("p a b -> p (a b)"), c.banks[b][0:64, 0:512],
                 [("ps", b)], [("Dtm", "u")])
        b = c.bank()
        for n in range(NCH):
            cs_ = slice(n * 64, (n + 1) * 64)
            o_mm(c, c.banks[b][:, cs_], TM["kbg"][:, n, :], A64("X")[:, cs_], [("Dtm", "kbg"), K64("X")], [("ps", b)])
        o_cp(c, "act", A("wT"), c.banks[b][:, 0:512], [("ps", b)], [K("wT")])
        bo = 6
        for n in range(NCH):
            gi = sc * NCH + n
            cur, nxt = Sst[gi % 2], Sst[(gi + 1) % 2]
            kcur, knxt = ("dnS", gi % 2), ("dnS", (gi + 1) % 2)
            vn, kvn = Vn[gi % 2], ("D_vn", gi % 2)
            cs_ = slice(n * 64, (n + 1) * 64)
            b1 = c.bank()
            o_mm(c, c.banks[b1][0:64, 0:128], A("wT")[:, cs_], cur[:, :], [K("wT"), kcur], [("ps", b1)])
            o_tt(c, "dve", vn[:, :], TM["u"][:, n, :], c.banks[b1][0:64, 0:128], OP.subtract, [("Dtm", "u"), ("ps", b1)], [kvn])
            o_mm(c, c.banks[bo][:, cs_], cur[:, :], A("qd")[:, cs_], [kcur, K("qd")], [("ps", bo)], start=True, stop=False)
            o_mm(c, c.banks[bo][:, cs_], vn[:, :], A64("AT")[:, cs_], [kvn, K64("AT")], [("ps", bo)], start=False, stop=True)
            b2 = c.bank()
            o_mm(c, c.banks[b2][:, 0:128], TM["kt"][:, n, :], vn[:, :], [("Dtm", "kt"), kvn], [("ps", b2)])
            o_stt(c, "dve", nxt[:, :], cur[:, :], A("eg")[:, n * 64 + 63:n * 64 + 64], c.banks[b2][:, 0:128], OP.mult, OP.add,
                  [kcur, K("eg"), ("ps", b2)], [knxt])
        o_cp(c, "act", A("oT"), c.banks[bo][:, 0:512], [("ps", bo)], [K("oT")])
        o_act(c, A("sq"), A("oT"), AF.Square, [K("oT")], [K("sq")])
        b = c.bank()
        o_mm(c, c.banks[b][:, 0:TC], ones[:, :], A("sq"), ["ones", K("sq")], [("ps", b)])
        o_act(c, A("rs"), c.banks[b][:, 0:TC], AF.Sqrt, [("ps", b)], [K("rs")], bias=1e-6, scale=1.0 / 128.0)
        c.s.op("dve", lambda e: e.reciprocal(A("rs"), A("rs")), reads=[K("rs")], writes=[K("rs")])
        o_tt(c, "dve", A("oT"), A("oT"), A("rs"), OP.mult, [K("oT"), K("rs")], [K("oT")])
        o_act(c, A("zs"), zin[:, :], AF.Silu, [kz], [K("zs")])
        o_stt(c, "dve", A("oT"), A("oT"), col(14), A("zs"), OP.mult, OP.mult, [K("oT"), "pcol", K("zs")], [K("oT")])
        o_dma(c, out[0, :, t0:t0 + TC], A("oT"), r=[K("oT")])

    return dn_chunk


HW = 16
NPC2 = 4 + 24 + 8 + 8 + 144 + 48 + 8 + 8
PC_BGLU, PC_BGATE, PC_L1G, PC_L1B, PC_CW, PC_CB, PC_L2G, PC_L2B = 0, 4, 28, 36, 44, 188, 236, 244


def build_KC(TOKC=4096):
    c = Ctx()
    s = c.s
    NT = TOKC // 512
    xT = c.din("xT", [D, HW + TOKC])
    brT = c.din("brT", [1536, HW + TOKC])
    flag_d = c.din("flag", [128, 1])
    w_gate = c.din("w_gate", [D, 3072])
    w_glu = c.din("w_glu", [512, 512])
    w_br = c.din("w_br", [1536, D])
    w_out = c.din("w_out", [D, D])
    w_up = c.din("w_up", [D, 6144])
    w_down = c.din("w_down", [3072, D])
    pc_d = c.din("pc", [128, NPC2])
    out = c.dout("x2T", [D, TOKC])

    pc = c.sb("pc", [128, NPC2])
    o_dma(c, pc[:], pc_d, w=["pc"])
    flag = c.sb("flag", [128, 1])
    o_dma(c, flag[:], flag_d, w=["flag"])
    onesm = c.sb("onesm", [128, 128])
    o_memset(c, "pool", onesm[:], 1.0 / D, ["onesm"])
    wb = make_wbufs(c)
    x_f = c.sb("x_f", [128, 8, 512])
    x_b = c.sb("x_b", [128, 8, 512], BF16)
    brst = c.sb("brst", [128, 4, 512])
    y_f = c.sb("y_f", [128, 4, 512])
    br_b = c.sb("br_b", [128, 12, 512], BF16)
    s5o = c.sb("s5o", [128, 4, 512], BF16)
    GA = c.sb("GA", [128, 24, 512], BF16)
    acc = c.sb("acc", [128, 8, 512])
    mx_b = c.sb("mx_b", [128, 8, 512], BF16)
    sq = [c.sb("sq%d" % i, [128, 512]) for i in range(2)]
    mean = c.sb("mean", [128, 512])
    var = c.sb("var", [128, 512])
    xc = [c.sb("xc%d" % i, [128, 512]) for i in range(2)]
    tmpf = [c.sb("tmpf%d" % i, [128, 512]) for i in range(2)]
    hst = [c.sb("hst%d" % i, [128, 2 + 512]) for i in range(2)]
    cv = [c.sb("cv%d" % i, [128, 512]) for i in range(2)]
    carry = c.sb("carry", [128, 48, 2])
    o_memset(c, "pool", carry[:], 0.0, ["carry"])
    xv = xT.rearrange("(kt p) t -> p kt t", p=128)
    bv = brT.rearrange("(kt p) t -> p kt t", p=128)
    ov = out.rearrange("(kt p) t -> p kt t", p=128)
    cnt = {"e": 0}

    def pcol(i):
        return pc[:, i:i + 1]

    def layer_norm(n, gi, bi_, make_bf):
        bA, bB = 6, 7
        for i in range(8):
            o_mm(c, c.banks[bA][:, 0:n], onesm[:, :], x_f[:, i, 0:n], ["onesm", ("x_f", i)], [("ps", bA)], start=(i == 0), stop=(i == 7))
        for i in range(8):
            q = sq[i % 2]
            o_act(c, q[:, 0:n], x_f[:, i, 0:n], AF.Square, [("x_f", i)], [("sq", i % 2)])
            o_mm(c, c.banks[bB][:, 0:n], onesm[:, :], q[:, 0:n], ["onesm", ("sq", i % 2)], [("ps", bB)], start=(i == 0), stop=(i == 7))
        o_cp(c, "act", mean[:, 0:n], c.banks[bA][:, 0:n], [("ps", bA)], ["mean"])
        o_tt(c, "dve", var[:, 0:n], mean[:, 0:n], mean[:, 0:n], OP.mult, ["mean"], ["var"])
        o_tt(c, "dve", var[:, 0:n], c.banks[bB][:, 0:n], var[:, 0:n], OP.subtract, [("ps", bB), "var"], ["var"])
        o_act(c, var[:, 0:n], var[:, 0:n], AF.Sqrt, ["var"], ["var"], bias=1e-5)
        c.s.op("dve", lambda e: e.reciprocal(var[:, 0:n], var[:, 0:n]), reads=["var"], writes=["var"])
        for i in range(8):
            t_ = xc[i % 2]
            o_tt(c, "dve", t_[:, 0:n], x_f[:, i, 0:n], mean[:, 0:n], OP.subtract, [("x_f", i), "mean"], [("xc", i % 2)])
            o_tt(c, "pool", t_[:, 0:n], t_[:, 0:n], var[:, 0:n], OP.mult, [("xc", i % 2), "var"], [("xc", i % 2)])
            o_act(c, x_f[:, i, 0:n], t_[:, 0:n], AF.Identity, [("xc", i % 2), "pc"], [("x_f", i)], bias=pcol(bi_ + i), scale=pcol(gi + i))
            if make_bf:
                o_cp(c, "pool", x_b[:, i, 0:n], x_f[:, i, 0:n], [("x_f", i)], [("x_b", i)])

    def tile(ti):
        halo = ti < 0
        n = HW if halo else 512
        tok0 = 0 if halo else HW + ti * 512
        o_dma(c, x_f[:, :, 0:n], xv[:, :, tok0:tok0 + n], w=[("x_f", i) for i in range(8)])
        for i in range(8):
            o_cp(c, "pool" if i % 2 else "dve", x_b[:, i, 0:n], x_f[:, i, 0:n], [("x_f", i)], [("x_b", i)])
        for g in range(3):
            dst = y_f if g == 1 else brst
            kd = "y_f" if g == 1 else "brst"
            o_dma(c, dst[:, :, 0:n], bv[:, g * 4:(g + 1) * 4, tok0:tok0 + n], w=[kd])
            for i in range(4):
                o_cp(c, "pool" if i % 2 else "act", br_b[:, g * 4 + i, 0:n], dst[:, i, 0:n], [kd], [("br_b", g * 4 + i)])
        def epi_glu(i, ps, pk):
            t_ = tmpf[i % 2]
            o_act(c, t_[:, 0:n], ps, AF.Sigmoid, [pk, "pc"], [("tmpf", i % 2)], bias=pcol(PC_BGLU + i))
            o_tt(c, "dve", s5o[:, i, 0:n], y_f[:, i, 0:n], t_[:, 0:n], OP.mult, ["y_f", ("tmpf", i % 2)], [("s5o", i)])
        stream_linear(c, w_glu, 512, [(i * 128, 128) for i in range(4)], lambda kt: (br_b[:, 4 + kt, 0:n], ("br_b", 4 + kt)), n, epi_glu, wb)
        def epi_gate(i, ps, pk):
            o_act(c, GA[:, i, 0:n], ps, AF.Sigmoid, [pk, "pc"], [("GA", i)], bias=pcol(PC_BGATE + i))
        stream_linear(c, w_gate, D, [(i * 128, 128) for i in range(24)], lambda kt: (x_b[:, kt, 0:n], ("x_b", kt)), n, epi_gate, wb)
        for g in range(3):
            def rhs(kt, g=g):
                if g == 1:
                    return s5o[:, kt, 0:n], ("s5o", kt)
                return br_b[:, g * 4 + kt, 0:n], ("br_b", g * 4 + kt)

            def epi_br(i, ps, pk, g=g):
                if g == 0:
                    o_tt(c, "dve", acc[:, i, 0:n], ps, GA[:, i, 0:n], OP.mult, [pk, ("GA", i)], [("acc", i)])
                else:
                    t_ = tmpf[i % 2]
                    o_tt(c, "dve", t_[:, 0:n], ps, GA[:, g * 8 + i, 0:n], OP.mult, [pk, ("GA", g * 8 + i)], [("tmpf", i % 2)])
                    if g == 1:
                        o_tt(c, "pool", acc[:, i, 0:n], acc[:, i, 0:n], t_[:, 0:n], OP.add, [("acc", i), ("tmpf", i % 2)], [("acc", i)])
                    else:
                        o_tt(c, "pool", mx_b[:, i, 0:n], acc[:, i, 0:n], t_[:, 0:n], OP.add, [("acc", i), ("tmpf", i % 2)], [("mx_b", i)])
            stream_linear(c, w_br[g * 512:(g + 1) * 512, :], 512, [(i * 128, 128) for i in range(8)], rhs, n, epi_br, wb)
        def epi_out(i, ps, pk):
            o_stt(c, "dve", x_f[:, i, 0:n], x_f[:, i, 0:n], ALPHA, ps, OP.mult, OP.add, [("x_f", i), pk], [("x_f", i)])
        stream_linear(c, w_out, D, [(i * 128, 128) for i in range(8)], lambda kt: (mx_b[:, kt, 0:n], ("mx_b", kt)), n, epi_out, wb)
        layer_norm(n, PC_L1G, PC_L1B, True)
        order = []
        for b_ in range(6):
            order += [4 * b_ + q for q in range(4)] + [24 + 4 * b_ + q for q in range(4)]
        cols = [(t_ * 128, 128) for t_ in order]

        def epi_up(idx, ps, pk):
            t_ = order[idx]
            h = hst[idx % 2]
            kh = ("hst", idx % 2)
            cvt = cv[idx % 2]
            kc = ("cv", idx % 2)
            o_cp(c, "pool", h[:, 0:2], carry[:, t_, :], ["carry"], [kh])
            o_cp(c, "act", h[:, 2:2 + n], ps, [pk], [kh])
            o_ts(c, "dve", cvt[:, 0:n], h[:, 0:n], pcol(PC_CW + t_), pcol(PC_CB + t_), OP.mult, OP.add, [kh, "pc"], [kc])
            o_stt(c, "dve", cvt[:, 0:n], h[:, 1:1 + n], pcol(PC_CW + 48 + t_), cvt[:, 0:n], OP.mult, OP.add, [kh, "pc", kc], [kc])
            o_stt(c, "dve", cvt[:, 0:n], h[:, 2:2 + n], pcol(PC_CW + 96 + t_), cvt[:, 0:n], OP.mult, OP.add, [kh, "pc", kc], [kc])
            if halo:
                o_ts(c, "pool", carry[:, t_, :], h[:, n:n + 2], flag[:, 0:1], None, OP.mult, None, [kh, "flag"], ["carry"])
            else:
                o_cp(c, "pool", carry[:, t_, :], h[:, n:n + 2], [kh], ["carry"])
                if t_ < 24:
                    o_act(c, GA[:, t_, 0:n], cvt[:, 0:n], AF.Gelu_apprx_tanh, [kc], [("GA", t_)])
                else:
                    o_tt(c, "pool", GA[:, t_ - 24, 0:n], GA[:, t_ - 24, 0:n], cvt[:, 0:n], OP.mult, [("GA", t_ - 24), kc], [("GA", t_ - 24)])
        stream_linear(c, w_up, D, cols, lambda kt: (x_b[:, kt, 0:n], ("x_b", kt)), n, epi_up, wb)
        if halo:
            return
        stream_linear(c, w_down, 3072, [(i * 128, 128) for i in range(8)], lambda kt: (GA[:, kt, 0:n], ("GA", kt)), n, epi_out, wb)
        layer_norm(n, PC_L2G, PC_L2B, False)
        o_dma(c, ov[:, :, ti * 512:(ti + 1) * 512], x_f[:, :, 0:n], r=[("x_f", i) for i in range(8)])

    tile(-1)
    for ti in range(NT):
        tile(ti)
    return c.finish()


def kc_params(inp, l):
    def cols(v):
        return np.ascontiguousarray(v.reshape(-1, 128).T)
    pc = np.concatenate([cols(inp["s5_b_glu"][l]), cols(inp["b_gate"][l].reshape(-1)), cols(inp["ln1_g"][l]), cols(inp["ln1_b"][l]),
                         cols(inp["ffn_conv_w"][l][0]), cols(inp["ffn_conv_w"][l][1]), cols(inp["ffn_conv_w"][l][2]),
                         cols(inp["ffn_conv_b"][l]), cols(inp["ln2_g"][l]), cols(inp["ln2_b"][l])], axis=1).astype(np.float32)
    return {"pc": np.ascontiguousarray(pc), "w_gate": np.ascontiguousarray(inp["w_in"][l][:, NMIX:]),
            "w_glu": inp["s5_w_glu"][l], "w_br": np.ascontiguousarray(inp["w_branch"][l].reshape(1536, D)),
            "w_out": inp["w_out"][l], "w_up": inp["ffn_w_up"][l], "w_down": inp["ffn_w_down"][l]}


def kb_consts():
    cm = np.ones((128, TC), np.float32)
    cm[:, ::64] = 0.0
    jj = np.arange(64)[:, None]
    cc = np.arange(64)[None, :]
    negmask = np.where(jj <= cc, 0.0, -30000.0).astype(np.float32)
    strict = (jj < cc).astype(np.float32)
    eye = np.eye(64, dtype=np.float32)
    c64 = np.stack([np.tile(negmask, (1, 8)), np.tile(strict, (1, 8)), np.tile(eye, (1, 8))], axis=1)
    sel = np.array([[0.0, 1.0, 1.0, 0.0], [-1.0, 0.0, 0.0, 1.0]], np.float32)
    return {"identd": np.eye(128, dtype=np.float32), "cmask": cm, "c64": np.ascontiguousarray(c64), "sel": sel}


def kb_params(inp, l, j):
    pcol = np.zeros((128, NPC), np.float32)
    sl = slice(j * 128, (j + 1) * 128)
    for which in range(3):
        for tap in range(4):
            pcol[:, which * 4 + tap] = inp["dn_conv_w"][l][tap, which * 512 + j * 128: which * 512 + (j + 1) * 128]
    pcol[:, 12] = inp["dn_a_log"][l][j]
    pcol[:, 13] = inp["dn_dt_bias"][l][j]
    pcol[:, 14] = inp["dn_norm_w"][l]
    for tap in range(4):
        pcol[:, 15 + tap] = inp["lru_conv_w"][l][tap, sl]
    pcol[:, 19] = inp["lru_conv_b"][l][sl]
    pcol[:, 20] = inp["lru_b_a"][l].reshape(512)[sl]
    pcol[:, 21] = inp["lru_b_x"][l].reshape(512)[sl]
    pcol[:, 22] = inp["lru_lam"][l][sl]
    pcol[:, 35] = inp["s5_d"][l].reshape(512)[sl]
    lruw = np.zeros((128, 2, 128), np.float32)
    for n in range(2):
        lruw[n * 64:(n + 1) * 64, 0, n * 64:(n + 1) * 64] = inp["lru_w_a"][l][2 * j + n]
        lruw[n * 64:(n + 1) * 64, 1, n * 64:(n + 1) * 64] = inp["lru_w_x"][l][2 * j + n]
    s5row = np.zeros((128, 3, 512), np.float32)
    s5b = np.zeros((128, 2, 4, 128), np.float32)
    s5c = np.zeros((128, 2, 4, 128), np.float32)
    for m in range(4):
        for gl in range(2):
            gloc = 2 * m + gl
            g = 8 * j + gloc
            rs = slice(gl * 64, (gl + 1) * 64)
            pcol[rs, 23 + m] = inp["s5_lam_re"][l][g]
            pcol[rs, 27 + m] = inp["s5_lam_im"][l][g]
            pcol[rs, 31 + m] = inp["s5_log_step"][l][g]
            cs_ = slice(m * 128 + gl * 64, m * 128 + (gl + 1) * 64)
            s5row[:, 0, cs_] = inp["s5_lam_re"][l][g][None, :]
            s5row[:, 1, cs_] = inp["s5_lam_im"][l][g][None, :]
            s5row[:, 2, cs_] = inp["s5_log_step"][l][g]
            ch = slice(gloc * 16, (gloc + 1) * 16)
            s5b[ch, 0, m, rs] = inp["s5_b_re"][l][g].T
            s5b[ch, 1, m, rs] = inp["s5_b_im"][l][g].T
            s5c[rs, 0, m, ch] = inp["s5_c_re"][l][g].T
            s5c[rs, 1, m, ch] = inp["s5_c_im"][l][g].T
    return {"pcol": pcol, "lruw": lruw, "s5row": s5row, "s5b": s5b, "s5c": s5c}


def kb_acts(projT, j):
    q = projT[0 + j * 128: 0 + (j + 1) * 128]
    k = projT[512 + j * 128: 512 + (j + 1) * 128]
    v = projT[1024 + j * 128: 1024 + (j + 1) * 128]
    z = projT[1536 + j * 128: 1536 + (j + 1) * 128]
    return {"qkvz": np.ascontiguousarray(np.stack([q, k, v, z])),
            "bd": np.ascontiguousarray(np.stack([projT[2048 + j], projT[2052 + j]])),
            "s5u": np.ascontiguousarray(projT[2056 + j * 128: 2056 + (j + 1) * 128]),
            "lrx": np.ascontiguousarray(projT[2568 + j * 128: 2568 + (j + 1) * 128]),
            "lry": np.ascontiguousarray(projT[3080 + j * 128: 3080 + (j + 1) * 128])}


_PROGS = {}


def _prog(name, fn):
    if name not in _PROGS:
        _PROGS[name] = fn()
    return _PROGS[name]


def _run(nc, in_maps):
    return run_bass_kernel_spmd(nc, in_maps, core_ids=list(range(NCORE))).results


def kernel(**inp):
    inp = {k: np.asarray(v) for k, v in inp.items()}
    x = inp["x"].astype(np.float32, copy=False)
    TOKC = SEQ // 4
    xT = [np.ascontiguousarray(x[b].T) for b in range(BATCH)]
    cst = kb_consts()
    for l in range(DEPTH):
        w = np.ascontiguousarray(inp["w_in"][l][:, :NMIX])
        ins = []
        for core in range(NCORE):
            b, q = core // 4, core % 4
            ins.append({"xT": np.ascontiguousarray(xT[b][:, q * TOKC:(q + 1) * TOKC]), "w": w})
        res = _run(_prog("KA", lambda: build_KA(TOKC, NMIX)), ins)
        projT = [np.concatenate([res[b * 4 + q]["pT"] for q in range(4)], axis=1) for b in range(BATCH)]
        ins = []
        for core in range(NCORE):
            b, j = core // 4, core % 4
            d = dict(cst)
            d.update(kb_params(inp, l, j))
            d.update(kb_acts(projT[b], j))
            ins.append(d)
        res = _run(_prog("KB", lambda: build_KB(SEQ)), ins)
        brT = []
        for b in range(BATCH):
            parts = [[res[b * 4 + j]["o"][g] for j in range(4)] for g in range(3)]
            brT.append(np.concatenate([np.concatenate(p, axis=0) for p in parts], axis=0))
        del projT
        par = kc_params(inp, l)
        ins = []
        for core in range(NCORE):
            b, q = core // 4, core % 4
            t0 = q * TOKC
            d = dict(par)
            xs = np.zeros((D, HW + TOKC), np.float32)
            bs = np.zeros((1536, HW + TOKC), np.float32)
            lo = max(0, t0 - HW)
            xs[:, HW - (t0 - lo):] = xT[b][:, lo:t0 + TOKC]
            bs[:, HW - (t0 - lo):] = brT[b][:, lo:t0 + TOKC]
            d["xT"], d["brT"] = xs, bs
            d["flag"] = np.full((128, 1), 0.0 if t0 == 0 else 1.0, np.float32)
            ins.append(d)
        res = _run(_prog("KC", lambda: build_KC(TOKC)), ins)
        xT = [np.concatenate([res[b * 4 + q]["x2T"] for q in range(4)], axis=1) for b in range(BATCH)]
    return np.ascontiguousarray(np.stack([xT[b].T for b in range(BATCH)])).astype(np.float32)
```

```python
import contextlib
import numpy as np
import concourse.bass as bass
import concourse.mybir as mybir
from concourse.bass_utils import run_bass_kernel_spmd

F32 = mybir.dt.float32
BF16 = mybir.dt.bfloat16
AF = mybir.ActivationFunctionType
OP = mybir.AluOpType

D = 1024
SEQ = 16384
BATCH = 2
DEPTH = 4
NCORE = 8
ALPHA = (2.0 * DEPTH) ** 0.25
NMIX = 3592


class Sched:
    ENGS = ("pe", "dve", "act", "pool", "sp")
    NLANE = 8

    def __init__(self, nc, stack, same_sync=("dve", "act", "pool")):
        self.nc = nc
        self.ops = {e: [] for e in self.ENGS}
        self.last_w = {}
        self.readers = {}
        self.same_sync = same_sync
        self.sem = {e: stack.enter_context(nc.semaphore("s_" + e)) for e in ("pe", "dve", "act", "pool")}
        self.lanes = {}
        for q in ("sp", "pool", "act"):
            self.lanes[q] = [stack.enter_context(nc.semaphore("d_%s%d" % (q, i))) for i in range(self.NLANE)]
        self.lane_cnt = {q: [0] * self.NLANE for q in self.lanes}
        self.lane_rr = {q: 0 for q in self.lanes}
        self.ccount = {e: 0 for e in self.ENGS}
        self.rec = None
        self.cur_tag = None

    def begin_rec(self):
        self.rec = []

    def end_rec(self):
        r, self.rec = self.rec, None
        return r

    def replay(self, ops):
        for (eng, fn, reads, writes, dma, tag) in ops:
            self.op(eng, fn, reads, writes, dma)

    def op(self, eng, fn, reads=(), writes=(), dma=False, tag=None):
        if self.rec is not None:
            self.rec.append((eng, fn, tuple(reads), tuple(writes), dma, tag if tag is not None else self.cur_tag))
            return
        deps = set()
        for k in reads:
            if k in self.last_w:
                deps.add(self.last_w[k])
        for k in writes:
            if k in self.last_w:
                deps.add(self.last_w[k])
            for r in self.readers.get(k, ()):
                deps.add(r)
        if dma:
            q = eng
            ln = self.lane_rr[q]
            self.lane_rr[q] = (ln + 1) % self.NLANE
            inc = 1 if dma == "cc" else 16
            self.lane_cnt[q][ln] += inc
            me = ("dma", q, ln, self.lane_cnt[q][ln], inc)
        else:
            self.ccount[eng] += 1
            me = ("c", eng, self.ccount[eng])
        deps.discard(me)
        for k in writes:
            self.last_w[k] = me
            self.readers[k] = []
        for k in reads:
            self.readers.setdefault(k, []).append(me)
        self.ops[eng].append((fn, deps, me))

    def emit(self):
        nc = self.nc
        sched = self

        def run(eng, e):
            waited = {}
            for (fn, deps, me) in sched.ops[eng]:
                need = {}
                for d in deps:
                    if d[0] == "dma":
                        s, v = sched.lanes[d[1]][d[2]], d[3]
                    else:
                        if d[1] == eng and me[0] == "c" and eng not in sched.same_sync:
                            continue
                        s, v = sched.sem[d[1]], d[2]
                    key = id(s)
                    if need.get(key, (None, 0))[1] < v:
                        need[key] = (s, v)
                for key, (s, v) in need.items():
                    if waited.get(key, 0) < v:
                        e.wait_ge(s, v)
                        waited[key] = v
                ins = fn(e)
                if me[0] == "dma":
                    ins.then_inc(sched.lanes[me[1]][me[2]], me[4])
                else:
                    ins.then_inc(sched.sem[eng], 1)
            if eng in sched.lanes:
                for ln, s in enumerate(sched.lanes[eng]):
                    if sched.lane_cnt[eng][ln] > 0:
                        e.wait_ge(s, sched.lane_cnt[eng][ln])

        with nc.Block() as block:
            @block.tensor
            def _(e):
                run("pe", e)

            @block.vector
            def _(e):
                run("dve", e)

            @block.scalar
            def _(e):
                run("act", e)

            @block.gpsimd
            def _(e):
                run("pool", e)

            @block.sync
            def _(e):
                run("sp", e)


class Ctx:
    def __init__(self, name="k"):
        self.nc = bass.Bass("TRN2", target_bir_lowering=False)
        self.stack = contextlib.ExitStack()
        self.s = Sched(self.nc, self.stack)
        self.nps = 0
        self.uid = 0
        self.banks = [self.stack.enter_context(self.nc.psum_tensor("ps%d" % i, [128, 512], F32)) for i in range(8)]
        self.bank_rr = 0

    def sb(self, name, shape, dt=F32):
        return self.stack.enter_context(self.nc.sbuf_tensor("sb_" + name, list(shape), dt))

    def din(self, name, shape, dt=F32):
        return self.nc.dram_tensor(name, list(shape), dt, kind="ExternalInput").ap()

    def dout(self, name, shape, dt=F32):
        return self.nc.dram_tensor(name, list(shape), dt, kind="ExternalOutput").ap()

    def bank(self, lo=0, hi=6):
        b = lo + (self.bank_rr % (hi - lo))
        self.bank_rr += 1
        return b

    def finish(self):
        self.s.emit()
        self.stack.close()
        return self.nc


def _rr(seq, state=[0]):
    state[0] += 1
    return seq[state[0] % len(seq)]


def stream_linear(c, W, K, cols, rhs, ntok, epi, wbufs, cast_engs=("act",), banks=(0, 6)):
    s = c.s
    KT = K // 128
    KTg = min(KT, 8)
    KG = KT // KTg
    Wv = W.rearrange("(kt p) n -> p kt n", p=128)
    i = 0
    while i < len(cols):
        blk = [cols[i]]
        for cc in cols[i + 1:i + 4]:
            if cc[0] == blk[-1][0] + blk[-1][1]:
                blk.append(cc)
            else:
                break
        c0 = blk[0][0]
        ncol = sum(b_[1] for b_ in blk)
        bks = [c.bank(*banks) for _ in blk]
        for kg in range(KG):
            bi = wbufs["rr"] % 2
            wbufs["rr"] += 1
            st, bf = wbufs["st"][bi], wbufs["bf"][bi]
            stv = st[:, 0:KTg * ncol].rearrange("p (kt n) -> p kt n", kt=KTg)
            bfv = bf[:, 0:KTg * ncol].rearrange("p (kt n) -> p kt n", kt=KTg)
            g_ = wbufs["rr"] - 1
            s.cur_tag = ("wl", g_)
            s.op("sp", lambda e, o=stv, i_=Wv[:, kg * KTg:(kg + 1) * KTg, c0:c0 + ncol]: e.dma_start(out=o, in_=i_),
                 writes=[("wst", bi)], dma=True)
            ce = _rr(cast_engs)
            o_cp(c, ce, bf[:, 0:KTg * ncol], st[:, 0:KTg * ncol], [("wst", bi)], [("wbf", bi)])
            s.cur_tag = ("wu", g_)
            off = 0
            for j, (col0, nc_) in enumerate(blk):
                ps = c.banks[bks[j]][0:nc_, 0:ntok]
                for k in range(KTg):
                    kt = kg * KTg + k
                    r_ap, r_key = rhs(kt)
                    s.op("pe", lambda e, o=ps, l=bfv[:, k, off:off + nc_], r=r_ap, a=(kt == 0), z=(kt == KT - 1):
                         e.matmul(o, l, r, start=a, stop=z),
                         reads=[("wbf", bi), r_key], writes=[("ps", bks[j])])
                off += nc_
        s.cur_tag = None
        for j, (col0, nc_) in enumerate(blk):
            epi(i + j, c.banks[bks[j]][0:nc_, 0:ntok], ("ps", bks[j]))
        i += len(blk)


def hoist_wloads(ops):
    wl = {}
    rest = []
    for o in ops:
        t = o[5]
        if isinstance(t, tuple) and t[0] == "wl":
            wl.setdefault(t[1], []).append(o)
        else:
            rest.append(o)
    out = []
    emitted = set()
    for o in rest:
        t = o[5]
        if isinstance(t, tuple) and t[0] == "wu":
            for g in (t[1], t[1] + 1):
                if g in wl and g not in emitted:
                    out.extend(wl[g])
                    emitted.add(g)
        out.append(o)
    for g in sorted(wl):
        assert g in emitted
    return out


def merge_streams(a, b):
    out = []
    ia = ib = 0
    na, nb = len(a), len(b)
    while ia < na or ib < nb:
        if ib >= nb or (ia < na and ia * nb <= ib * na):
            out.append(a[ia])
            ia += 1
        else:
            out.append(b[ib])
            ib += 1
    return out


def make_wbufs(c):
    return {"st": [c.sb("wst%d" % i, [128, 4096], F32) for i in range(2)],
            "bf": [c.sb("wbf%d" % i, [128, 4096], BF16) for i in range(2)], "rr": 0}


def build_KA(TOK=4096, NW=NMIX):
    c = Ctx()
    s = c.s
    xT = c.din("xT", [D, TOK])
    w = c.din("w", [D, NW])
    out = c.dout("pT", [NW, TOK])
    xbf = c.sb("xbf", [128, 8, TOK], BF16)
    xst = [c.sb("xst%d" % i, [128, 1024], F32) for i in range(2)]
    ost = [c.sb("ost%d" % i, [128, 512], F32) for i in range(4)]
    wb = make_wbufs(c)
    xv = xT.rearrange("(kt p) t -> p kt t", p=128)
    n = 0
    for kt in range(8):
        for t0 in range(0, TOK, 1024):
            bi = n % 2
            s.op("sp", lambda e, o=xst[bi][:, :], i_=xv[:, kt, t0:t0 + 1024]: e.dma_start(out=o, in_=i_),
                 writes=[("xst", bi)], dma=True)
            s.op("dve", lambda e, o=xbf[:, kt, t0:t0 + 1024], i_=xst[bi][:, :]: e.tensor_copy(o, i_),
                 reads=[("xst", bi)], writes=[("xbf", kt, t0 // 512), ("xbf", kt, t0 // 512 + 1)])
            n += 1
    cols = [(c0, min(128, NW - c0)) for c0 in range(0, NW, 128)]
    cnt = [0]
    s.begin_rec()
    for tt in range(TOK // 512):
        def rhs(kt, tt=tt):
            return xbf[:, kt, tt * 512:(tt + 1) * 512], ("xbf", kt, tt)

        def epi(i, ps, pkey, tt=tt):
            k = cnt[0] % 4
            cnt[0] += 1
            nr = cols[i][1]
            eng = "act" if cnt[0] % 2 else "dve"
            if eng == "act":
                s.op("act", lambda e, o=ost[k][0:nr, :], i_=ps: e.copy(o, i_), reads=[pkey], writes=[("ost", k)])
            else:
                s.op("dve", lambda e, o=ost[k][0:nr, :], i_=ps: e.tensor_copy(o, i_), reads=[pkey], writes=[("ost", k)])
            s.op("sp", lambda e, o=out[cols[i][0]:cols[i][0] + nr, tt * 512:(tt + 1) * 512], i_=ost[k][0:nr, :]:
                 e.dma_start(out=o, in_=i_), reads=[("ost", k)], dma=True)
        stream_linear(c, w, D, cols, rhs, 512, epi, wb)
    s.replay(hoist_wloads(s.end_rec()))
    return c.finish()


def o_tt(c, eng, out, a, b, op, r, w):
    c.s.op(eng, lambda e: e.tensor_tensor(out, a, b, op), reads=r, writes=w)


def o_ts(c, eng, out, a, s1, s2, op0, op1, r, w):
    if s2 is None:
        c.s.op(eng, lambda e: e.tensor_scalar(out, a, s1, None, op0), reads=r, writes=w)
    else:
        c.s.op(eng, lambda e: e.tensor_scalar(out, a, s1, s2, op0, op1), reads=r, writes=w)


def o_stt(c, eng, out, a, sc, b, op0, op1, r, w):
    eng = "dve"
    c.s.op(eng, lambda e: e.scalar_tensor_tensor(out, a, sc, b, op0, op1), reads=r, writes=w)


def o_act(c, out, a, func, r, w, bias=None, scale=None):
    kw = {}
    if bias is not None:
        kw["bias"] = bias
    if scale is not None:
        kw["scale"] = scale
    c.s.op("act", lambda e: e.activation(out, a, func, **kw), reads=r, writes=w)


def o_cp(c, eng, out, a, r, w):
    if eng == "act":
        c.s.op("act", lambda e: e.copy(out, a), reads=r, writes=w)
    else:
        c.s.op(eng, lambda e: e.tensor_copy(out, a), reads=r, writes=w)


def o_mm(c, out, l, rr, r, w, start=True, stop=True):
    c.s.op("pe", lambda e: e.matmul(out, l, rr, start=start, stop=stop), reads=r, writes=w)


def o_tr(c, out, a, ident, r, w):
    c.s.op("pe", lambda e: e.transpose(out, a, ident), reads=r, writes=w)


def o_dma(c, out, a, r=(), w=(), q="sp"):
    c.s.op(q, lambda e: e.dma_start(out=out, in_=a), reads=r, writes=w, dma=True)


def o_scan(c, out, d0, d1, init, r, w):
    c.s.op("dve", lambda e: e.tensor_tensor_scan(out, d0, d1, init, OP.mult, OP.add), reads=r, writes=w)


def o_memset(c, eng, out, v, w):
    c.s.op(eng, lambda e: e.memset(out, v), writes=w)


TWO_PI = float(2 * np.pi)


def sin_cos(c, th_ap, r, temps, ki, kq):
    outs = []
    for idx, shift in ((0, 0.0), (1, float(np.pi / 2))):
        (a, ka), (kf, kk), (res, kr) = temps[3 * idx: 3 * idx + 3]
        o_ts(c, "dve", a, th_ap, shift, None, OP.add, None, r, [ka])
        o_ts(c, "dve", kf, a, 1.0 / TWO_PI, None, OP.mult, None, [ka], [kk])
        o_cp(c, "dve", ki, kf, [kk], [kq])
        o_cp(c, "dve", kf, ki, [kq], [kk])
        o_stt(c, "dve", a, kf, -TWO_PI, a, OP.mult, OP.add, [kk, ka], [ka])
        o_ts(c, "dve", a, a, float(np.pi), float(-np.pi), OP.min, OP.max, [ka], [ka])
        o_act(c, res, a, AF.Sin, [ka], [kr])
        outs.append((res, kr))
    return outs[0], outs[1]


TC = 512
NPC = 36
S5POOL = "pool"


def build_KB(S=SEQ, do=("lru", "s5", "dn"), dbg=False):
    c = Ctx()
    s = c.s
    NSC = S // TC
    qkvz = c.din("qkvz", [4, 128, S])
    bd = c.din("bd", [2, S])
    s5u = c.din("s5u", [128, S])
    lrx = c.din("lrx", [128, S])
    lry = c.din("lry", [128, S])
    pcol_d = c.din("pcol", [128, NPC])
    lruw_d = c.din("lruw", [128, 2, 128])
    s5row_d = c.din("s5row", [128, 3, 512])
    s5b_d = c.din("s5b", [128, 2, 4, 128])
    s5c_d = c.din("s5c", [128, 2, 4, 128])
    ident_d = c.din("identd", [128, 128])
    cmask_d = c.din("cmask", [128, TC])
    c64_d = c.din("c64", [64, 3, 512])
    sel_d = c.din("sel", [2, 4])
    out = c.dout("o", [3, 128, S])

    pcol = c.sb("pcol", [128, NPC])
    o_dma(c, pcol[:], pcol_d, w=["pcol"])
    ident = c.sb("ident", [128, 128])
    o_dma(c, ident[:], ident_d, w=["ident"])
    ones = c.sb("ones", [128, 128])
    o_memset(c, "pool", ones[:], 1.0, ["ones"])

    def col(i):
        return pcol[:, i:i + 1]

    if "lru" in do:
        lruw = c.sb("lruw", [128, 2, 128])
        o_dma(c, lruw[:], lruw_d, w=["lruw"])
        c8 = c.sb("c8", [128, 1])
        o_act(c, c8[:], col(22), AF.Exp, ["pcol"], ["c8"], scale=-1.0)
        o_act(c, c8[:], c8[:], AF.Ln, ["c8"], ["c8"], bias=1.0)
        o_ts(c, "dve", c8[:], c8[:], -8.0, None, OP.mult, None, ["c8"], ["c8"])
        lru_h = c.sb("lru_h", [128, 1])
        o_memset(c, "dve", lru_h[:], 0.0, ["lru_h"])
        L_xin = [c.sb("L_xin%d" % i, [128, 3 + TC]) for i in range(2)]
        L_yin = [c.sb("L_yin%d" % i, [128, TC]) for i in range(2)]
        L_t = [c.sb("L_t%d" % i, [128, TC]) for i in range(6)]

    def lru_chunk(sc):
        t0 = sc * TC
        bi = sc % 2
        xin, yin = L_xin[bi], L_yin[bi]
        kx, ky = ("L_xin", bi), ("L_yin", bi)
        if sc == 0:
            o_memset(c, "pool", xin[:, 0:3], 0.0, [kx])
            o_dma(c, xin[:, 3:3 + TC], lrx[:, 0:TC], w=[kx])
        else:
            o_dma(c, xin[:, :], lrx[:, t0 - 3:t0 + TC], w=[kx])
        o_dma(c, yin[:, :], lry[:, t0:t0 + TC], w=[ky])
        xc, r_, i_, a_, m_, h_ = [t[:, :] for t in L_t]
        K = ["L_t%d" % i for i in range(6)]
        o_act(c, xc, xin[:, 0:TC], AF.Identity, [kx, "pcol"], [K[0]], bias=col(19), scale=col(15))
        for k in (1, 2, 3):
            o_stt(c, "pool", xc, xin[:, k:k + TC], col(15 + k), xc, OP.mult, OP.add, [kx, "pcol", K[0]], [K[0]])
        b1, b2 = c.bank(), c.bank()
        o_mm(c, c.banks[b1][:, 0:TC], lruw[:, 0, :], xc, ["lruw", K[0]], [("ps", b1)])
        o_mm(c, c.banks[b2][:, 0:TC], lruw[:, 1, :], xc, ["lruw", K[0]], [("ps", b2)])
        o_act(c, r_, c.banks[b1][:, 0:TC], AF.Sigmoid, [("ps", b1), "pcol"], [K[1]], bias=col(20))
        o_act(c, i_, c.banks[b2][:, 0:TC], AF.Sigmoid, [("ps", b2), "pcol"], [K[2]], bias=col(21))
        o_act(c, a_, r_, AF.Exp, [K[1], "c8"], [K[3]], scale=c8[:, 0:1])
        o_tt(c, "pool", m_, a_, a_, OP.mult, [K[3]], [K[4]])
        o_act(c, m_, m_, AF.Sqrt, [K[4]], [K[4]], bias=1.0, scale=-1.0)
        if sc == 0:
            o_memset(c, "pool", m_[:, 0:1], 1.0, [K[4]])
        o_tt(c, "pool", m_, m_, i_, OP.mult, [K[4], K[2]], [K[4]])
        o_tt(c, "pool", m_, m_, xc, OP.mult, [K[4], K[0]], [K[4]])
        o_scan(c, h_, a_, m_, lru_h[:, 0:1], [K[3], K[4], "lru_h"], [K[5]])
        o_cp(c, "dve", lru_h[:, 0:1], h_[:, TC - 1:TC], [K[5]], ["lru_h"])
        o_act(c, r_, yin[:, :], AF.Gelu_apprx_tanh, [ky], [K[1]])
        o_tt(c, "pool", r_, r_, h_, OP.mult, [K[1], K[5]], [K[1]])
        o_dma(c, out[2, :, t0:t0 + TC], r_, r=[K[1]])

    if "s5" in do:
        dl = c.sb("s5_dl", [128, 4])
        o_act(c, dl[:], pcol[:, 31:35], AF.Exp, ["pcol"], ["s5_dl"])
        rr = c.sb("s5_r", [128, 4])
        o_tt(c, "dve", rr[:], pcol[:, 23:27], dl[:], OP.mult, ["pcol", "s5_dl"], ["s5_r"])
        o_act(c, rr[:], rr[:], AF.Exp, ["s5_r"], ["s5_r"])
        th = c.sb("s5_th", [128, 4])
        o_tt(c, "dve", th[:], pcol[:, 27:31], dl[:], OP.mult, ["pcol", "s5_dl"], ["s5_th"])
        sct = [c.sb("s5ct%d" % i, [128, 4]) for i in range(6)]
        scki = c.sb("s5cki", [128, 4], mybir.dt.int32)
        (sn, ksn), (cs, kcs) = sin_cos(c, th[:], ["s5_th"], [(t_[:], "s5ct%d" % i) for i, t_ in enumerate(sct)], scki[:], "s5cki")
        Ct = c.sb("s5_Ct", [128, 4, TC])
        St = c.sb("s5_St", [128, 4, TC])
        o_memset(c, "dve", Ct[:, :, 0:1], 1.0, ["s5_Ct"])
        o_memset(c, "dve", St[:, :, 0:1], 0.0, ["s5_St"])
        cw = c.sb("s5_cw", [128, 4])
        sw = c.sb("s5_sw", [128, 4])
        nsw = c.sb("s5_nsw", [128, 4])
        tq = c.sb("s5_tq", [128, 4])
        o_cp(c, "dve", cw[:], cs, [kcs], ["s5_cw"])
        o_cp(c, "dve", sw[:], sn, [ksn], ["s5_sw"])
        tmpT = c.sb("s5_tmpT", [128, TC // 2])
        w_ = 1
        while w_ < TC:
            o_ts(c, "dve", nsw[:], sw[:], -1.0, None, OP.mult, None, ["s5_sw"], ["s5_nsw"])
            for m in range(4):
                o_ts(c, "dve", tmpT[:, 0:w_], Ct[:, m, 0:w_], cw[:, m:m + 1], None, OP.mult, None, ["s5_Ct", "s5_cw"], ["s5_tmpT"])
                o_stt(c, "dve", Ct[:, m, w_:2 * w_], St[:, m, 0:w_], nsw[:, m:m + 1], tmpT[:, 0:w_], OP.mult, OP.add,
                      ["s5_St", "s5_nsw", "s5_tmpT"], ["s5_Ct"])
                o_ts(c, "dve", tmpT[:, 0:w_], St[:, m, 0:w_], cw[:, m:m + 1], None, OP.mult, None, ["s5_St", "s5_cw"], ["s5_tmpT"])
                o_stt(c, "dve", St[:, m, w_:2 * w_], Ct[:, m, 0:w_], sw[:, m:m + 1], tmpT[:, 0:w_], OP.mult, OP.add,
                      ["s5_Ct", "s5_sw", "s5_tmpT"], ["s5_St"])
            o_tt(c, "dve", tq[:], sw[:], sw[:], OP.mult, ["s5_sw"], ["s5_tq"])
            o_tt(c, "dve", sw[:], cw[:], sw[:], OP.mult, ["s5_cw", "s5_sw"], ["s5_sw"])
            o_ts(c, "dve", sw[:], sw[:], 2.0, None, OP.mult, None, ["s5_sw"], ["s5_sw"])
            o_tt(c, "dve", cw[:], cw[:], cw[:], OP.mult, ["s5_cw"], ["s5_cw"])
            o_tt(c, "dve", cw[:], cw[:], tq[:], OP.subtract, ["s5_cw", "s5_tq"], ["s5_cw"])
            w_ *= 2
        o_ts(c, "dve", nsw[:], sw[:], -1.0, None, OP.mult, None, ["s5_sw"], ["s5_nsw"])
        row = c.sb("s5row", [128, 3, 512])
        o_dma(c, row[:], s5row_d, w=["s5row"])
        S_w = [[c.sb("S_w%d_%d" % (p, i), [128, TC]) for i in range(8)] for p in range(2)]

        def SW(p, i):
            return S_w[p][i][:, :], ("S_w", p, i)
        (dlr, kdlr), (er, ker), (thr, kthr), (nr, knr), (ni, kni), (den, kden), (t2, kt2), (fr, kfr) = [SW(0, i) for i in range(8)]
        (fi, kfi) = SW(1, 0)
        rki = c.sb("s5rki", [128, 512], mybir.dt.int32)
        o_act(c, dlr, row[:, 2, :], AF.Exp, ["s5row"], [kdlr])
        o_tt(c, "dve", er, row[:, 0, :], dlr, OP.mult, ["s5row", kdlr], [ker])
        o_act(c, er, er, AF.Exp, [ker], [ker])
        o_tt(c, "dve", thr, row[:, 1, :], dlr, OP.mult, ["s5row", kdlr], [kthr])
        (snr, ksnr), (csr, kcsr) = sin_cos(c, thr, [kthr], [SW(1, i) for i in range(1, 7)], rki[:], "s5rki")
        o_tt(c, "dve", nr, er, csr, OP.mult, [ker, kcsr], [knr])
        o_ts(c, "dve", nr, nr, -1.0, None, OP.add, None, [knr], [knr])
        o_tt(c, "dve", ni, er, snr, OP.mult, [ker, ksnr], [kni])
        o_tt(c, "dve", den, row[:, 0, :], row[:, 0, :], OP.mult, ["s5row"], [kden])
        o_tt(c, "dve", t2, row[:, 1, :], row[:, 1, :], OP.mult, ["s5row"], [kt2])
        o_tt(c, "dve", t2, den, t2, OP.add, [kden, kt2], [kt2])
        c.s.op("dve", lambda e: e.reciprocal(den, t2), reads=[kt2], writes=[kden])
        o_tt(c, "dve", fr, nr, row[:, 0, :], OP.mult, [knr, "s5row"], [kfr])
        o_tt(c, "dve", t2, ni, row[:, 1, :], OP.mult, [kni, "s5row"], [kt2])
        o_tt(c, "dve", fr, fr, t2, OP.add, [kfr, kt2], [kfr])
        o_tt(c, "dve", fr, fr, den, OP.mult, [kfr, kden], [kfr])
        o_tt(c, "dve", fi, ni, row[:, 0, :], OP.mult, [kni, "s5row"], [kfi])
        o_tt(c, "dve", t2, nr, row[:, 1, :], OP.mult, [knr, "s5row"], [kt2])
        o_tt(c, "dve", fi, fi, t2, OP.subtract, [kfi, kt2], [kfi])
        o_tt(c, "dve", fi, fi, den, OP.mult, [kfi, kden], [kfi])
        Bin = c.sb("s5_Bin", [128, 2, 512])
        o_dma(c, Bin[:], s5b_d.rearrange("p a m n -> p a (m n)"), w=["s5_Bin"])
        Bb = c.sb("s5_Bb", [128, 2, 512])
        o_tt(c, "dve", Bb[:, 0, :], fr, Bin[:, 0, :], OP.mult, [kfr, "s5_Bin"], ["s5_Bb"])
        o_tt(c, "dve", t2, fi, Bin[:, 1, :], OP.mult, [kfi, "s5_Bin"], [kt2])
        o_tt(c, "dve", Bb[:, 0, :], Bb[:, 0, :], t2, OP.subtract, ["s5_Bb", kt2], ["s5_Bb"])
        o_tt(c, "dve", Bb[:, 1, :], fr, Bin[:, 1, :], OP.mult, [kfr, "s5_Bin"], ["s5_Bb"])
        o_tt(c, "dve", t2, fi, Bin[:, 0, :], OP.mult, [kfi, "s5_Bin"], [kt2])
        o_tt(c, "dve", Bb[:, 1, :], Bb[:, 1, :], t2, OP.add, ["s5_Bb", kt2], ["s5_Bb"])
        Cm = c.sb("s5_Cm", [128, 2, 512])
        o_dma(c, Cm[:], s5c_d.rearrange("p a m n -> p a (m n)"), w=["s5_Cm"])
        o_ts(c, "dve", Cm[:, 1, :], Cm[:, 1, :], -1.0, None, OP.mult, None, ["s5_Cm"], ["s5_Cm"])
        s5_init = c.sb("s5_init", [128, 2, 4])
        o_memset(c, "dve", s5_init[:], 0.0, ["s5_init"])
        s5_tc = c.sb("s5_tc", [128, 4])
        if dbg:
            for nm, t_, shp, keys in (("Ct", Ct, [128, 4, TC], ["s5_Ct"]), ("St", St, [128, 4, TC], ["s5_St"]),
                                      ("Bb", Bb, [128, 2, 512], ["s5_Bb"]), ("rr", rr, [128, 4], ["s5_r"]),
                                      ("cw", cw, [128, 4], ["s5_cw"]), ("sw", sw, [128, 4], ["s5_sw"]),
                                      ("Cm", Cm, [128, 2, 512], ["s5_Cm"])):
                o_dma(c, c.dout("dbg_" + nm, shp), t_[:], r=keys)
        S_uin = [c.sb("S_uin%d" % i, [128, TC]) for i in range(2)]
        S_y = c.sb("S_y", [128, TC])

    def s5_chunk(sc):
        t0 = sc * TC
        bi = sc % 2
        uin = S_uin[bi]
        ku = ("S_uin", bi)
        o_dma(c, uin[:, :], s5u[:, t0:t0 + TC], w=[ku])
        by = 7
        for m in range(4):
            W = S_w[m % 2]
            KW = [("S_w", m % 2, i) for i in range(8)]
            b1, b2 = c.bank(), c.bank()
            pr, pi = c.banks[b1][:, 0:TC], c.banks[b2][:, 0:TC]
            o_mm(c, pr, Bb[:, 0, m * 128:(m + 1) * 128], uin[:, :], ["s5_Bb", ku], [("ps", b1)])
            o_mm(c, pi, Bb[:, 1, m * 128:(m + 1) * 128], uin[:, :], ["s5_Bb", ku], [("ps", b2)])
            Cj, Sj = Ct[:, m, :], St[:, m, :]
            t1, t2_, t3, t4, Xr, Xi, hr, hi = [t[:, :] for t in W]
            o_tt(c, "dve", t1, pr, Cj, OP.mult, [("ps", b1), "s5_Ct"], [KW[0]])
            o_tt(c, "dve", t2_, pi, Sj, OP.mult, [("ps", b2), "s5_St"], [KW[1]])
            o_tt(c, "dve", t3, pi, Cj, OP.mult, [("ps", b2), "s5_Ct"], [KW[2]])
            o_tt(c, "dve", t4, pr, Sj, OP.mult, [("ps", b1), "s5_St"], [KW[3]])
            o_tt(c, S5POOL, Xr, t1, t2_, OP.add, [KW[0], KW[1]], [KW[4]])
            o_tt(c, S5POOL, Xi, t3, t4, OP.subtract, [KW[2], KW[3]], [KW[5]])
            rb = rr[:, m:m + 1].to_broadcast([128, TC])
            o_scan(c, t1, rb, Xr, s5_init[:, 0, m:m + 1], ["s5_r", KW[4], "s5_init"], [KW[0]])
            o_scan(c, t3, rb, Xi, s5_init[:, 1, m:m + 1], ["s5_r", KW[5], "s5_init"], [KW[2]])
            o_ts(c, "dve", s5_tc[:, m:m + 1], t1[:, TC - 1:TC], cw[:, m:m + 1], None, OP.mult, None, [KW[0], "s5_cw"], ["s5_tc"])
            o_stt(c, "dve", s5_init[:, 0, m:m + 1], t3[:, TC - 1:TC], nsw[:, m:m + 1], s5_tc[:, m:m + 1], OP.mult, OP.add,
                  [KW[2], "s5_nsw", "s5_tc"], ["s5_init"])
            o_ts(c, "dve", s5_tc[:, m:m + 1], t3[:, TC - 1:TC], cw[:, m:m + 1], None, OP.mult, None, [KW[2], "s5_cw"], ["s5_tc"])
            o_stt(c, "dve", s5_init[:, 1, m:m + 1], t1[:, TC - 1:TC], sw[:, m:m + 1], s5_tc[:, m:m + 1], OP.mult, OP.add,
                  [KW[0], "s5_sw", "s5_tc"], ["s5_init"])
            o_tt(c, S5POOL, t2_, t1, Cj, OP.mult, [KW[0], "s5_Ct"], [KW[1]])
            o_tt(c, S5POOL, t4, t3, Sj, OP.mult, [KW[2], "s5_St"], [KW[3]])
            o_tt(c, "dve", hr, t2_, t4, OP.subtract, [KW[1], KW[3]], [KW[6]])
            o_tt(c, S5POOL, Xr, t1, Sj, OP.mult, [KW[0], "s5_St"], [KW[4]])
            o_tt(c, S5POOL, Xi, t3, Cj, OP.mult, [KW[2], "s5_Ct"], [KW[5]])
            o_tt(c, "dve", hi, Xr, Xi, OP.add, [KW[4], KW[5]], [KW[7]])
            o_mm(c, c.banks[by][:, 0:TC], Cm[:, 0, m * 128:(m + 1) * 128], hr, ["s5_Cm", KW[6]], [("ps", by)],
                 start=(m == 0), stop=False)
            o_mm(c, c.banks[by][:, 0:TC], Cm[:, 1, m * 128:(m + 1) * 128], hi, ["s5_Cm", KW[7]], [("ps", by)],
                 start=False, stop=(m == 3))
        o_stt(c, "dve", S_y[:, :], uin[:, :], col(35), c.banks[by][:, 0:TC], OP.mult, OP.add, [ku, "pcol", ("ps", by)], ["S_y"])
        o_act(c, S_y[:, :], S_y[:, :], AF.Gelu_apprx_tanh, ["S_y"], ["S_y"])
        o_dma(c, out[1, :, t0:t0 + TC], S_y[:, :], r=["S_y"])

    dn_chunk = build_dn(c, do, S, qkvz, bd, out, pcol, ident, ones, cmask_d, c64_d, sel_d)
    for sc in range(NSC):
        if "lru" in do:
            lru_chunk(sc)
        if "s5" in do:
            s5_chunk(sc)
        if "dn" in do:
            dn_chunk(sc)
    return c.finish()


def build_dn(c, do, S, qkvz, bd, out, pcol, ident, ones, cmask_d, c64_d, sel_d):
    if "dn" not in do:
        return None
    NCH = TC // 64

    def col(i):
        return pcol[:, i:i + 1]

    cmask = c.sb("cmask", [128, TC])
    o_dma(c, cmask[:], cmask_d, w=["cmask"])
    c64 = c.sb("c64", [64, 3, 512])
    o_dma(c, c64[:], c64_d, w=["c64"])
    sel = c.sb("sel", [2, 4])
    o_dma(c, sel[:], sel_d, w=["sel"])
    negmask, strict, ident8 = c64[:, 0, :], c64[:, 1, :], c64[:, 2, :]
    negA = c.sb("negA", [128, 1])
    o_act(c, negA[:], col(12), AF.Exp, ["pcol"], ["negA"])
    o_ts(c, "dve", negA[:], negA[:], -1.0, None, OP.mult, None, ["negA"], ["negA"])
    Sst = [c.sb("dnS%d" % i, [128, 128]) for i in range(2)]
    o_memset(c, "dve", Sst[0][:], 0.0, [("dnS", 0)])
    Xin = [[c.sb("D_in%d_%d" % (w, i), [128, 3 + TC]) for i in range(2)] for w in range(3)]
    Zin = [c.sb("D_z%d" % i, [128, TC]) for i in range(2)]
    Brow = [c.sb("D_br%d" % i, [1, TC]) for i in range(2)]
    Drow = [c.sb("D_dr%d" % i, [1, TC]) for i in range(2)]
    names = ["qc", "kc", "vc", "sq", "rs", "beta", "gc", "eg", "egl", "kb", "kbg", "vb", "kt", "qd", "oT", "wT", "zs", "tmp"]
    T = {n: c.sb("D_" + n, [128, TC]) for n in names}
    T64 = {n: c.sb("D64_" + n, [64, 512]) for n in ["dm", "AT", "U0", "U1", "L0", "L1", "X"]}
    TM = {n: c.sb("Dtm_" + n, [64, NCH, 128]) for n in ["kbg", "vb", "kt", "u"]}
    Vn = [c.sb("D_vn%d" % i, [64, 128]) for i in range(2)]
    G1 = c.sb("D_G1", [2, TC])
    G2 = c.sb("D_G2", [2, TC])

    def A(n):
        return T[n][:, :]

    def K(n):
        return "D_" + n

    def A64(n):
        return T64[n][:, :]

    def K64(n):
        return "D64_" + n

    def dn_chunk(sc):
        t0 = sc * TC
        bi = sc % 2
        for w in range(3):
            xin, kx = Xin[w][bi], ("D_in", w, bi)
            if sc == 0:
                o_memset(c, "pool", xin[:, 0:3], 0.0, [kx])
                o_dma(c, xin[:, 3:3 + TC], qkvz[w, :, 0:TC], w=[kx])
            else:
                o_dma(c, xin[:, :], qkvz[w, :, t0 - 3:t0 + TC], w=[kx])
        zin, kz = Zin[bi], ("D_z", bi)
        o_dma(c, zin[:, :], qkvz[3, :, t0:t0 + TC], w=[kz])
        brow, drow = Brow[bi], Drow[bi]
        kbr, kdr = ("D_br", bi), ("D_dr", bi)
        o_dma(c, brow[:, :], bd[0:1, t0:t0 + TC], w=[kbr])
        o_dma(c, drow[:, :], bd[1:2, t0:t0 + TC], w=[kdr])
        for w, nm in enumerate(["qc", "kc", "vc"]):
            xin, kx = Xin[w][bi], ("D_in", w, bi)
            o_act(c, A(nm), xin[:, 0:TC], AF.Identity, [kx, "pcol"], [K(nm)], scale=col(w * 4 + 0))
            for k in (1, 2, 3):
                o_stt(c, "dve", A(nm), xin[:, k:k + TC], col(w * 4 + k), A(nm), OP.mult, OP.add, [kx, "pcol", K(nm)], [K(nm)])
            o_act(c, A(nm), A(nm), AF.Silu, [K(nm)], [K(nm)])
        for nm, scale in (("qc", 128.0 ** -0.5), ("kc", 1.0)):
            o_act(c, A("sq"), A(nm), AF.Square, [K(nm)], [K("sq")])
            b = c.bank()
            o_mm(c, c.banks[b][:, 0:TC], ones[:, :], A("sq"), ["ones", K("sq")], [("ps", b)])
            o_act(c, A("rs"), c.banks[b][:, 0:TC], AF.Sqrt, [("ps", b)], [K("rs")], bias=1e-6)
            c.s.op("dve", lambda e: e.reciprocal(A("rs"), A("rs")), reads=[K("rs")], writes=[K("rs")])
            o_stt(c, "dve", A(nm), A(nm), scale, A("rs"), OP.mult, OP.mult, [K(nm), K("rs")], [K(nm)])
        b = c.bank()
        o_mm(c, c.banks[b][:, 0:TC], ones[0:1, :], brow[:, :], ["ones", kbr], [("ps", b)])
        o_act(c, A("beta"), c.banks[b][:, 0:TC], AF.Sigmoid, [("ps", b)], [K("beta")])
        b = c.bank()
        o_mm(c, c.banks[b][:, 0:TC], ones[0:1, :], drow[:, :], ["ones", kdr], [("ps", b)])
        o_act(c, A("tmp"), c.banks[b][:, 0:TC], AF.Exp, [("ps", b), "pcol"], [K("tmp")], bias=col(13))
        o_act(c, A("tmp"), A("tmp"), AF.Ln, [K("tmp")], [K("tmp")], bias=1.0)
        o_ts(c, "dve", A("tmp"), A("tmp"), negA[:, 0:1], None, OP.mult, None, [K("tmp"), "negA"], [K("tmp")])
        o_scan(c, A("gc"), cmask[:, :], A("tmp"), 0.0, ["cmask", K("tmp")], [K("gc")])
        o_act(c, A("eg"), A("gc"), AF.Exp, [K("gc")], [K("eg")])
        gc3 = A("gc").rearrange("p (n c) -> p n c", c=64)
        o_tt(c, "dve", A("egl").rearrange("p (n c) -> p n c", c=64), gc3[:, :, 63:64].to_broadcast([128, NCH, 64]), gc3,
             OP.subtract, [K("gc")], [K("egl")])
        o_act(c, A("egl"), A("egl"), AF.Exp, [K("egl")], [K("egl")])
        o_tt(c, "dve", A("kb"), A("kc"), A("beta"), OP.mult, [K("kc"), K("beta")], [K("kb")])
        o_tt(c, "pool", A("kbg"), A("kb"), A("eg"), OP.mult, [K("kb"), K("eg")], [K("kbg")])
        o_tt(c, "pool", A("vb"), A("vc"), A("beta"), OP.mult, [K("vc"), K("beta")], [K("vb")])
        o_tt(c, "pool", A("kt"), A("kc"), A("egl"), OP.mult, [K("kc"), K("egl")], [K("kt")])
        o_tt(c, "pool", A("qd"), A("qc"), A("eg"), OP.mult, [K("qc"), K("eg")], [K("qd")])
        o_ts(c, "dve", G1[:, :], A("gc")[0:2, :], sel[:, 0:1], sel[:, 1:2], OP.mult, OP.add, [K("gc"), "sel"], ["D_G1"])
        o_ts(c, "dve", G2[:, :], A("gc")[0:2, :], sel[:, 2:3], sel[:, 3:4], OP.mult, OP.add, [K("gc"), "sel"], ["D_G2"])
        b = c.bank()
        for n in range(NCH):
            cs_ = slice(n * 64, (n + 1) * 64)
            o_mm(c, c.banks[b][0:64, cs_], G1[:, cs_], G2[:, cs_], ["D_G1", "D_G2"], [("ps", b)])
        o_stt(c, "dve", A64("dm"), c.banks[b][0:64, 0:512], 0.0, negmask, OP.min, OP.add, [("ps", b), "c64"], [K64("dm")])
        o_act(c, A64("dm"), A64("dm"), AF.Exp, [K64("dm")], [K64("dm")])
        b = c.bank()
        for n in range(NCH):
            cs_ = slice(n * 64, (n + 1) * 64)
            o_mm(c, c.banks[b][0:64, cs_], A("kc")[:, cs_], A("qc")[:, cs_], [K("kc"), K("qc")], [("ps", b)])
        o_tt(c, "dve", A64("AT"), c.banks[b][0:64, 0:512], A64("dm"), OP.mult, [("ps", b), K64("dm")], [K64("AT")])
        b = c.bank()
        for n in range(NCH):
            cs_ = slice(n * 64, (n + 1) * 64)
            o_mm(c, c.banks[b][0:64, cs_], A("kc")[:, cs_], A("kb")[:, cs_], [K("kc"), K("kb")], [("ps", b)])
        o_tt(c, "dve", A64("U0"), c.banks[b][0:64, 0:512], A64("dm"), OP.mult, [("ps", b), K64("dm")], [K64("U0")])
        o_tt(c, "pool", A64("U0"), A64("U0"), strict, OP.mult, [K64("U0"), "c64"], [K64("U0")])
        b = c.bank()
        for n in range(NCH):
            cs_ = slice(n * 64, (n + 1) * 64)
            o_tr(c, c.banks[b][0:64, cs_], A64("U0")[:, cs_], ident[0:64, 0:64], [K64("U0"), "ident"], [("ps", b)])
        o_cp(c, "act", A64("L0"), c.banks[b][0:64, 0:512], [("ps", b)], [K64("L0")])
        o_tt(c, "dve", A64("X"), ident8, A64("U0"), OP.subtract, ["c64", K64("U0")], [K64("X")])
        cu, cl = "U0", "L0"
        for lvl in range(5):
            nu, nl = ("U1", "L1") if cu == "U0" else ("U0", "L0")
            b = c.bank()
            for n in range(NCH):
                cs_ = slice(n * 64, (n + 1) * 64)
                o_mm(c, c.banks[b][0:64, cs_], A64(cl)[:, cs_], A64(cu)[:, cs_], [K64(cl), K64(cu)], [("ps", b)])
            o_cp(c, "act", A64(nu), c.banks[b][0:64, 0:512], [("ps", b)], [K64(nu)])
            b = c.bank()
            for n in range(NCH):
                cs_ = slice(n * 64, (n + 1) * 64)
                o_mm(c, c.banks[b][0:64, cs_], A64(cu)[:, cs_], A64(cl)[:, cs_], [K64(cl), K64(cu)], [("ps", b)])
            o_cp(c, "dve", A64(nl), c.banks[b][0:64, 0:512], [("ps", b)], [K64(nl)])
            b = c.bank()
            for n in range(NCH):
                cs_ = slice(n * 64, (n + 1) * 64)
                o_mm(c, c.banks[b][0:64, cs_], A64(nl)[:, cs_], A64("X")[:, cs_], [K64(nl), K64("X")], [("ps", b)])
            o_tt(c, "dve", A64("X"), A64("X"), c.banks[b][0:64, 0:512], OP.add, [K64("X"), ("ps", b)], [K64("X")])
            cu, cl = nu, nl
        ei = 0
        for nm in ("kbg", "vb", "kt"):
            for half in range(2):
                b = c.bank()
                for q in range(4):
                    n = half * 4 + q
                    o_tr(c, c.banks[b][0:64, q * 128:(q + 1) * 128], A(nm)[:, n * 64:(n + 1) * 64], ident[:, :],
                         [K(nm), "ident"], [("ps", b)])
                o_cp(c, "act" if ei % 2 == 0 else "dve", TM[nm][:, half * 4:(half + 1) * 4, :].rearrange("p a b -> p (a b)"),
                     c.banks[b][0:64, 0:512], [("ps", b)], [("Dtm", nm)])
                ei += 1
        for half in range(2):
            b = c.bank()
            for q in range(4):
                n = half * 4 + q
                o_mm(c, c.banks[b][0:64, q * 128:(q + 1) * 128], A64("X")[:, n * 64:(n + 1) * 64], TM["vb"][:, n, :],
                     [K64("X"), ("Dtm", "vb")], [("ps", b)])
            o_cp(c, "act", TM["u"][:, half * 4:(half + 1) * 4, :].rearrange("p a b -> p (a b)"), c.banks[b][0:64, 0:512],
                 [("ps", b)], [("Dtm", "u")])
        b = c.bank()
        for n in range(NCH):
            cs_ = slice(n * 64, (n + 1) * 64)
            o_mm(c, c.banks[b][:, cs_], TM["kbg"][:, n, :], A64("X")[:, cs_], [("Dtm", "kbg"), K64("X")], [("ps", b)])
        o_cp(c, "act", A("wT"), c.banks[b][:, 0:512], [("ps", b)], [K("wT")])
        bo = 6
        for n in range(NCH):
            gi = sc * NCH + n
            cur, nxt = Sst[gi % 2], Sst[(gi + 1) % 2]
            kcur, knxt = ("dnS", gi % 2), ("dnS", (gi + 1) % 2)
            vn, kvn = Vn[gi % 2], ("D_vn", gi % 2)
            cs_ = slice(n * 64, (n + 1) * 64)
            b1 = c.bank()
            o_mm(c, c.banks[b1][0:64, 0:128], A("wT")[:, cs_], cur[:, :], [K("wT"), kcur], [("ps", b1)])
            o_tt(c, "dve", vn[:, :], TM["u"][:, n, :], c.banks[b1][0:64, 0:128], OP.subtract, [("Dtm", "u"), ("ps", b1)], [kvn])
            o_mm(c, c.banks[bo][:, cs_], cur[:, :], A("qd")[:, cs_], [kcur, K("qd")], [("ps", bo)], start=True, stop=False)
            o_mm(c, c.banks[bo][:, cs_], vn[:, :], A64("AT")[:, cs_], [kvn, K64("AT")], [("ps", bo)], start=False, stop=True)
            b2 = c.bank()
            o_mm(c, c.banks[b2][:, 0:128], TM["kt"][:, n, :], vn[:, :], [("Dtm", "kt"), kvn], [("ps", b2)])
            o_stt(c, "dve", nxt[:, :], cur[:, :], A("eg")[:, n * 64 + 63:n * 64 + 64], c.banks[b2][:, 0:128], OP.mult, OP.add,
                  [kcur, K("eg"), ("ps", b2)], [knxt])
        o_cp(c, "act", A("oT"), c.banks[bo][:, 0:512], [("ps", bo)], [K("oT")])
        o_act(c, A("sq"), A("oT"), AF.Square, [K("oT")], [K("sq")])
        b = c.bank()
        o_mm(c, c.banks[b][:, 0:TC], ones[:, :], A("sq"), ["ones", K("sq")], [("ps", b)])
        o_act(c, A("rs"), c.banks[b][:, 0:TC], AF.Sqrt, [("ps", b)], [K("rs")], bias=1e-6, scale=1.0 / 128.0)
        c.s.op("dve", lambda e: e.reciprocal(A("rs"), A("rs")), reads=[K("rs")], writes=[K("rs")])
        o_tt(c, "dve", A("oT"), A("oT"), A("rs"), OP.mult, [K("oT"), K("rs")], [K("oT")])
        o_act(c, A("zs"), zin[:, :], AF.Silu, [kz], [K("zs")])
        o_stt(c, "dve", A("oT"), A("oT"), col(14), A("zs"), OP.mult, OP.mult, [K("oT"), "pcol", K("zs")], [K("oT")])
        o_dma(c, out[0, :, t0:t0 + TC], A("oT"), r=[K("oT")])

    return dn_chunk


HW = 16
NPC2 = 4 + 24 + 8 + 8 + 144 + 48 + 8 + 8
PC_BGLU, PC_BGATE, PC_L1G, PC_L1B, PC_CW, PC_CB, PC_L2G, PC_L2B = 0, 4, 28, 36, 44, 188, 236, 244


def build_KC(TOKC=4096):
    c = Ctx()
    s = c.s
    NT = TOKC // 512
    xT = c.din("xT", [D, HW + TOKC])
    brT = c.din("brT", [1536, HW + TOKC])
    flag_d = c.din("flag", [128, 1])
    w_gate = c.din("w_gate", [D, 3072])
    w_glu = c.din("w_glu", [512, 512])
    w_br = c.din("w_br", [1536, D])
    w_out = c.din("w_out", [D, D])
    w_up = c.din("w_up", [D, 6144])
    w_down = c.din("w_down", [3072, D])
    pc_d = c.din("pc", [128, NPC2])
    out = c.dout("x2T", [D, TOKC])

    pc = c.sb("pc", [128, NPC2])
    o_dma(c, pc[:], pc_d, w=["pc"])
    flag = c.sb("flag", [128, 1])
    o_dma(c, flag[:], flag_d, w=["flag"])
    onesm = c.sb("onesm", [128, 128])
    o_memset(c, "pool", onesm[:], 1.0 / D, ["onesm"])
    wb = make_wbufs(c)
    x_f = c.sb("x_f", [128, 8, 512])
    x_b = c.sb("x_b", [128, 8, 512], BF16)
    brst = c.sb("brst", [128, 4, 512])
    y_f = c.sb("y_f", [128, 4, 512])
    br_b = c.sb("br_b", [128, 12, 512], BF16)
    s5o = c.sb("s5o", [128, 4, 512], BF16)
    GA = c.sb("GA", [128, 24, 512], BF16)
    acc = c.sb("acc", [128, 8, 512])
    mx_b = c.sb("mx_b", [128, 8, 512], BF16)
    sq = [c.sb("sq%d" % i, [128, 512]) for i in range(2)]
    mean = c.sb("mean", [128, 512])
    var = c.sb("var", [128, 512])
    xc = [c.sb("xc%d" % i, [128, 512]) for i in range(2)]
    tmpf = [c.sb("tmpf%d" % i, [128, 512]) for i in range(2)]
    hst = [c.sb("hst%d" % i, [128, 2 + 512]) for i in range(2)]
    cv = [c.sb("cv%d" % i, [128, 512]) for i in range(2)]
    carry = c.sb("carry", [128, 48, 2])
    o_memset(c, "pool", carry[:], 0.0, ["carry"])
    xv = xT.rearrange("(kt p) t -> p kt t", p=128)
    bv = brT.rearrange("(kt p) t -> p kt t", p=128)
    ov = out.rearrange("(kt p) t -> p kt t", p=128)
    cnt = {"e": 0}

    def pcol(i):
        return pc[:, i:i + 1]

    def layer_norm(n, gi, bi_, make_bf):
        bA, bB = 6, 7
        for i in range(8):
            o_mm(c, c.banks[bA][:, 0:n], onesm[:, :], x_f[:, i, 0:n], ["onesm", ("x_f", i)], [("ps", bA)], start=(i == 0), stop=(i == 7))
        for i in range(8):
            q = sq[i % 2]
            o_act(c, q[:, 0:n], x_f[:, i, 0:n], AF.Square, [("x_f", i)], [("sq", i % 2)])
            o_mm(c, c.banks[bB][:, 0:n], onesm[:, :], q[:, 0:n], ["onesm", ("sq", i % 2)], [("ps", bB)], start=(i == 0), stop=(i == 7))
        o_cp(c, "act", mean[:, 0:n], c.banks[bA][:, 0:n], [("ps", bA)], ["mean"])
        o_tt(c, "dve", var[:, 0:n], mean[:, 0:n], mean[:, 0:n], OP.mult, ["mean"], ["var"])
        o_tt(c, "dve", var[:, 0:n], c.banks[bB][:, 0:n], var[:, 0:n], OP.subtract, [("ps", bB), "var"], ["var"])
        o_act(c, var[:, 0:n], var[:, 0:n], AF.Sqrt, ["var"], ["var"], bias=1e-5)
        c.s.op("dve", lambda e: e.reciprocal(var[:, 0:n], var[:, 0:n]), reads=["var"], writes=["var"])
        for i in range(8):
            t_ = xc[i % 2]
            o_tt(c, "dve", t_[:, 0:n], x_f[:, i, 0:n], mean[:, 0:n], OP.subtract, [("x_f", i), "mean"], [("xc", i % 2)])
            o_tt(c, "pool", t_[:, 0:n], t_[:, 0:n], var[:, 0:n], OP.mult, [("xc", i % 2), "var"], [("xc", i % 2)])
            o_act(c, x_f[:, i, 0:n], t_[:, 0:n], AF.Identity, [("xc", i % 2), "pc"], [("x_f", i)], bias=pcol(bi_ + i), scale=pcol(gi + i))
            if make_bf:
                o_cp(c, "act", x_b[:, i, 0:n], x_f[:, i, 0:n], [("x_f", i)], [("x_b", i)])

    def tile(ti):
        halo = ti < 0
        n = HW if halo else 512
        tok0 = 0 if halo else HW + ti * 512
        o_dma(c, x_f[:, :, 0:n], xv[:, :, tok0:tok0 + n], w=[("x_f", i) for i in range(8)])
        for i in range(8):
            o_cp(c, "act" if i % 2 else "dve", x_b[:, i, 0:n], x_f[:, i, 0:n], [("x_f", i)], [("x_b", i)])
        for g in range(3):
            dst = y_f if g == 1 else brst
            kd = "y_f" if g == 1 else "brst"
            o_dma(c, dst[:, :, 0:n], bv[:, g * 4:(g + 1) * 4, tok0:tok0 + n], w=[kd])
            for i in range(4):
                o_cp(c, "dve" if i % 2 else "act", br_b[:, g * 4 + i, 0:n], dst[:, i, 0:n], [kd], [("br_b", g * 4 + i)])
        def epi_glu(i, ps, pk):
            t_ = tmpf[i % 2]
            o_act(c, t_[:, 0:n], ps, AF.Sigmoid, [pk, "pc"], [("tmpf", i % 2)], bias=pcol(PC_BGLU + i))
            o_tt(c, "dve", s5o[:, i, 0:n], y_f[:, i, 0:n], t_[:, 0:n], OP.mult, ["y_f", ("tmpf", i % 2)], [("s5o", i)])
        stream_linear(c, w_glu, 512, [(i * 128, 128) for i in range(4)], lambda kt: (br_b[:, 4 + kt, 0:n], ("br_b", 4 + kt)), n, epi_glu, wb)
        def epi_gate(i, ps, pk):
            o_act(c, GA[:, i, 0:n], ps, AF.Sigmoid, [pk, "pc"], [("GA", i)], bias=pcol(PC_BGATE + i))
        stream_linear(c, w_gate, D, [(i * 128, 128) for i in range(24)], lambda kt: (x_b[:, kt, 0:n], ("x_b", kt)), n, epi_gate, wb)
        for g in range(3):
            def rhs(kt, g=g):
                if g == 1:
                    return s5o[:, kt, 0:n], ("s5o", kt)
                return br_b[:, g * 4 + kt, 0:n], ("br_b", g * 4 + kt)

            def epi_br(i, ps, pk, g=g):
                if g == 0:
                    o_tt(c, "dve", acc[:, i, 0:n], ps, GA[:, i, 0:n], OP.mult, [pk, ("GA", i)], [("acc", i)])
                else:
                    t_ = tmpf[i % 2]
                    o_tt(c, "dve", t_[:, 0:n], ps, GA[:, g * 8 + i, 0:n], OP.mult, [pk, ("GA", g * 8 + i)], [("tmpf", i % 2)])
                    if g == 1:
                        o_tt(c, "pool", acc[:, i, 0:n], acc[:, i, 0:n], t_[:, 0:n], OP.add, [("acc", i), ("tmpf", i % 2)], [("acc", i)])
                    else:
                        o_tt(c, "pool", mx_b[:, i, 0:n], acc[:, i, 0:n], t_[:, 0:n], OP.add, [("acc", i), ("tmpf", i % 2)], [("mx_b", i)])
            stream_linear(c, w_br[g * 512:(g + 1) * 512, :], 512, [(i * 128, 128) for i in range(8)], rhs, n, epi_br, wb)
        def epi_out(i, ps, pk):
            o_stt(c, "dve", x_f[:, i, 0:n], x_f[:, i, 0:n], ALPHA, ps, OP.mult, OP.add, [("x_f", i), pk], [("x_f", i)])
        stream_linear(c, w_out, D, [(i * 128, 128) for i in range(8)], lambda kt: (mx_b[:, kt, 0:n], ("mx_b", kt)), n, epi_out, wb)
        layer_norm(n, PC_L1G, PC_L1B, True)
        order = []
        for b_ in range(6):
            order += [4 * b_ + q for q in range(4)] + [24 + 4 * b_ + q for q in range(4)]
        cols = [(t_ * 128, 128) for t_ in order]

        def epi_up(idx, ps, pk):
            t_ = order[idx]
            h = hst[idx % 2]
            kh = ("hst", idx % 2)
            cvt = cv[idx % 2]
            kc = ("cv", idx % 2)
            o_cp(c, "pool", h[:, 0:2], carry[:, t_, :], ["carry"], [kh])
            o_cp(c, "act", h[:, 2:2 + n], ps, [pk], [kh])
            o_ts(c, "dve", cvt[:, 0:n], h[:, 0:n], pcol(PC_CW + t_), pcol(PC_CB + t_), OP.mult, OP.add, [kh, "pc"], [kc])
            o_stt(c, "dve", cvt[:, 0:n], h[:, 1:1 + n], pcol(PC_CW + 48 + t_), cvt[:, 0:n], OP.mult, OP.add, [kh, "pc", kc], [kc])
            o_stt(c, "dve", cvt[:, 0:n], h[:, 2:2 + n], pcol(PC_CW + 96 + t_), cvt[:, 0:n], OP.mult, OP.add, [kh, "pc", kc], [kc])
            if halo:
                o_ts(c, "pool", carry[:, t_, :], h[:, n:n + 2], flag[:, 0:1], None, OP.mult, None, [kh, "flag"], ["carry"])
            else:
                o_cp(c, "pool", carry[:, t_, :], h[:, n:n + 2], [kh], ["carry"])
                if t_ < 24:
                    o_act(c, GA[:, t_, 0:n], cvt[:, 0:n], AF.Gelu_apprx_tanh, [kc], [("GA", t_)])
                else:
                    o_tt(c, "pool", GA[:, t_ - 24, 0:n], GA[:, t_ - 24, 0:n], cvt[:, 0:n], OP.mult, [("GA", t_ - 24), kc], [("GA", t_ - 24)])
        stream_linear(c, w_up, D, cols, lambda kt: (x_b[:, kt, 0:n], ("x_b", kt)), n, epi_up, wb)
        if halo:
            return
        stream_linear(c, w_down, 3072, [(i * 128, 128) for i in range(8)], lambda kt: (GA[:, kt, 0:n], ("GA", kt)), n, epi_out, wb)
        layer_norm(n, PC_L2G, PC_L2B, False)
        o_dma(c, ov[:, :, ti * 512:(ti + 1) * 512], x_f[:, :, 0:n], r=[("x_f", i) for i in range(8)])

    s.begin_rec()
    tile(-1)
    for ti in range(NT):
        tile(ti)
    s.replay(hoist_wloads(s.end_rec()))
    return c.finish()


def kc_params(inp, l):
    def cols(v):
        return np.ascontiguousarray(v.reshape(-1, 128).T)
    pc = np.concatenate([cols(inp["s5_b_glu"][l]), cols(inp["b_gate"][l].reshape(-1)), cols(inp["ln1_g"][l]), cols(inp["ln1_b"][l]),
                         cols(inp["ffn_conv_w"][l][0]), cols(inp["ffn_conv_w"][l][1]), cols(inp["ffn_conv_w"][l][2]),
                         cols(inp["ffn_conv_b"][l]), cols(inp["ln2_g"][l]), cols(inp["ln2_b"][l])], axis=1).astype(np.float32)
    return {"pc": np.ascontiguousarray(pc), "w_gate": np.ascontiguousarray(inp["w_in"][l][:, NMIX:]),
            "w_glu": inp["s5_w_glu"][l], "w_br": np.ascontiguousarray(inp["w_branch"][l].reshape(1536, D)),
            "w_out": inp["w_out"][l], "w_up": inp["ffn_w_up"][l], "w_down": inp["ffn_w_down"][l]}


def kb_consts():
    cm = np.ones((128, TC), np.float32)
    cm[:, ::64] = 0.0
    jj = np.arange(64)[:, None]
    cc = np.arange(64)[None, :]
    negmask = np.where(jj <= cc, 0.0, -30000.0).astype(np.float32)
    strict = (jj < cc).astype(np.float32)
    eye = np.eye(64, dtype=np.float32)
    c64 = np.stack([np.tile(negmask, (1, 8)), np.tile(strict, (1, 8)), np.tile(eye, (1, 8))], axis=1)
    sel = np.array([[0.0, 1.0, 1.0, 0.0], [-1.0, 0.0, 0.0, 1.0]], np.float32)
    return {"identd": np.eye(128, dtype=np.float32), "cmask": cm, "c64": np.ascontiguousarray(c64), "sel": sel}


def kb_params(inp, l, j):
    pcol = np.zeros((128, NPC), np.float32)
    sl = slice(j * 128, (j + 1) * 128)
    for which in range(3):
        for tap in range(4):
            pcol[:, which * 4 + tap] = inp["dn_conv_w"][l][tap, which * 512 + j * 128: which * 512 + (j + 1) * 128]
    pcol[:, 12] = inp["dn_a_log"][l][j]
    pcol[:, 13] = inp["dn_dt_bias"][l][j]
    pcol[:, 14] = inp["dn_norm_w"][l]
    for tap in range(4):
        pcol[:, 15 + tap] = inp["lru_conv_w"][l][tap, sl]
    pcol[:, 19] = inp["lru_conv_b"][l][sl]
    pcol[:, 20] = inp["lru_b_a"][l].reshape(512)[sl]
    pcol[:, 21] = inp["lru_b_x"][l].reshape(512)[sl]
    pcol[:, 22] = inp["lru_lam"][l][sl]
    pcol[:, 35] = inp["s5_d"][l].reshape(512)[sl]
    lruw = np.zeros((128, 2, 128), np.float32)
    for n in range(2):
        lruw[n * 64:(n + 1) * 64, 0, n * 64:(n + 1) * 64] = inp["lru_w_a"][l][2 * j + n]
        lruw[n * 64:(n + 1) * 64, 1, n * 64:(n + 1) * 64] = inp["lru_w_x"][l][2 * j + n]
    s5row = np.zeros((128, 3, 512), np.float32)
    s5b = np.zeros((128, 2, 4, 128), np.float32)
    s5c = np.zeros((128, 2, 4, 128), np.float32)
    for m in range(4):
        for gl in range(2):
            gloc = 2 * m + gl
            g = 8 * j + gloc
            rs = slice(gl * 64, (gl + 1) * 64)
            pcol[rs, 23 + m] = inp["s5_lam_re"][l][g]
            pcol[rs, 27 + m] = inp["s5_lam_im"][l][g]
            pcol[rs, 31 + m] = inp["s5_log_step"][l][g]
            cs_ = slice(m * 128 + gl * 64, m * 128 + (gl + 1) * 64)
            s5row[:, 0, cs_] = inp["s5_lam_re"][l][g][None, :]
            s5row[:, 1, cs_] = inp["s5_lam_im"][l][g][None, :]
            s5row[:, 2, cs_] = inp["s5_log_step"][l][g]
            ch = slice(gloc * 16, (gloc + 1) * 16)
            s5b[ch, 0, m, rs] = inp["s5_b_re"][l][g].T
            s5b[ch, 1, m, rs] = inp["s5_b_im"][l][g].T
            s5c[rs, 0, m, ch] = inp["s5_c_re"][l][g].T
            s5c[rs, 1, m, ch] = inp["s5_c_im"][l][g].T
    return {"pcol": pcol, "lruw": lruw, "s5row": s5row, "s5b": s5b, "s5c": s5c}


def kb_acts(projT, j):
    q = projT[0 + j * 128: 0 + (j + 1) * 128]
    k = projT[512 + j * 128: 512 + (j + 1) * 128]
    v = projT[1024 + j * 128: 1024 + (j + 1) * 128]
    z = projT[1536 + j * 128: 1536 + (j + 1) * 128]
    return {"qkvz": np.ascontiguousarray(np.stack([q, k, v, z])),
            "bd": np.ascontiguousarray(np.stack([projT[2048 + j], projT[2052 + j]])),
            "s5u": np.ascontiguousarray(projT[2056 + j * 128: 2056 + (j + 1) * 128]),
            "lrx": np.ascontiguousarray(projT[2568 + j * 128: 2568 + (j + 1) * 128]),
            "lry": np.ascontiguousarray(projT[3080 + j * 128: 3080 + (j + 1) * 128])}


_PROGS = {}


def _prog(name, fn):
    if name not in _PROGS:
        _PROGS[name] = fn()
    return _PROGS[name]


def _run(nc, in_maps):
    return run_bass_kernel_spmd(nc, in_maps, core_ids=list(range(NCORE))).results


def kernel(**inp):
    inp = {k: np.asarray(v) for k, v in inp.items()}
    x = inp["x"].astype(np.float32, copy=False)
    TOKC = SEQ // 4
    xT = [np.ascontiguousarray(x[b].T) for b in range(BATCH)]
    cst = kb_consts()
    for l in range(DEPTH):
        w = np.ascontiguousarray(inp["w_in"][l][:, :NMIX])
        ins = []
        for core in range(NCORE):
            b, q = core // 4, core % 4
            ins.append({"xT": np.ascontiguousarray(xT[b][:, q * TOKC:(q + 1) * TOKC]), "w": w})
        res = _run(_prog("KA", lambda: build_KA(TOKC, NMIX)), ins)
        projT = [np.concatenate([res[b * 4 + q]["pT"] for q in range(4)], axis=1) for b in range(BATCH)]
        ins = []
        for core in range(NCORE):
            b, j = core // 4, core % 4
            d = dict(cst)
            d.update(kb_params(inp, l, j))
            d.update(kb_acts(projT[b], j))
            ins.append(d)
        res = _run(_prog("KB", lambda: build_KB(SEQ)), ins)
        brT = []
        for b in range(BATCH):
            parts = [[res[b * 4 + j]["o"][g] for j in range(4)] for g in range(3)]
            brT.append(np.concatenate([np.concatenate(p, axis=0) for p in parts], axis=0))
        del projT
        par = kc_params(inp, l)
        ins = []
        for core in range(NCORE):
            b, q = core // 4, core % 4
            t0 = q * TOKC
            d = dict(par)
            xs = np.zeros((D, HW + TOKC), np.float32)
            bs = np.zeros((1536, HW + TOKC), np.float32)
            lo = max(0, t0 - HW)
            xs[:, HW - (t0 - lo):] = xT[b][:, lo:t0 + TOKC]
            bs[:, HW - (t0 - lo):] = brT[b][:, lo:t0 + TOKC]
            d["xT"], d["brT"] = xs, bs
            d["flag"] = np.full((128, 1), 0.0 if t0 == 0 else 1.0, np.float32)
            ins.append(d)
        res = _run(_prog("KC", lambda: build_KC(TOKC)), ins)
        xT = [np.concatenate([res[b * 4 + q]["x2T"] for q in range(4)], axis=1) for b in range(BATCH)]
    return np.ascontiguousarray(np.stack([xT[b].T for b in range(BATCH)])).astype(np.float32)
```

```python
import contextlib
import numpy as np
import concourse.bass as bass
import concourse.mybir as mybir
from concourse.bass_utils import run_bass_kernel_spmd

F32 = mybir.dt.float32
BF16 = mybir.dt.bfloat16
AF = mybir.ActivationFunctionType
OP = mybir.AluOpType

D = 1024
SEQ = 16384
BATCH = 2
DEPTH = 4
NCORE = 8
ALPHA = (2.0 * DEPTH) ** 0.25
NMIX = 3592


class Sched:
    ENGS = ("pe", "dve", "act", "pool", "sp")
    NLANE = 8

    def __init__(self, nc, stack, same_sync=("dve", "act", "pool")):
        self.nc = nc
        self.ops = {e: [] for e in self.ENGS}
        self.last_w = {}
        self.readers = {}
        self.same_sync = same_sync
        self.sem = {e: stack.enter_context(nc.semaphore("s_" + e)) for e in ("pe", "dve", "act", "pool")}
        self.lanes = {}
        for q in ("sp", "pool", "act"):
            self.lanes[q] = [stack.enter_context(nc.semaphore("d_%s%d" % (q, i))) for i in range(self.NLANE)]
        self.lane_cnt = {q: [0] * self.NLANE for q in self.lanes}
        self.lane_rr = {q: 0 for q in self.lanes}
        self.ccount = {e: 0 for e in self.ENGS}
        self.rec = None
        self.cur_tag = None

    def begin_rec(self):
        self.rec = []

    def end_rec(self):
        r, self.rec = self.rec, None
        return r

    def replay(self, ops):
        for (eng, fn, reads, writes, dma, tag) in ops:
            self.op(eng, fn, reads, writes, dma)

    def op(self, eng, fn, reads=(), writes=(), dma=False, tag=None):
        if self.rec is not None:
            self.rec.append((eng, fn, tuple(reads), tuple(writes), dma, tag if tag is not None else self.cur_tag))
            return
        deps = set()
        for k in reads:
            if k in self.last_w:
                deps.add(self.last_w[k])
        for k in writes:
            if k in self.last_w:
                deps.add(self.last_w[k])
            for r in self.readers.get(k, ()):
                deps.add(r)
        if dma:
            q = eng
            ln = self.lane_rr[q]
            self.lane_rr[q] = (ln + 1) % self.NLANE
            inc = 1 if dma == "cc" else 16
            self.lane_cnt[q][ln] += inc
            me = ("dma", q, ln, self.lane_cnt[q][ln], inc)
        else:
            self.ccount[eng] += 1
            me = ("c", eng, self.ccount[eng])
        deps.discard(me)
        for k in writes:
            self.last_w[k] = me
            self.readers[k] = []
        for k in reads:
            self.readers.setdefault(k, []).append(me)
        self.ops[eng].append((fn, deps, me))

    def emit(self):
        nc = self.nc
        sched = self

        def run(eng, e):
            waited = {}
            for (fn, deps, me) in sched.ops[eng]:
                need = {}
                for d in deps:
                    if d[0] == "dma":
                        s, v = sched.lanes[d[1]][d[2]], d[3]
                    else:
                        if d[1] == eng and me[0] == "c" and eng not in sched.same_sync:
                            continue
                        s, v = sched.sem[d[1]], d[2]
                    key = id(s)
                    if need.get(key, (None, 0))[1] < v:
                        need[key] = (s, v)
                for key, (s, v) in need.items():
                    if waited.get(key, 0) < v:
                        e.wait_ge(s, v)
                        waited[key] = v
                ins = fn(e)
                if me[0] == "dma":
                    ins.then_inc(sched.lanes[me[1]][me[2]], me[4])
                else:
                    ins.then_inc(sched.sem[eng], 1)
            if eng in sched.lanes:
                for ln, s in enumerate(sched.lanes[eng]):
                    if sched.lane_cnt[eng][ln] > 0:
                        e.wait_ge(s, sched.lane_cnt[eng][ln])

        with nc.Block() as block:
            @block.tensor
            def _(e):
                run("pe", e)

            @block.vector
            def _(e):
                run("dve", e)

            @block.scalar
            def _(e):
                run("act", e)

            @block.gpsimd
            def _(e):
                run("pool", e)

            @block.sync
            def _(e):
                run("sp", e)


class Ctx:
    def __init__(self, name="k"):
        self.nc = bass.Bass("TRN2", target_bir_lowering=False)
        self.stack = contextlib.ExitStack()
        self.s = Sched(self.nc, self.stack)
        self.nps = 0
        self.uid = 0
        self.banks = [self.stack.enter_context(self.nc.psum_tensor("ps%d" % i, [128, 512], F32)) for i in range(8)]
        self.bank_rr = 0

    def sb(self, name, shape, dt=F32):
        return self.stack.enter_context(self.nc.sbuf_tensor("sb_" + name, list(shape), dt))

    def din(self, name, shape, dt=F32):
        return self.nc.dram_tensor(name, list(shape), dt, kind="ExternalInput").ap()

    def dout(self, name, shape, dt=F32):
        return self.nc.dram_tensor(name, list(shape), dt, kind="ExternalOutput").ap()

    def bank(self, lo=0, hi=4):
        b = lo + (self.bank_rr % (hi - lo))
        self.bank_rr += 1
        return b

    def finish(self):
        self.s.emit()
        self.stack.close()
        return self.nc


def _rr(seq, state=[0]):
    state[0] += 1
    return seq[state[0] % len(seq)]


def stream_linear(c, W, K, cols, rhs, ntok, epi, wbufs, cast_engs=("act",), banks=(0, 6)):
    s = c.s
    KT = K // 128
    KTg = min(KT, 8)
    KG = KT // KTg
    Wv = W.rearrange("(kt p) n -> p kt n", p=128)
    i = 0
    while i < len(cols):
        blk = [cols[i]]
        for cc in cols[i + 1:i + 4]:
            if cc[0] == blk[-1][0] + blk[-1][1]:
                blk.append(cc)
            else:
                break
        c0 = blk[0][0]
        ncol = sum(b_[1] for b_ in blk)
        bks = [c.bank(*banks) for _ in blk]
        for kg in range(KG):
            bi = wbufs["rr"] % 2
            wbufs["rr"] += 1
            st, bf = wbufs["st"][bi], wbufs["bf"][bi]
            stv = st[:, 0:KTg * ncol].rearrange("p (kt n) -> p kt n", kt=KTg)
            bfv = bf[:, 0:KTg * ncol].rearrange("p (kt n) -> p kt n", kt=KTg)
            g_ = wbufs["rr"] - 1
            s.cur_tag = ("wl", g_)
            s.op("sp", lambda e, o=stv, i_=Wv[:, kg * KTg:(kg + 1) * KTg, c0:c0 + ncol]: e.dma_start(out=o, in_=i_),
                 writes=[("wst", bi)], dma=True)
            ce = _rr(cast_engs)
            o_cp(c, ce, bf[:, 0:KTg * ncol], st[:, 0:KTg * ncol], [("wst", bi)], [("wbf", bi)])
            s.cur_tag = ("wu", g_)
            off = 0
            for j, (col0, nc_) in enumerate(blk):
                ps = c.banks[bks[j]][0:nc_, 0:ntok]
                for k in range(KTg):
                    kt = kg * KTg + k
                    r_ap, r_key = rhs(kt)
                    s.op("pe", lambda e, o=ps, l=bfv[:, k, off:off + nc_], r=r_ap, a=(kt == 0), z=(kt == KT - 1):
                         e.matmul(o, l, r, start=a, stop=z),
                         reads=[("wbf", bi), r_key], writes=[("ps", bks[j])])
                off += nc_
        s.cur_tag = None
        for j, (col0, nc_) in enumerate(blk):
            epi(i + j, c.banks[bks[j]][0:nc_, 0:ntok], ("ps", bks[j]))
        i += len(blk)


def hoist_wloads(ops):
    wl = {}
    rest = []
    for o in ops:
        t = o[5]
        if isinstance(t, tuple) and t[0] == "wl":
            wl.setdefault(t[1], []).append(o)
        else:
            rest.append(o)
    out = []
    emitted = set()
    for o in rest:
        t = o[5]
        if isinstance(t, tuple) and t[0] == "wu":
            for g in (t[1], t[1] + 1):
                if g in wl and g not in emitted:
                    out.extend(wl[g])
                    emitted.add(g)
        out.append(o)
    for g in sorted(wl):
        assert g in emitted
    return out


def merge_streams(a, b):
    out = []
    ia = ib = 0
    na, nb = len(a), len(b)
    while ia < na or ib < nb:
        if ib >= nb or (ia < na and ia * nb <= ib * na):
            out.append(a[ia])
            ia += 1
        else:
            out.append(b[ib])
            ib += 1
    return out


def make_wbufs(c):
    return {"st": [c.sb("wst%d" % i, [128, 4096], F32) for i in range(2)],
            "bf": [c.sb("wbf%d" % i, [128, 4096], BF16) for i in range(2)], "rr": 0}


def build_KA(TOK=4096, NW=NMIX):
    c = Ctx()
    s = c.s
    xT = c.din("xT", [D, TOK])
    w = c.din("w", [D, NW])
    out = c.dout("pT", [NW, TOK])
    xbf = c.sb("xbf", [128, 8, TOK], BF16)
    xst = [c.sb("xst%d" % i, [128, 1024], F32) for i in range(2)]
    ost = [c.sb("ost%d" % i, [128, 512], F32) for i in range(4)]
    wb = make_wbufs(c)
    xv = xT.rearrange("(kt p) t -> p kt t", p=128)
    n = 0
    for kt in range(8):
        for t0 in range(0, TOK, 1024):
            bi = n % 2
            s.op("sp", lambda e, o=xst[bi][:, :], i_=xv[:, kt, t0:t0 + 1024]: e.dma_start(out=o, in_=i_),
                 writes=[("xst", bi)], dma=True)
            s.op("dve", lambda e, o=xbf[:, kt, t0:t0 + 1024], i_=xst[bi][:, :]: e.tensor_copy(o, i_),
                 reads=[("xst", bi)], writes=[("xbf", kt, t0 // 512), ("xbf", kt, t0 // 512 + 1)])
            n += 1
    cols = [(c0, min(128, NW - c0)) for c0 in range(0, NW, 128)]
    cnt = [0]
    s.begin_rec()
    for tt in range(TOK // 512):
        def rhs(kt, tt=tt):
            return xbf[:, kt, tt * 512:(tt + 1) * 512], ("xbf", kt, tt)

        def epi(i, ps, pkey, tt=tt):
            k = cnt[0] % 4
            cnt[0] += 1
            nr = cols[i][1]
            eng = "act" if cnt[0] % 2 else "dve"
            if eng == "act":
                s.op("act", lambda e, o=ost[k][0:nr, :], i_=ps: e.copy(o, i_), reads=[pkey], writes=[("ost", k)])
            else:
                s.op("dve", lambda e, o=ost[k][0:nr, :], i_=ps: e.tensor_copy(o, i_), reads=[pkey], writes=[("ost", k)])
            s.op("sp", lambda e, o=out[cols[i][0]:cols[i][0] + nr, tt * 512:(tt + 1) * 512], i_=ost[k][0:nr, :]:
                 e.dma_start(out=o, in_=i_), reads=[("ost", k)], dma=True)
        stream_linear(c, w, D, cols, rhs, 512, epi, wb)
    s.replay(hoist_wloads(s.end_rec()))
    return c.finish()


F32R = mybir.dt.float32r
MMK = set()


def _ro(out, w):
    for k in w:
        if k in MMK:
            return out.bitcast(F32R)
    return out


def o_tt(c, eng, out, a, b, op, r, w):
    out = _ro(out, w)
    c.s.op(eng, lambda e: e.tensor_tensor(out, a, b, op), reads=r, writes=w)


def o_ts(c, eng, out, a, s1, s2, op0, op1, r, w):
    out = _ro(out, w)
    if s2 is None:
        c.s.op(eng, lambda e: e.tensor_scalar(out, a, s1, None, op0), reads=r, writes=w)
    else:
        c.s.op(eng, lambda e: e.tensor_scalar(out, a, s1, s2, op0, op1), reads=r, writes=w)


def o_stt(c, eng, out, a, sc, b, op0, op1, r, w):
    eng = "dve"
    out = _ro(out, w)
    c.s.op(eng, lambda e: e.scalar_tensor_tensor(out, a, sc, b, op0, op1), reads=r, writes=w)


def o_act(c, out, a, func, r, w, bias=None, scale=None):
    out = _ro(out, w)
    kw = {}
    if bias is not None:
        kw["bias"] = bias
    if scale is not None:
        kw["scale"] = scale
    c.s.op("act", lambda e: e.activation(out, a, func, **kw), reads=r, writes=w)


def o_cp(c, eng, out, a, r, w):
    out = _ro(out, w)
    if eng == "act":
        c.s.op("act", lambda e: e.copy(out, a), reads=r, writes=w)
    else:
        c.s.op(eng, lambda e: e.tensor_copy(out, a), reads=r, writes=w)


def o_mm(c, out, l, rr, r, w, start=True, stop=True, f32r=False):
    if f32r:
        l, rr = l.bitcast(F32R), rr.bitcast(F32R)
    c.s.op("pe", lambda e: e.matmul(out, l, rr, start=start, stop=stop), reads=r, writes=w)


def o_tr(c, out, a, ident, r, w):
    c.s.op("pe", lambda e: e.transpose(out, a, ident), reads=r, writes=w)


def o_dma(c, out, a, r=(), w=(), q="sp"):
    c.s.op(q, lambda e: e.dma_start(out=out, in_=a), reads=r, writes=w, dma=True)


def o_scan(c, out, d0, d1, init, r, w):
    out = _ro(out, w)
    c.s.op("dve", lambda e: e.tensor_tensor_scan(out, d0, d1, init, OP.mult, OP.add), reads=r, writes=w)


def o_memset(c, eng, out, v, w):
    c.s.op(eng, lambda e: e.memset(out, v), writes=w)


TWO_PI = float(2 * np.pi)


def sin_cos(c, th_ap, r, temps, ki, kq):
    outs = []
    for idx, shift in ((0, 0.0), (1, float(np.pi / 2))):
        (a, ka), (kf, kk), (res, kr) = temps[3 * idx: 3 * idx + 3]
        o_ts(c, "dve", a, th_ap, shift, None, OP.add, None, r, [ka])
        o_ts(c, "dve", kf, a, 1.0 / TWO_PI, None, OP.mult, None, [ka], [kk])
        o_cp(c, "dve", ki, kf, [kk], [kq])
        o_cp(c, "dve", kf, ki, [kq], [kk])
        o_stt(c, "dve", a, kf, -TWO_PI, a, OP.mult, OP.add, [kk, ka], [ka])
        o_ts(c, "dve", a, a, float(np.pi), float(-np.pi), OP.min, OP.max, [ka], [ka])
        o_act(c, res, a, AF.Sin, [ka], [kr])
        outs.append((res, kr))
    return outs[0], outs[1]


TC = 512
NPC = 36
S5POOL = "pool"


def build_KB(S=SEQ, do=("lru", "s5", "dn"), dbg=False):
    c = Ctx()
    s = c.s
    MMK.clear()
    MMK.update(["s5_Bb", "s5_Cm2", "ones_r", "D_sq", "D_kc", "D_qc", "D_kb", "D_wT", "D_qd",
                "D64_U0", "D64_U1", "D64_L0", "D64_L1", "D64_X", "D64_AT",
                ("Dtm", "kbg"), ("Dtm", "vb"), ("Dtm", "kt"), ("dnS", 0), ("dnS", 1), ("D_vn", 0), ("D_vn", 1),
                ("S_uinr", 0), ("S_uinr", 1), ("S_h", 0, 0), ("S_h", 0, 1), ("S_h", 1, 0), ("S_h", 1, 1)])
    NSC = S // TC
    qkvz = c.din("qkvz", [4, 128, S])
    bd = c.din("bd", [2, S])
    s5u = c.din("s5u", [128, S])
    lrx = c.din("lrx", [128, S])
    lry = c.din("lry", [128, S])
    pcol_d = c.din("pcol", [128, NPC])
    lruw_d = c.din("lruw", [128, 2, 128])
    s5row_d = c.din("s5row", [128, 3, 512])
    s5b_d = c.din("s5b", [128, 2, 4, 128])
    s5c_d = c.din("s5c", [128, 2, 4, 128])
    ident_d = c.din("identd", [128, 128])
    cmask_d = c.din("cmask", [128, TC])
    c64_d = c.din("c64", [64, 3, 512])
    sel_d = c.din("sel", [2, 4])
    out = c.dout("o", [3, 128, S])

    pcol = c.sb("pcol", [128, NPC])
    o_dma(c, pcol[:], pcol_d, w=["pcol"])
    ident = c.sb("ident", [128, 128])
    o_dma(c, ident[:], ident_d, w=["ident"])
    ones = c.sb("ones", [128, 128])
    o_memset(c, "pool", ones[:], 1.0, ["ones"])
    ones_r = c.sb("ones_r", [128, 128])
    o_cp(c, "dve", ones_r[:], ones[:], ["ones"], ["ones_r"])

    def col(i):
        return pcol[:, i:i + 1]

    if "lru" in do:
        lruw = c.sb("lruw", [128, 2, 128])
        o_dma(c, lruw[:], lruw_d, w=["lruw"])
        c8 = c.sb("c8", [128, 1])
        o_act(c, c8[:], col(22), AF.Exp, ["pcol"], ["c8"], scale=-1.0)
        o_act(c, c8[:], c8[:], AF.Ln, ["c8"], ["c8"], bias=1.0)
        o_ts(c, "dve", c8[:], c8[:], -8.0, None, OP.mult, None, ["c8"], ["c8"])
        lru_h = c.sb("lru_h", [128, 1])
        o_memset(c, "dve", lru_h[:], 0.0, ["lru_h"])
        L_xin = [c.sb("L_xin%d" % i, [128, 3 + TC]) for i in range(2)]
        L_yin = [c.sb("L_yin%d" % i, [128, TC]) for i in range(2)]
        L_t = [c.sb("L_t%d" % i, [128, TC]) for i in range(6)]

    def lru_chunk(sc):
        t0 = sc * TC
        bi = sc % 2
        xin, yin = L_xin[bi], L_yin[bi]
        kx, ky = ("L_xin", bi), ("L_yin", bi)
        if sc == 0:
            o_memset(c, "pool", xin[:, 0:3], 0.0, [kx])
            o_dma(c, xin[:, 3:3 + TC], lrx[:, 0:TC], w=[kx])
        else:
            o_dma(c, xin[:, :], lrx[:, t0 - 3:t0 + TC], w=[kx])
        o_dma(c, yin[:, :], lry[:, t0:t0 + TC], w=[ky])
        xc, r_, i_, a_, m_, h_ = [t[:, :] for t in L_t]
        K = ["L_t%d" % i for i in range(6)]
        o_act(c, xc, xin[:, 0:TC], AF.Identity, [kx, "pcol"], [K[0]], bias=col(19), scale=col(15))
        for k in (1, 2, 3):
            o_stt(c, "pool", xc, xin[:, k:k + TC], col(15 + k), xc, OP.mult, OP.add, [kx, "pcol", K[0]], [K[0]])
        b1, b2 = c.bank(), c.bank()
        o_mm(c, c.banks[b1][:, 0:TC], lruw[:, 0, :], xc, ["lruw", K[0]], [("ps", b1)])
        o_mm(c, c.banks[b2][:, 0:TC], lruw[:, 1, :], xc, ["lruw", K[0]], [("ps", b2)])
        o_act(c, r_, c.banks[b1][:, 0:TC], AF.Sigmoid, [("ps", b1), "pcol"], [K[1]], bias=col(20))
        o_act(c, i_, c.banks[b2][:, 0:TC], AF.Sigmoid, [("ps", b2), "pcol"], [K[2]], bias=col(21))
        o_act(c, a_, r_, AF.Exp, [K[1], "c8"], [K[3]], scale=c8[:, 0:1])
        o_tt(c, "pool", m_, a_, a_, OP.mult, [K[3]], [K[4]])
        o_act(c, m_, m_, AF.Sqrt, [K[4]], [K[4]], bias=1.0, scale=-1.0)
        if sc == 0:
            o_memset(c, "pool", m_[:, 0:1], 1.0, [K[4]])
        o_tt(c, "pool", m_, m_, i_, OP.mult, [K[4], K[2]], [K[4]])
        o_tt(c, "pool", m_, m_, xc, OP.mult, [K[4], K[0]], [K[4]])
        o_scan(c, h_, a_, m_, lru_h[:, 0:1], [K[3], K[4], "lru_h"], [K[5]])
        o_cp(c, "dve", lru_h[:, 0:1], h_[:, TC - 1:TC], [K[5]], ["lru_h"])
        o_act(c, r_, yin[:, :], AF.Gelu_apprx_tanh, [ky], [K[1]])
        o_tt(c, "pool", r_, r_, h_, OP.mult, [K[1], K[5]], [K[1]])
        o_dma(c, out[2, :, t0:t0 + TC], r_, r=[K[1]])

    if "s5" in do:
        dl = c.sb("s5_dl", [128, 4])
        o_act(c, dl[:], pcol[:, 31:35], AF.Exp, ["pcol"], ["s5_dl"])
        rr = c.sb("s5_r", [128, 4])
        o_tt(c, "dve", rr[:], pcol[:, 23:27], dl[:], OP.mult, ["pcol", "s5_dl"], ["s5_r"])
        o_act(c, rr[:], rr[:], AF.Exp, ["s5_r"], ["s5_r"])
        th = c.sb("s5_th", [128, 4])
        o_tt(c, "dve", th[:], pcol[:, 27:31], dl[:], OP.mult, ["pcol", "s5_dl"], ["s5_th"])
        sct = [c.sb("s5ct%d" % i, [128, 4]) for i in range(6)]
        scki = c.sb("s5cki", [128, 4], mybir.dt.int32)
        (sn, ksn), (cs, kcs) = sin_cos(c, th[:], ["s5_th"], [(t_[:], "s5ct%d" % i) for i, t_ in enumerate(sct)], scki[:], "s5cki")
        Ct = c.sb("s5_Ct", [128, 4, TC])
        St = c.sb("s5_St", [128, 4, TC])
        o_memset(c, "dve", Ct[:, :, 0:1], 1.0, ["s5_Ct"])
        o_memset(c, "dve", St[:, :, 0:1], 0.0, ["s5_St"])
        cw = c.sb("s5_cw", [128, 4])
        sw = c.sb("s5_sw", [128, 4])
        nsw = c.sb("s5_nsw", [128, 4])
        tq = c.sb("s5_tq", [128, 4])
        o_cp(c, "dve", cw[:], cs, [kcs], ["s5_cw"])
        o_cp(c, "dve", sw[:], sn, [ksn], ["s5_sw"])
        S_w = [[c.sb("S_w%d_%d" % (p, i), [128, TC]) for i in range(8)] for p in range(2)]
        tmpT = S_w[1][7]
        w_ = 1
        while w_ < TC:
            o_ts(c, "dve", nsw[:], sw[:], -1.0, None, OP.mult, None, ["s5_sw"], ["s5_nsw"])
            for m in range(4):
                o_ts(c, "dve", tmpT[:, 0:w_], Ct[:, m, 0:w_], cw[:, m:m + 1], None, OP.mult, None, ["s5_Ct", "s5_cw"], [("S_w", 1, 7)])
                o_stt(c, "dve", Ct[:, m, w_:2 * w_], St[:, m, 0:w_], nsw[:, m:m + 1], tmpT[:, 0:w_], OP.mult, OP.add,
                      ["s5_St", "s5_nsw", ("S_w", 1, 7)], ["s5_Ct"])
                o_ts(c, "dve", tmpT[:, 0:w_], St[:, m, 0:w_], cw[:, m:m + 1], None, OP.mult, None, ["s5_St", "s5_cw"], [("S_w", 1, 7)])
                o_stt(c, "dve", St[:, m, w_:2 * w_], Ct[:, m, 0:w_], sw[:, m:m + 1], tmpT[:, 0:w_], OP.mult, OP.add,
                      ["s5_Ct", "s5_sw", ("S_w", 1, 7)], ["s5_St"])
            o_tt(c, "dve", tq[:], sw[:], sw[:], OP.mult, ["s5_sw"], ["s5_tq"])
            o_tt(c, "dve", sw[:], cw[:], sw[:], OP.mult, ["s5_cw", "s5_sw"], ["s5_sw"])
            o_ts(c, "dve", sw[:], sw[:], 2.0, None, OP.mult, None, ["s5_sw"], ["s5_sw"])
            o_tt(c, "dve", cw[:], cw[:], cw[:], OP.mult, ["s5_cw"], ["s5_cw"])
            o_tt(c, "dve", cw[:], cw[:], tq[:], OP.subtract, ["s5_cw", "s5_tq"], ["s5_cw"])
            w_ *= 2
        o_ts(c, "dve", nsw[:], sw[:], -1.0, None, OP.mult, None, ["s5_sw"], ["s5_nsw"])
        row = c.sb("s5row", [128, 3, 512])
        o_dma(c, row[:], s5row_d, w=["s5row"])
        def SW(p, i):
            return S_w[p][i][:, :], ("S_w", p, i)
        (dlr, kdlr), (er, ker), (thr, kthr), (nr, knr), (ni, kni), (den, kden), (t2, kt2), (fr, kfr) = [SW(0, i) for i in range(8)]
        (fi, kfi) = SW(1, 0)
        rki = c.sb("s5rki", [128, 512], mybir.dt.int32)
        o_act(c, dlr, row[:, 2, :], AF.Exp, ["s5row"], [kdlr])
        o_tt(c, "dve", er, row[:, 0, :], dlr, OP.mult, ["s5row", kdlr], [ker])
        o_act(c, er, er, AF.Exp, [ker], [ker])
        o_tt(c, "dve", thr, row[:, 1, :], dlr, OP.mult, ["s5row", kdlr], [kthr])
        (snr, ksnr), (csr, kcsr) = sin_cos(c, thr, [kthr], [SW(1, i) for i in range(1, 7)], rki[:], "s5rki")
        o_tt(c, "dve", nr, er, csr, OP.mult, [ker, kcsr], [knr])
        o_ts(c, "dve", nr, nr, -1.0, None, OP.add, None, [knr], [knr])
        o_tt(c, "dve", ni, er, snr, OP.mult, [ker, ksnr], [kni])
        o_tt(c, "dve", den, row[:, 0, :], row[:, 0, :], OP.mult, ["s5row"], [kden])
        o_tt(c, "dve", t2, row[:, 1, :], row[:, 1, :], OP.mult, ["s5row"], [kt2])
        o_tt(c, "dve", t2, den, t2, OP.add, [kden, kt2], [kt2])
        c.s.op("dve", lambda e: e.reciprocal(den, t2), reads=[kt2], writes=[kden])
        o_tt(c, "dve", fr, nr, row[:, 0, :], OP.mult, [knr, "s5row"], [kfr])
        o_tt(c, "dve", t2, ni, row[:, 1, :], OP.mult, [kni, "s5row"], [kt2])
        o_tt(c, "dve", fr, fr, t2, OP.add, [kfr, kt2], [kfr])
        o_tt(c, "dve", fr, fr, den, OP.mult, [kfr, kden], [kfr])
        o_tt(c, "dve", fi, ni, row[:, 0, :], OP.mult, [kni, "s5row"], [kfi])
        o_tt(c, "dve", t2, nr, row[:, 1, :], OP.mult, [knr, "s5row"], [kt2])
        o_tt(c, "dve", fi, fi, t2, OP.subtract, [kfi, kt2], [kfi])
        o_tt(c, "dve", fi, fi, den, OP.mult, [kfi, kden], [kfi])
        Bin = c.sb("s5_Bin", [128, 2, 512])
        o_dma(c, Bin[:], s5b_d.rearrange("p a m n -> p a (m n)"), w=["s5_Bin"])
        Bb = c.sb("s5_Bb", [128, 2, 512])
        o_tt(c, "dve", Bb[:, 0, :], fr, Bin[:, 0, :], OP.mult, [kfr, "s5_Bin"], ["s5_Bb"])
        o_tt(c, "dve", t2, fi, Bin[:, 1, :], OP.mult, [kfi, "s5_Bin"], [kt2])
        o_tt(c, "dve", Bb[:, 0, :], Bb[:, 0, :], t2, OP.subtract, ["s5_Bb", kt2], ["s5_Bb"])
        o_tt(c, "dve", Bb[:, 1, :], fr, Bin[:, 1, :], OP.mult, [kfr, "s5_Bin"], ["s5_Bb"])
        o_tt(c, "dve", t2, fi, Bin[:, 0, :], OP.mult, [kfi, "s5_Bin"], [kt2])
        o_tt(c, "dve", Bb[:, 1, :], Bb[:, 1, :], t2, OP.add, ["s5_Bb", kt2], ["s5_Bb"])
        Cm = row[:, 0:2, :]
        o_dma(c, Cm, s5c_d.rearrange("p a m n -> p a (m n)"), w=["s5row"])
        Cm2 = c.sb("s5_Cm2", [128, 2, 512])
        o_ts(c, "dve", Cm2[:, 0, :], Cm[:, 0, :], 1.0, None, OP.mult, None, ["s5row"], ["s5_Cm2"])
        o_ts(c, "dve", Cm2[:, 1, :], Cm[:, 1, :], -1.0, None, OP.mult, None, ["s5row"], ["s5_Cm2"])
        S_h = [[c.sb("S_h%d_%d" % (p, i), [128, TC]) for i in range(2)] for p in range(1)] * 2
        S_uinr = [c.sb("S_uinr%d" % i, [128, TC]) for i in range(1)] * 2
        s5_init = c.sb("s5_init", [128, 2, 4])
        o_memset(c, "dve", s5_init[:], 0.0, ["s5_init"])
        s5_tc = c.sb("s5_tc", [128, 4])
        if dbg:
            for nm, t_, shp, keys in (("Ct", Ct, [128, 4, TC], ["s5_Ct"]), ("St", St, [128, 4, TC], ["s5_St"]),
                                      ("Bb", Bb, [128, 2, 512], ["s5_Bb"]), ("rr", rr, [128, 4], ["s5_r"]),
                                      ("cw", cw, [128, 4], ["s5_cw"]), ("sw", sw, [128, 4], ["s5_sw"]),
                                      ("Cm", Cm2, [128, 2, 512], ["s5_Cm2"])):
                o_dma(c, c.dout("dbg_" + nm, shp), t_[:], r=keys)
        S_uin = [c.sb("S_uin%d" % i, [128, TC]) for i in range(2)]
        S_y = c.sb("S_y", [128, TC])

    def s5_chunk(sc):
        t0 = sc * TC
        bi = sc % 2
        uin = S_uin[bi]
        ku = ("S_uin", bi)
        o_dma(c, uin[:, :], s5u[:, t0:t0 + TC], w=[ku])
        uinr, kur = S_uinr[bi], ("S_uinr", 0)
        o_cp(c, "act", uinr[:, :], uin[:, :], [ku], [kur])
        by = 7
        for m in range(4):
            W = S_w[m % 2]
            KW = [("S_w", m % 2, i) for i in range(8)]
            b1, b2 = c.bank(), c.bank()
            pr, pi = c.banks[b1][:, 0:TC], c.banks[b2][:, 0:TC]
            o_mm(c, pr, Bb[:, 0, m * 128:(m + 1) * 128], uinr[:, :], ["s5_Bb", kur], [("ps", b1)], f32r=True)
            o_mm(c, pi, Bb[:, 1, m * 128:(m + 1) * 128], uinr[:, :], ["s5_Bb", kur], [("ps", b2)], f32r=True)
            Cj, Sj = Ct[:, m, :], St[:, m, :]
            t1, t2_, t3, t4, Xr, Xi = [t[:, :] for t in W[0:6]]
            hr, hi = S_h[m % 2][0][:, :], S_h[m % 2][1][:, :]
            KW[6], KW[7] = ("S_h", 0, 0), ("S_h", 0, 1)
            o_tt(c, "dve", t1, pr, Cj, OP.mult, [("ps", b1), "s5_Ct"], [KW[0]])
            o_tt(c, "dve", t2_, pi, Sj, OP.mult, [("ps", b2), "s5_St"], [KW[1]])
            o_tt(c, "dve", t3, pi, Cj, OP.mult, [("ps", b2), "s5_Ct"], [KW[2]])
            o_tt(c, "dve", t4, pr, Sj, OP.mult, [("ps", b1), "s5_St"], [KW[3]])
            o_tt(c, S5POOL, Xr, t1, t2_, OP.add, [KW[0], KW[1]], [KW[4]])
            o_tt(c, S5POOL, Xi, t3, t4, OP.subtract, [KW[2], KW[3]], [KW[5]])
            rb = rr[:, m:m + 1].to_broadcast([128, TC])
            o_scan(c, t1, rb, Xr, s5_init[:, 0, m:m + 1], ["s5_r", KW[4], "s5_init"], [KW[0]])
            o_scan(c, t3, rb, Xi, s5_init[:, 1, m:m + 1], ["s5_r", KW[5], "s5_init"], [KW[2]])
            o_ts(c, "dve", s5_tc[:, m:m + 1], t1[:, TC - 1:TC], cw[:, m:m + 1], None, OP.mult, None, [KW[0], "s5_cw"], ["s5_tc"])
            o_stt(c, "dve", s5_init[:, 0, m:m + 1], t3[:, TC - 1:TC], nsw[:, m:m + 1], s5_tc[:, m:m + 1], OP.mult, OP.add,
                  [KW[2], "s5_nsw", "s5_tc"], ["s5_init"])
            o_ts(c, "dve", s5_tc[:, m:m + 1], t3[:, TC - 1:TC], cw[:, m:m + 1], None, OP.mult, None, [KW[2], "s5_cw"], ["s5_tc"])
            o_stt(c, "dve", s5_init[:, 1, m:m + 1], t1[:, TC - 1:TC], sw[:, m:m + 1], s5_tc[:, m:m + 1], OP.mult, OP.add,
                  [KW[0], "s5_sw", "s5_tc"], ["s5_init"])
            o_tt(c, S5POOL, t2_, t1, Cj, OP.mult, [KW[0], "s5_Ct"], [KW[1]])
            o_tt(c, S5POOL, t4, t3, Sj, OP.mult, [KW[2], "s5_St"], [KW[3]])
            o_tt(c, "dve", hr, t2_, t4, OP.subtract, [KW[1], KW[3]], [KW[6]])
            o_tt(c, S5POOL, Xr, t1, Sj, OP.mult, [KW[0], "s5_St"], [KW[4]])
            o_tt(c, S5POOL, Xi, t3, Cj, OP.mult, [KW[2], "s5_Ct"], [KW[5]])
            o_tt(c, "dve", hi, Xr, Xi, OP.add, [KW[4], KW[5]], [KW[7]])
            o_mm(c, c.banks[by][:, 0:TC], Cm2[:, 0, m * 128:(m + 1) * 128], hr, ["s5_Cm2", KW[6]], [("ps", by)],
                 start=(m == 0), stop=False, f32r=True)
            o_mm(c, c.banks[by][:, 0:TC], Cm2[:, 1, m * 128:(m + 1) * 128], hi, ["s5_Cm2", KW[7]], [("ps", by)],
                 start=False, stop=(m == 3), f32r=True)
        o_stt(c, "dve", S_y[:, :], uin[:, :], col(35), c.banks[by][:, 0:TC], OP.mult, OP.add, [ku, "pcol", ("ps", by)], ["S_y"])
        o_act(c, S_y[:, :], S_y[:, :], AF.Gelu_apprx_tanh, ["S_y"], ["S_y"])
        o_dma(c, out[1, :, t0:t0 + TC], S_y[:, :], r=["S_y"])

    dn_chunk = build_dn(c, do, S, qkvz, bd, out, pcol, ident, ones, cmask_d, c64_d, sel_d, ones_r)
    LS, DP, DC = [], [], []
    for sc in range(NSC):
        s.begin_rec()
        if "lru" in do:
            lru_chunk(sc)
        if "s5" in do:
            s5_chunk(sc)
        LS.append(s.end_rec())
        s.begin_rec()
        if "dn" in do:
            dn_chunk(sc)
        d_ = s.end_rec()
        DP.append([o for o in d_ if o[5] != "chain"])
        DC.append([o for o in d_ if o[5] == "chain"])
    s.replay(LS[0])
    s.replay(DP[0])
    for sc in range(NSC):
        nxt = LS[sc + 1] if sc + 1 < NSC else []
        s.replay(merge_streams(DC[sc], nxt))
        if sc + 1 < NSC:
            s.replay(DP[sc + 1])
    return c.finish()


def build_dn(c, do, S, qkvz, bd, out, pcol, ident, ones, cmask_d, c64_d, sel_d, ones_r):
    if "dn" not in do:
        return None
    NCH = TC // 64

    def col(i):
        return pcol[:, i:i + 1]

    cmask = c.sb("cmask", [128, TC])
    o_dma(c, cmask[:], cmask_d, w=["cmask"])
    c64 = c.sb("c64", [64, 3, 512])
    o_dma(c, c64[:], c64_d, w=["c64"])
    sel = c.sb("sel", [2, 4])
    o_dma(c, sel[:], sel_d, w=["sel"])
    negmask, strict, ident8 = c64[:, 0, :], c64[:, 1, :], c64[:, 2, :]
    negA = c.sb("negA", [128, 1])
    o_act(c, negA[:], col(12), AF.Exp, ["pcol"], ["negA"])
    o_ts(c, "dve", negA[:], negA[:], -1.0, None, OP.mult, None, ["negA"], ["negA"])
    Sst = [c.sb("dnS%d" % i, [128, 128]) for i in range(2)]
    o_memset(c, "dve", Sst[1][:], 0.0, [("dnS", 1)])
    o_cp(c, "dve", Sst[0][:], Sst[1][:], [("dnS", 1)], [("dnS", 0)])
    Xin = [[c.sb("D_in%d_%d" % (w, i), [128, 3 + TC]) for i in range(2)] for w in range(3)]
    Zin = [c.sb("D_z%d" % i, [128, TC]) for i in range(2)]
    Brow = [c.sb("D_br%d" % i, [1, TC]) for i in range(2)]
    Drow = [c.sb("D_dr%d" % i, [1, TC]) for i in range(2)]
    names = ["qc", "kc", "vc", "sq", "rs", "beta", "gc", "eg", "egl", "kb", "kbg", "vb", "kt", "qd", "oT", "wT", "zs", "tmp"]
    T = {n: c.sb("D_" + n, [128, TC]) for n in names}
    T64 = {n: c.sb("D64_" + n, [64, 512]) for n in ["dm", "AT", "U0", "U1", "L0", "L1", "X"]}
    TM = {n: c.sb("Dtm_" + n, [64, NCH, 128]) for n in ["kbg", "vb", "kt", "u"]}
    Vn = [c.sb("D_vn%d" % i, [64, 128]) for i in range(2)]
    G1 = c.sb("D_G1", [2, TC])
    G2 = c.sb("D_G2", [2, TC])

    def A(n):
        return T[n][:, :]

    def K(n):
        return "D_" + n

    def A64(n):
        return T64[n][:, :]

    def K64(n):
        return "D64_" + n

    def dn_chunk(sc):
        t0 = sc * TC
        bi = sc % 2
        for w in range(3):
            xin, kx = Xin[w][bi], ("D_in", w, bi)
            if sc == 0:
                o_memset(c, "pool", xin[:, 0:3], 0.0, [kx])
                o_dma(c, xin[:, 3:3 + TC], qkvz[w, :, 0:TC], w=[kx])
            else:
                o_dma(c, xin[:, :], qkvz[w, :, t0 - 3:t0 + TC], w=[kx])
        zin, kz = Zin[bi], ("D_z", bi)
        o_dma(c, zin[:, :], qkvz[3, :, t0:t0 + TC], w=[kz])
        brow, drow = Brow[bi], Drow[bi]
        kbr, kdr = ("D_br", bi), ("D_dr", bi)
        o_dma(c, brow[:, :], bd[0:1, t0:t0 + TC], w=[kbr])
        o_dma(c, drow[:, :], bd[1:2, t0:t0 + TC], w=[kdr])
        for w, nm in enumerate(["qc", "kc", "vc"]):
            xin, kx = Xin[w][bi], ("D_in", w, bi)
            o_act(c, A(nm), xin[:, 0:TC], AF.Identity, [kx, "pcol"], [K(nm)], scale=col(w * 4 + 0))
            for k in (1, 2, 3):
                o_stt(c, "dve", A(nm), xin[:, k:k + TC], col(w * 4 + k), A(nm), OP.mult, OP.add, [kx, "pcol", K(nm)], [K(nm)])
            o_act(c, A(nm), A(nm), AF.Silu, [K(nm)], [K(nm)])
        for nm, scale in (("qc", 128.0 ** -0.5), ("kc", 1.0)):
            o_act(c, A("sq"), A(nm), AF.Square, [K(nm)], [K("sq")])
            b = c.bank()
            o_mm(c, c.banks[b][:, 0:TC], ones_r[:, :], A("sq"), ["ones_r", K("sq")], [("ps", b)], f32r=True)
            o_act(c, A("rs"), c.banks[b][:, 0:TC], AF.Sqrt, [("ps", b)], [K("rs")], bias=1e-6)
            c.s.op("dve", lambda e: e.reciprocal(A("rs"), A("rs")), reads=[K("rs")], writes=[K("rs")])
            o_stt(c, "dve", A(nm), A(nm), scale, A("rs"), OP.mult, OP.mult, [K(nm), K("rs")], [K(nm)])
        b = c.bank()
        o_mm(c, c.banks[b][:, 0:TC], ones[0:1, :], brow[:, :], ["ones", kbr], [("ps", b)])
        o_act(c, A("beta"), c.banks[b][:, 0:TC], AF.Sigmoid, [("ps", b)], [K("beta")])
        b = c.bank()
        o_mm(c, c.banks[b][:, 0:TC], ones[0:1, :], drow[:, :], ["ones", kdr], [("ps", b)])
        o_act(c, A("tmp"), c.banks[b][:, 0:TC], AF.Exp, [("ps", b), "pcol"], [K("tmp")], bias=col(13))
        o_act(c, A("tmp"), A("tmp"), AF.Ln, [K("tmp")], [K("tmp")], bias=1.0)
        o_ts(c, "dve", A("tmp"), A("tmp"), negA[:, 0:1], None, OP.mult, None, [K("tmp"), "negA"], [K("tmp")])
        o_scan(c, A("gc"), cmask[:, :], A("tmp"), 0.0, ["cmask", K("tmp")], [K("gc")])
        o_act(c, A("eg"), A("gc"), AF.Exp, [K("gc")], [K("eg")])
        gc3 = A("gc").rearrange("p (n c) -> p n c", c=64)
        o_tt(c, "dve", A("egl").rearrange("p (n c) -> p n c", c=64), gc3[:, :, 63:64].to_broadcast([128, NCH, 64]), gc3,
             OP.subtract, [K("gc")], [K("egl")])
        o_act(c, A("egl"), A("egl"), AF.Exp, [K("egl")], [K("egl")])
        o_tt(c, "dve", A("kb"), A("kc"), A("beta"), OP.mult, [K("kc"), K("beta")], [K("kb")])
        o_tt(c, "pool", A("kbg"), A("kb"), A("eg"), OP.mult, [K("kb"), K("eg")], [K("kbg")])
        o_tt(c, "pool", A("vb"), A("vc"), A("beta"), OP.mult, [K("vc"), K("beta")], [K("vb")])
        o_tt(c, "pool", A("kt"), A("kc"), A("egl"), OP.mult, [K("kc"), K("egl")], [K("kt")])
        o_tt(c, "pool", A("qd"), A("qc"), A("eg"), OP.mult, [K("qc"), K("eg")], [K("qd")])
        o_ts(c, "dve", G1[:, :], A("gc")[0:2, :], sel[:, 0:1], sel[:, 1:2], OP.mult, OP.add, [K("gc"), "sel"], ["D_G1"])
        o_ts(c, "dve", G2[:, :], A("gc")[0:2, :], sel[:, 2:3], sel[:, 3:4], OP.mult, OP.add, [K("gc"), "sel"], ["D_G2"])
        b = c.bank()
        for n in range(NCH):
            cs_ = slice(n * 64, (n + 1) * 64)
            o_mm(c, c.banks[b][0:64, cs_], G1[:, cs_], G2[:, cs_], ["D_G1", "D_G2"], [("ps", b)])
        o_stt(c, "dve", A64("dm"), c.banks[b][0:64, 0:512], 0.0, negmask, OP.min, OP.add, [("ps", b), "c64"], [K64("dm")])
        o_act(c, A64("dm"), A64("dm"), AF.Exp, [K64("dm")], [K64("dm")])
        b = c.bank()
        for n in range(NCH):
            cs_ = slice(n * 64, (n + 1) * 64)
            o_mm(c, c.banks[b][0:64, cs_], A("kc")[:, cs_], A("qc")[:, cs_], [K("kc"), K("qc")], [("ps", b)], f32r=True)
        o_tt(c, "dve", A64("AT"), c.banks[b][0:64, 0:512], A64("dm"), OP.mult, [("ps", b), K64("dm")], [K64("AT")])
        b = c.bank()
        for n in range(NCH):
            cs_ = slice(n * 64, (n + 1) * 64)
            o_mm(c, c.banks[b][0:64, cs_], A("kc")[:, cs_], A("kb")[:, cs_], [K("kc"), K("kb")], [("ps", b)], f32r=True)
        o_tt(c, "dve", A64("U0"), c.banks[b][0:64, 0:512], A64("dm"), OP.mult, [("ps", b), K64("dm")], [K64("U0")])
        o_tt(c, "pool", A64("U0"), A64("U0"), strict, OP.mult, [K64("U0"), "c64"], [K64("U0")])
        b = c.bank()
        for n in range(NCH):
            cs_ = slice(n * 64, (n + 1) * 64)
            o_tr(c, c.banks[b][0:64, cs_], A64("U0")[:, cs_], ident[0:64, 0:64], [K64("U0"), "ident"], [("ps", b)])
        o_cp(c, "act", A64("L0"), c.banks[b][0:64, 0:512], [("ps", b)], [K64("L0")])
        o_tt(c, "dve", A64("X"), ident8, A64("U0"), OP.subtract, ["c64", K64("U0")], [K64("X")])
        cu, cl = "U0", "L0"
        for lvl in range(5):
            nu, nl = ("U1", "L1") if cu == "U0" else ("U0", "L0")
            b = c.bank()
            for n in range(NCH):
                cs_ = slice(n * 64, (n + 1) * 64)
                o_mm(c, c.banks[b][0:64, cs_], A64(cl)[:, cs_], A64(cu)[:, cs_], [K64(cl), K64(cu)], [("ps", b)], f32r=True)
            o_cp(c, "act", A64(nu), c.banks[b][0:64, 0:512], [("ps", b)], [K64(nu)])
            b = c.bank()
            for n in range(NCH):
                cs_ = slice(n * 64, (n + 1) * 64)
                o_mm(c, c.banks[b][0:64, cs_], A64(cu)[:, cs_], A64(cl)[:, cs_], [K64(cl), K64(cu)], [("ps", b)], f32r=True)
            o_cp(c, "dve", A64(nl), c.banks[b][0:64, 0:512], [("ps", b)], [K64(nl)])
            b = c.bank()
            for n in range(NCH):
                cs_ = slice(n * 64, (n + 1) * 64)
                o_mm(c, c.banks[b][0:64, cs_], A64(nl)[:, cs_], A64("X")[:, cs_], [K64(nl), K64("X")], [("ps", b)], f32r=True)
            o_tt(c, "dve", A64("X"), A64("X"), c.banks[b][0:64, 0:512], OP.add, [K64("X"), ("ps", b)], [K64("X")])
            cu, cl = nu, nl
        ei = 0
        for nm in ("kbg", "vb", "kt"):
            for half in range(2):
                b = c.bank()
                for q in range(4):
                    n = half * 4 + q
                    o_tr(c, c.banks[b][0:64, q * 128:(q + 1) * 128], A(nm)[:, n * 64:(n + 1) * 64], ident[:, :],
                         [K(nm), "ident"], [("ps", b)])
                o_cp(c, "act" if ei % 2 == 0 else "dve", TM[nm][:, half * 4:(half + 1) * 4, :].rearrange("p a b -> p (a b)"),
                     c.banks[b][0:64, 0:512], [("ps", b)], [("Dtm", nm)])
                ei += 1
        for half in range(2):
            b = c.bank()
            for q in range(4):
                n = half * 4 + q
                o_mm(c, c.banks[b][0:64, q * 128:(q + 1) * 128], A64("X")[:, n * 64:(n + 1) * 64], TM["vb"][:, n, :],
                     [K64("X"), ("Dtm", "vb")], [("ps", b)], f32r=True)
            o_cp(c, "act", TM["u"][:, half * 4:(half + 1) * 4, :].rearrange("p a b -> p (a b)"), c.banks[b][0:64, 0:512],
                 [("ps", b)], [("Dtm", "u")])
        b = c.bank()
        for n in range(NCH):
            cs_ = slice(n * 64, (n + 1) * 64)
            o_mm(c, c.banks[b][:, cs_], TM["kbg"][:, n, :], A64("X")[:, cs_], [("Dtm", "kbg"), K64("X")], [("ps", b)], f32r=True)
        o_cp(c, "act", A("wT"), c.banks[b][:, 0:512], [("ps", b)], [K("wT")])
        c.s.cur_tag = "chain"
        bo = 6
        for n in range(NCH):
            gi = sc * NCH + n
            cur, nxt = Sst[gi % 2], Sst[(gi + 1) % 2]
            kcur, knxt = ("dnS", gi % 2), ("dnS", (gi + 1) % 2)
            vn, kvn = Vn[gi % 2], ("D_vn", gi % 2)
            cs_ = slice(n * 64, (n + 1) * 64)
            b1 = c.bank(4, 6)
            o_mm(c, c.banks[b1][0:64, 0:128], A("wT")[:, cs_], cur[:, :], [K("wT"), kcur], [("ps", b1)], f32r=True)
            o_tt(c, "dve", vn[:, :], TM["u"][:, n, :], c.banks[b1][0:64, 0:128], OP.subtract, [("Dtm", "u"), ("ps", b1)], [kvn])
            o_mm(c, c.banks[bo][:, cs_], cur[:, :], A("qd")[:, cs_], [kcur, K("qd")], [("ps", bo)], start=True, stop=False, f32r=True)
            o_mm(c, c.banks[bo][:, cs_], vn[:, :], A64("AT")[:, cs_], [kvn, K64("AT")], [("ps", bo)], start=False, stop=True, f32r=True)
            b2 = c.bank(4, 6)
            o_mm(c, c.banks[b2][:, 0:128], TM["kt"][:, n, :], vn[:, :], [("Dtm", "kt"), kvn], [("ps", b2)], f32r=True)
            o_stt(c, "dve", nxt[:, :], cur[:, :], A("eg")[:, n * 64 + 63:n * 64 + 64], c.banks[b2][:, 0:128], OP.mult, OP.add,
                  [kcur, K("eg"), ("ps", b2)], [knxt])
        o_cp(c, "act", A("oT"), c.banks[bo][:, 0:512], [("ps", bo)], [K("oT")])
        o_act(c, A("sq"), A("oT"), AF.Square, [K("oT")], [K("sq")])
        b = c.bank(4, 6)
        o_mm(c, c.banks[b][:, 0:TC], ones_r[:, :], A("sq"), ["ones_r", K("sq")], [("ps", b)], f32r=True)
        o_act(c, A("rs"), c.banks[b][:, 0:TC], AF.Sqrt, [("ps", b)], [K("rs")], bias=1e-6, scale=1.0 / 128.0)
        c.s.op("dve", lambda e: e.reciprocal(A("rs"), A("rs")), reads=[K("rs")], writes=[K("rs")])
        o_tt(c, "dve", A("oT"), A("oT"), A("rs"), OP.mult, [K("oT"), K("rs")], [K("oT")])
        o_act(c, A("zs"), zin[:, :], AF.Silu, [kz], [K("zs")])
        o_stt(c, "dve", A("oT"), A("oT"), col(14), A("zs"), OP.mult, OP.mult, [K("oT"), "pcol", K("zs")], [K("oT")])
        o_dma(c, out[0, :, t0:t0 + TC], A("oT"), r=[K("oT")])
        c.s.cur_tag = None

    return dn_chunk


HW = 16
NPC2 = 4 + 24 + 8 + 8 + 144 + 48 + 8 + 8
PC_BGLU, PC_BGATE, PC_L1G, PC_L1B, PC_CW, PC_CB, PC_L2G, PC_L2B = 0, 4, 28, 36, 44, 188, 236, 244


def build_KC(TOKC=4096):
    c = Ctx()
    s = c.s
    NT = TOKC // 512
    xT = c.din("xT", [D, HW + TOKC])
    brT = c.din("brT", [1536, HW + TOKC])
    flag_d = c.din("flag", [128, 1])
    w_gate = c.din("w_gate", [D, 3072])
    w_glu = c.din("w_glu", [512, 512])
    w_br = c.din("w_br", [1536, D])
    w_out = c.din("w_out", [D, D])
    w_up = c.din("w_up", [D, 6144])
    w_down = c.din("w_down", [3072, D])
    pc_d = c.din("pc", [128, NPC2])
    out = c.dout("x2T", [D, TOKC])

    pc = c.sb("pc", [128, NPC2])
    o_dma(c, pc[:], pc_d, w=["pc"])
    flag = c.sb("flag", [128, 1])
    o_dma(c, flag[:], flag_d, w=["flag"])
    onesm = c.sb("onesm", [128, 128])
    o_memset(c, "pool", onesm[:], 1.0 / D, ["onesm"])
    wb = make_wbufs(c)
    x_f = c.sb("x_f", [128, 8, 512])
    x_b = c.sb("x_b", [128, 8, 512], BF16)
    brst = c.sb("brst", [128, 4, 512])
    y_f = c.sb("y_f", [128, 4, 512])
    br_b = c.sb("br_b", [128, 12, 512], BF16)
    s5o = c.sb("s5o", [128, 4, 512], BF16)
    GA = c.sb("GA", [128, 24, 512], BF16)
    acc = c.sb("acc", [128, 8, 512])
    mx_b = c.sb("mx_b", [128, 8, 512], BF16)
    sq = [c.sb("sq%d" % i, [128, 512]) for i in range(2)]
    mean = c.sb("mean", [128, 512])
    var = c.sb("var", [128, 512])
    xc = [c.sb("xc%d" % i, [128, 512]) for i in range(2)]
    tmpf = [c.sb("tmpf%d" % i, [128, 512]) for i in range(2)]
    hst = [c.sb("hst%d" % i, [128, 2 + 512]) for i in range(2)]
    cv = [c.sb("cv%d" % i, [128, 512]) for i in range(2)]
    carry = c.sb("carry", [128, 48, 2])
    o_memset(c, "pool", carry[:], 0.0, ["carry"])
    xv = xT.rearrange("(kt p) t -> p kt t", p=128)
    bv = brT.rearrange("(kt p) t -> p kt t", p=128)
    ov = out.rearrange("(kt p) t -> p kt t", p=128)
    cnt = {"e": 0}

    def pcol(i):
        return pc[:, i:i + 1]

    def layer_norm(n, gi, bi_, make_bf):
        bA, bB = 6, 7
        for i in range(8):
            o_mm(c, c.banks[bA][:, 0:n], onesm[:, :], x_f[:, i, 0:n], ["onesm", ("x_f", i)], [("ps", bA)], start=(i == 0), stop=(i == 7))
        for i in range(8):
            q = sq[i % 2]
            o_act(c, q[:, 0:n], x_f[:, i, 0:n], AF.Square, [("x_f", i)], [("sq", i % 2)])
            o_mm(c, c.banks[bB][:, 0:n], onesm[:, :], q[:, 0:n], ["onesm", ("sq", i % 2)], [("ps", bB)], start=(i == 0), stop=(i == 7))
        o_cp(c, "act", mean[:, 0:n], c.banks[bA][:, 0:n], [("ps", bA)], ["mean"])
        o_tt(c, "dve", var[:, 0:n], mean[:, 0:n], mean[:, 0:n], OP.mult, ["mean"], ["var"])
        o_tt(c, "dve", var[:, 0:n], c.banks[bB][:, 0:n], var[:, 0:n], OP.subtract, [("ps", bB), "var"], ["var"])
        o_act(c, var[:, 0:n], var[:, 0:n], AF.Sqrt, ["var"], ["var"], bias=1e-5)
        c.s.op("dve", lambda e: e.reciprocal(var[:, 0:n], var[:, 0:n]), reads=["var"], writes=["var"])
        for i in range(8):
            t_ = xc[i % 2]
            o_tt(c, "dve", t_[:, 0:n], x_f[:, i, 0:n], mean[:, 0:n], OP.subtract, [("x_f", i), "mean"], [("xc", i % 2)])
            o_tt(c, "pool", t_[:, 0:n], t_[:, 0:n], var[:, 0:n], OP.mult, [("xc", i % 2), "var"], [("xc", i % 2)])
            o_act(c, x_f[:, i, 0:n], t_[:, 0:n], AF.Identity, [("xc", i % 2), "pc"], [("x_f", i)], bias=pcol(bi_ + i), scale=pcol(gi + i))
            if make_bf:
                o_cp(c, "act", x_b[:, i, 0:n], x_f[:, i, 0:n], [("x_f", i)], [("x_b", i)])

    def tile(ti):
        halo = ti < 0
        n = HW if halo else 512
        tok0 = 0 if halo else HW + ti * 512
        o_dma(c, x_f[:, :, 0:n], xv[:, :, tok0:tok0 + n], w=[("x_f", i) for i in range(8)])
        for i in range(8):
            o_cp(c, "act" if i % 2 else "dve", x_b[:, i, 0:n], x_f[:, i, 0:n], [("x_f", i)], [("x_b", i)])
        for g in range(3):
            dst = y_f if g == 1 else brst
            kd = "y_f" if g == 1 else "brst"
            o_dma(c, dst[:, :, 0:n], bv[:, g * 4:(g + 1) * 4, tok0:tok0 + n], w=[kd])
            for i in range(4):
                o_cp(c, "dve" if i % 2 else "act", br_b[:, g * 4 + i, 0:n], dst[:, i, 0:n], [kd], [("br_b", g * 4 + i)])
        def epi_glu(i, ps, pk):
            t_ = tmpf[i % 2]
            o_act(c, t_[:, 0:n], ps, AF.Sigmoid, [pk, "pc"], [("tmpf", i % 2)], bias=pcol(PC_BGLU + i))
            o_tt(c, "dve", s5o[:, i, 0:n], y_f[:, i, 0:n], t_[:, 0:n], OP.mult, ["y_f", ("tmpf", i % 2)], [("s5o", i)])
        stream_linear(c, w_glu, 512, [(i * 128, 128) for i in range(4)], lambda kt: (br_b[:, 4 + kt, 0:n], ("br_b", 4 + kt)), n, epi_glu, wb)
        def epi_gate(i, ps, pk):
            o_act(c, GA[:, i, 0:n], ps, AF.Sigmoid, [pk, "pc"], [("GA", i)], bias=pcol(PC_BGATE + i))
        stream_linear(c, w_gate, D, [(i * 128, 128) for i in range(24)], lambda kt: (x_b[:, kt, 0:n], ("x_b", kt)), n, epi_gate, wb)
        for g in range(3):
            def rhs(kt, g=g):
                if g == 1:
                    return s5o[:, kt, 0:n], ("s5o", kt)
                return br_b[:, g * 4 + kt, 0:n], ("br_b", g * 4 + kt)

            def epi_br(i, ps, pk, g=g):
                if g == 0:
                    o_tt(c, "dve", acc[:, i, 0:n], ps, GA[:, i, 0:n], OP.mult, [pk, ("GA", i)], [("acc", i)])
                else:
                    t_ = tmpf[i % 2]
                    o_tt(c, "dve", t_[:, 0:n], ps, GA[:, g * 8 + i, 0:n], OP.mult, [pk, ("GA", g * 8 + i)], [("tmpf", i % 2)])
                    if g == 1:
                        o_tt(c, "pool", acc[:, i, 0:n], acc[:, i, 0:n], t_[:, 0:n], OP.add, [("acc", i), ("tmpf", i % 2)], [("acc", i)])
                    else:
                        o_tt(c, "pool", mx_b[:, i, 0:n], acc[:, i, 0:n], t_[:, 0:n], OP.add, [("acc", i), ("tmpf", i % 2)], [("mx_b", i)])
            stream_linear(c, w_br[g * 512:(g + 1) * 512, :], 512, [(i * 128, 128) for i in range(8)], rhs, n, epi_br, wb)
        def epi_out(i, ps, pk):
            o_stt(c, "dve", x_f[:, i, 0:n], x_f[:, i, 0:n], ALPHA, ps, OP.mult, OP.add, [("x_f", i), pk], [("x_f", i)])
        stream_linear(c, w_out, D, [(i * 128, 128) for i in range(8)], lambda kt: (mx_b[:, kt, 0:n], ("mx_b", kt)), n, epi_out, wb)
        layer_norm(n, PC_L1G, PC_L1B, True)
        order = []
        for b_ in range(6):
            order += [4 * b_ + q for q in range(4)] + [24 + 4 * b_ + q for q in range(4)]
        cols = [(t_ * 128, 128) for t_ in order]

        def epi_up(idx, ps, pk):
            t_ = order[idx]
            h = hst[idx % 2]
            kh = ("hst", idx % 2)
            cvt = cv[idx % 2]
            kc = ("cv", idx % 2)
            o_cp(c, "pool", h[:, 0:2], carry[:, t_, :], ["carry"], [kh])
            o_cp(c, "act", h[:, 2:2 + n], ps, [pk], [kh])
            o_ts(c, "dve", cvt[:, 0:n], h[:, 0:n], pcol(PC_CW + t_), pcol(PC_CB + t_), OP.mult, OP.add, [kh, "pc"], [kc])
            o_stt(c, "dve", cvt[:, 0:n], h[:, 1:1 + n], pcol(PC_CW + 48 + t_), cvt[:, 0:n], OP.mult, OP.add, [kh, "pc", kc], [kc])
            o_stt(c, "dve", cvt[:, 0:n], h[:, 2:2 + n], pcol(PC_CW + 96 + t_), cvt[:, 0:n], OP.mult, OP.add, [kh, "pc", kc], [kc])
            if halo:
                o_ts(c, "pool", carry[:, t_, :], h[:, n:n + 2], flag[:, 0:1], None, OP.mult, None, [kh, "flag"], ["carry"])
            else:
                o_cp(c, "pool", carry[:, t_, :], h[:, n:n + 2], [kh], ["carry"])
                if t_ < 24:
                    o_act(c, GA[:, t_, 0:n], cvt[:, 0:n], AF.Gelu_apprx_tanh, [kc], [("GA", t_)])
                else:
                    o_tt(c, "pool", GA[:, t_ - 24, 0:n], GA[:, t_ - 24, 0:n], cvt[:, 0:n], OP.mult, [("GA", t_ - 24), kc], [("GA", t_ - 24)])
        stream_linear(c, w_up, D, cols, lambda kt: (x_b[:, kt, 0:n], ("x_b", kt)), n, epi_up, wb)
        if halo:
            return
        stream_linear(c, w_down, 3072, [(i * 128, 128) for i in range(8)], lambda kt: (GA[:, kt, 0:n], ("GA", kt)), n, epi_out, wb)
        layer_norm(n, PC_L2G, PC_L2B, False)
        o_dma(c, ov[:, :, ti * 512:(ti + 1) * 512], x_f[:, :, 0:n], r=[("x_f", i) for i in range(8)])

    s.begin_rec()
    tile(-1)
    for ti in range(NT):
        tile(ti)
    s.replay(hoist_wloads(s.end_rec()))
    return c.finish()


def kc_params(inp, l):
    def cols(v):
        return np.ascontiguousarray(v.reshape(-1, 128).T)
    pc = np.concatenate([cols(inp["s5_b_glu"][l]), cols(inp["b_gate"][l].reshape(-1)), cols(inp["ln1_g"][l]), cols(inp["ln1_b"][l]),
                         cols(inp["ffn_conv_w"][l][0]), cols(inp["ffn_conv_w"][l][1]), cols(inp["ffn_conv_w"][l][2]),
                         cols(inp["ffn_conv_b"][l]), cols(inp["ln2_g"][l]), cols(inp["ln2_b"][l])], axis=1).astype(np.float32)
    return {"pc": np.ascontiguousarray(pc), "w_gate": np.ascontiguousarray(inp["w_in"][l][:, NMIX:]),
            "w_glu": inp["s5_w_glu"][l], "w_br": np.ascontiguousarray(inp["w_branch"][l].reshape(1536, D)),
            "w_out": inp["w_out"][l], "w_up": inp["ffn_w_up"][l], "w_down": inp["ffn_w_down"][l]}


def kb_consts():
    cm = np.ones((128, TC), np.float32)
    cm[:, ::64] = 0.0
    jj = np.arange(64)[:, None]
    cc = np.arange(64)[None, :]
    negmask = np.where(jj <= cc, 0.0, -30000.0).astype(np.float32)
    strict = (jj < cc).astype(np.float32)
    eye = np.eye(64, dtype=np.float32)
    c64 = np.stack([np.tile(negmask, (1, 8)), np.tile(strict, (1, 8)), np.tile(eye, (1, 8))], axis=1)
    sel = np.array([[0.0, 1.0, 1.0, 0.0], [-1.0, 0.0, 0.0, 1.0]], np.float32)
    return {"identd": np.eye(128, dtype=np.float32), "cmask": cm, "c64": np.ascontiguousarray(c64), "sel": sel}


def kb_params(inp, l, j):
    pcol = np.zeros((128, NPC), np.float32)
    sl = slice(j * 128, (j + 1) * 128)
    for which in range(3):
        for tap in range(4):
            pcol[:, which * 4 + tap] = inp["dn_conv_w"][l][tap, which * 512 + j * 128: which * 512 + (j + 1) * 128]
    pcol[:, 12] = inp["dn_a_log"][l][j]
    pcol[:, 13] = inp["dn_dt_bias"][l][j]
    pcol[:, 14] = inp["dn_norm_w"][l]
    for tap in range(4):
        pcol[:, 15 + tap] = inp["lru_conv_w"][l][tap, sl]
    pcol[:, 19] = inp["lru_conv_b"][l][sl]
    pcol[:, 20] = inp["lru_b_a"][l].reshape(512)[sl]
    pcol[:, 21] = inp["lru_b_x"][l].reshape(512)[sl]
    pcol[:, 22] = inp["lru_lam"][l][sl]
    pcol[:, 35] = inp["s5_d"][l].reshape(512)[sl]
    lruw = np.zeros((128, 2, 128), np.float32)
    for n in range(2):
        lruw[n * 64:(n + 1) * 64, 0, n * 64:(n + 1) * 64] = inp["lru_w_a"][l][2 * j + n]
        lruw[n * 64:(n + 1) * 64, 1, n * 64:(n + 1) * 64] = inp["lru_w_x"][l][2 * j + n]
    s5row = np.zeros((128, 3, 512), np.float32)
    s5b = np.zeros((128, 2, 4, 128), np.float32)
    s5c = np.zeros((128, 2, 4, 128), np.float32)
    for m in range(4):
        for gl in range(2):
            gloc = 2 * m + gl
            g = 8 * j + gloc
            rs = slice(gl * 64, (gl + 1) * 64)
            pcol[rs, 23 + m] = inp["s5_lam_re"][l][g]
            pcol[rs, 27 + m] = inp["s5_lam_im"][l][g]
            pcol[rs, 31 + m] = inp["s5_log_step"][l][g]
            cs_ = slice(m * 128 + gl * 64, m * 128 + (gl + 1) * 64)
            s5row[:, 0, cs_] = inp["s5_lam_re"][l][g][None, :]
            s5row[:, 1, cs_] = inp["s5_lam_im"][l][g][None, :]
            s5row[:, 2, cs_] = inp["s5_log_step"][l][g]
            ch = slice(gloc * 16, (gloc + 1) * 16)
            s5b[ch, 0, m, rs] = inp["s5_b_re"][l][g].T
            s5b[ch, 1, m, rs] = inp["s5_b_im"][l][g].T
            s5c[rs, 0, m, ch] = inp["s5_c_re"][l][g].T
            s5c[rs, 1, m, ch] = inp["s5_c_im"][l][g].T
    return {"pcol": pcol, "lruw": lruw, "s5row": s5row, "s5b": s5b, "s5c": s5c}


def kb_acts(projT, j):
    q = projT[0 + j * 128: 0 + (j + 1) * 128]
    k = projT[512 + j * 128: 512 + (j + 1) * 128]
    v = projT[1024 + j * 128: 1024 + (j + 1) * 128]
    z = projT[1536 + j * 128: 1536 + (j + 1) * 128]
    return {"qkvz": np.ascontiguousarray(np.stack([q, k, v, z])),
            "bd": np.ascontiguousarray(np.stack([projT[2048 + j], projT[2052 + j]])),
            "s5u": np.ascontiguousarray(projT[2056 + j * 128: 2056 + (j + 1) * 128]),
            "lrx": np.ascontiguousarray(projT[2568 + j * 128: 2568 + (j + 1) * 128]),
            "lry": np.ascontiguousarray(projT[3080 + j * 128: 3080 + (j + 1) * 128])}


_PROGS = {}


def _prog(name, fn):
    if name not in _PROGS:
        _PROGS[name] = fn()
    return _PROGS[name]


def _run(nc, in_maps):
    return run_bass_kernel_spmd(nc, in_maps, core_ids=list(range(NCORE))).results


def kernel(**inp):
    inp = {k: np.asarray(v) for k, v in inp.items()}
    x = inp["x"].astype(np.float32, copy=False)
    TOKC = SEQ // 4
    xT = [np.ascontiguousarray(x[b].T) for b in range(BATCH)]
    cst = kb_consts()
    for l in range(DEPTH):
        w = np.ascontiguousarray(inp["w_in"][l][:, :NMIX])
        ins = []
        for core in range(NCORE):
            b, q = core // 4, core % 4
            ins.append({"xT": np.ascontiguousarray(xT[b][:, q * TOKC:(q + 1) * TOKC]), "w": w})
        res = _run(_prog("KA", lambda: build_KA(TOKC, NMIX)), ins)
        projT = [np.concatenate([res[b * 4 + q]["pT"] for q in range(4)], axis=1) for b in range(BATCH)]
        ins = []
        for core in range(NCORE):
            b, j = core // 4, core % 4
            d = dict(cst)
            d.update(kb_params(inp, l, j))
            d.update(kb_acts(projT[b], j))
            ins.append(d)
        res = _run(_prog("KB", lambda: build_KB(SEQ)), ins)
        brT = []
        for b in range(BATCH):
            parts = [[res[b * 4 + j]["o"][g] for j in range(4)] for g in range(3)]
            brT.append(np.concatenate([np.concatenate(p, axis=0) for p in parts], axis=0))
        del projT
        par = kc_params(inp, l)
        ins = []
        for core in range(NCORE):
            b, q = core // 4, core % 4
            t0 = q * TOKC
            d = dict(par)
            xs = np.zeros((D, HW + TOKC), np.float32)
            bs = np.zeros((1536, HW + TOKC), np.float32)
            lo = max(0, t0 - HW)
            xs[:, HW - (t0 - lo):] = xT[b][:, lo:t0 + TOKC]
            bs[:, HW - (t0 - lo):] = brT[b][:, lo:t0 + TOKC]
            d["xT"], d["brT"] = xs, bs
            d["flag"] = np.full((128, 1), 0.0 if t0 == 0 else 1.0, np.float32)
            ins.append(d)
        res = _run(_prog("KC", lambda: build_KC(TOKC)), ins)
        xT = [np.concatenate([res[b * 4 + q]["x2T"] for q in range(4)], axis=1) for b in range(BATCH)]
    return np.ascontiguousarray(np.stack([xT[b].T for b in range(BATCH)])).astype(np.float32)
```

```python
import contextlib
import numpy as np
import concourse.bass as bass
import concourse.mybir as mybir
from concourse.bass_utils import run_bass_kernel_spmd

F32 = mybir.dt.float32
BF16 = mybir.dt.bfloat16
AF = mybir.ActivationFunctionType
OP = mybir.AluOpType

D = 1024
SEQ = 16384
BATCH = 2
DEPTH = 4
NCORE = 8
ALPHA = (2.0 * DEPTH) ** 0.25
NMIX = 3592


class Sched:
    ENGS = ("pe", "dve", "act", "pool", "sp")
    NLANE = 8

    def __init__(self, nc, stack, same_sync=("dve", "act", "pool")):
        self.nc = nc
        self.ops = {e: [] for e in self.ENGS}
        self.last_w = {}
        self.readers = {}
        self.same_sync = same_sync
        self.sem = {e: stack.enter_context(nc.semaphore("s_" + e)) for e in ("pe", "dve", "act", "pool")}
        self.lanes = {}
        for q in ("sp", "pool", "act"):
            self.lanes[q] = [stack.enter_context(nc.semaphore("d_%s%d" % (q, i))) for i in range(self.NLANE)]
        self.lane_cnt = {q: [0] * self.NLANE for q in self.lanes}
        self.lane_rr = {q: 0 for q in self.lanes}
        self.ccount = {e: 0 for e in self.ENGS}
        self.rec = None
        self.cur_tag = None

    def begin_rec(self):
        self.rec = []

    def end_rec(self):
        r, self.rec = self.rec, None
        return r

    def replay(self, ops):
        for (eng, fn, reads, writes, dma, tag) in ops:
            self.op(eng, fn, reads, writes, dma)

    def op(self, eng, fn, reads=(), writes=(), dma=False, tag=None):
        if self.rec is not None:
            self.rec.append((eng, fn, tuple(reads), tuple(writes), dma, tag if tag is not None else self.cur_tag))
            return
        deps = set()
        for k in reads:
            if k in self.last_w:
                deps.add(self.last_w[k])
        for k in writes:
            if k in self.last_w:
                deps.add(self.last_w[k])
            for r in self.readers.get(k, ()):
                deps.add(r)
        if dma:
            q = eng
            ln = self.lane_rr[q]
            self.lane_rr[q] = (ln + 1) % self.NLANE
            inc = 1 if dma == "cc" else 16
            self.lane_cnt[q][ln] += inc
            me = ("dma", q, ln, self.lane_cnt[q][ln], inc)
        else:
            self.ccount[eng] += 1
            me = ("c", eng, self.ccount[eng])
        deps.discard(me)
        for k in writes:
            self.last_w[k] = me
            self.readers[k] = []
        for k in reads:
            self.readers.setdefault(k, []).append(me)
        self.ops[eng].append((fn, deps, me))

    def emit(self):
        nc = self.nc
        sched = self

        def run(eng, e):
            waited = {}
            for (fn, deps, me) in sched.ops[eng]:
                need = {}
                for d in deps:
                    if d[0] == "dma":
                        s, v = sched.lanes[d[1]][d[2]], d[3]
                    else:
                        if d[1] == eng and me[0] == "c" and eng not in sched.same_sync:
                            continue
                        s, v = sched.sem[d[1]], d[2]
                    key = id(s)
                    if need.get(key, (None, 0))[1] < v:
                        need[key] = (s, v)
                for key, (s, v) in need.items():
                    if waited.get(key, 0) < v:
                        e.wait_ge(s, v)
                        waited[key] = v
                ins = fn(e)
                if me[0] == "dma":
                    ins.then_inc(sched.lanes[me[1]][me[2]], me[4])
                else:
                    ins.then_inc(sched.sem[eng], 1)
            if eng in sched.lanes:
                for ln, s in enumerate(sched.lanes[eng]):
                    if sched.lane_cnt[eng][ln] > 0:
                        e.wait_ge(s, sched.lane_cnt[eng][ln])

        with nc.Block() as block:
            @block.tensor
            def _(e):
                run("pe", e)

            @block.vector
            def _(e):
                run("dve", e)

            @block.scalar
            def _(e):
                run("act", e)

            @block.gpsimd
            def _(e):
                run("pool", e)

            @block.sync
            def _(e):
                run("sp", e)


class Ctx:
    def __init__(self, name="k"):
        self.nc = bass.Bass("TRN2", target_bir_lowering=False)
        self.stack = contextlib.ExitStack()
        self.s = Sched(self.nc, self.stack)
        self.nps = 0
        self.uid = 0
        self.banks = [self.stack.enter_context(self.nc.psum_tensor("ps%d" % i, [128, 512], F32)) for i in range(8)]
        self.bank_rr = 0

    def sb(self, name, shape, dt=F32):
        return self.stack.enter_context(self.nc.sbuf_tensor("sb_" + name, list(shape), dt))

    def din(self, name, shape, dt=F32):
        return self.nc.dram_tensor(name, list(shape), dt, kind="ExternalInput").ap()

    def dout(self, name, shape, dt=F32):
        return self.nc.dram_tensor(name, list(shape), dt, kind="ExternalOutput").ap()

    def bank(self, lo=0, hi=4):
        b = lo + (self.bank_rr % (hi - lo))
        self.bank_rr += 1
        return b

    def finish(self):
        self.s.emit()
        self.stack.close()
        return self.nc


def _rr(seq, state=[0]):
    state[0] += 1
    return seq[state[0] % len(seq)]


def stream_linear(c, W, K, cols, rhs, ntok, epi, wbufs, cast_engs=("act",), banks=(0, 6)):
    s = c.s
    KT = K // 128
    KTg = min(KT, 8)
    KG = KT // KTg
    Wv = W.rearrange("(kt p) n -> p kt n", p=128)
    i = 0
    while i < len(cols):
        blk = [cols[i]]
        for cc in cols[i + 1:i + 4]:
            if cc[0] == blk[-1][0] + blk[-1][1]:
                blk.append(cc)
            else:
                break
        c0 = blk[0][0]
        ncol = sum(b_[1] for b_ in blk)
        bks = [c.bank(*banks) for _ in blk]
        for kg in range(KG):
            bi = wbufs["rr"] % 2
            wbufs["rr"] += 1
            st, bf = wbufs["st"][bi], wbufs["bf"][bi]
            stv = st[:, 0:KTg * ncol].rearrange("p (kt n) -> p kt n", kt=KTg)
            bfv = bf[:, 0:KTg * ncol].rearrange("p (kt n) -> p kt n", kt=KTg)
            g_ = wbufs["rr"] - 1
            s.cur_tag = ("wl", g_)
            s.op("sp", lambda e, o=stv, i_=Wv[:, kg * KTg:(kg + 1) * KTg, c0:c0 + ncol]: e.dma_start(out=o, in_=i_),
                 writes=[("wst", bi)], dma=True)
            ce = _rr(cast_engs)
            o_cp(c, ce, bf[:, 0:KTg * ncol], st[:, 0:KTg * ncol], [("wst", bi)], [("wbf", bi)])
            s.cur_tag = ("wu", g_)
            off = 0
            for j, (col0, nc_) in enumerate(blk):
                ps = c.banks[bks[j]][0:nc_, 0:ntok]
                for k in range(KTg):
                    kt = kg * KTg + k
                    r_ap, r_key = rhs(kt)
                    s.op("pe", lambda e, o=ps, l=bfv[:, k, off:off + nc_], r=r_ap, a=(kt == 0), z=(kt == KT - 1):
                         e.matmul(o, l, r, start=a, stop=z),
                         reads=[("wbf", bi), r_key], writes=[("ps", bks[j])])
                off += nc_
        s.cur_tag = None
        for j, (col0, nc_) in enumerate(blk):
            epi(i + j, c.banks[bks[j]][0:nc_, 0:ntok], ("ps", bks[j]))
        i += len(blk)


def hoist_wloads(ops):
    wl = {}
    rest = []
    for o in ops:
        t = o[5]
        if isinstance(t, tuple) and t[0] == "wl":
            wl.setdefault(t[1], []).append(o)
        else:
            rest.append(o)
    out = []
    emitted = set()
    for o in rest:
        t = o[5]
        if isinstance(t, tuple) and t[0] == "wu":
            for g in (t[1], t[1] + 1):
                if g in wl and g not in emitted:
                    out.extend(wl[g])
                    emitted.add(g)
        out.append(o)
    for g in sorted(wl):
        assert g in emitted
    return out


def merge_streams(a, b):
    out = []
    ia = ib = 0
    na, nb = len(a), len(b)
    while ia < na or ib < nb:
        if ib >= nb or (ia < na and ia * nb <= ib * na):
            out.append(a[ia])
            ia += 1
        else:
            out.append(b[ib])
            ib += 1
    return out


def make_wbufs(c):
    return {"st": [c.sb("wst%d" % i, [128, 4096], F32) for i in range(2)],
            "bf": [c.sb("wbf%d" % i, [128, 4096], BF16) for i in range(2)], "rr": 0}


def build_KA(TOK=4096, NW=NMIX):
    c = Ctx()
    s = c.s
    xT = c.din("xT", [D, TOK])
    w = c.din("w", [D, NW])
    out = c.dout("pT", [NW, TOK])
    xbf = c.sb("xbf", [128, 8, TOK], BF16)
    xst = [c.sb("xst%d" % i, [128, 1024], F32) for i in range(2)]
    ost = [c.sb("ost%d" % i, [128, 512], F32) for i in range(4)]
    wb = make_wbufs(c)
    xv = xT.rearrange("(kt p) t -> p kt t", p=128)
    n = 0
    for kt in range(8):
        for t0 in range(0, TOK, 1024):
            bi = n % 2
            s.op("sp", lambda e, o=xst[bi][:, :], i_=xv[:, kt, t0:t0 + 1024]: e.dma_start(out=o, in_=i_),
                 writes=[("xst", bi)], dma=True)
            s.op("dve", lambda e, o=xbf[:, kt, t0:t0 + 1024], i_=xst[bi][:, :]: e.tensor_copy(o, i_),
                 reads=[("xst", bi)], writes=[("xbf", kt, t0 // 512), ("xbf", kt, t0 // 512 + 1)])
            n += 1
    cols = [(c0, min(128, NW - c0)) for c0 in range(0, NW, 128)]
    cnt = [0]
    NTT = TOK // 512
    assert NTT <= 8
    Wv = w.rearrange("(kt p) n -> p kt n", p=128)
    s.begin_rec()
    for g, i in enumerate(range(0, len(cols), 4)):
        blk = cols[i:i + 4]
        c0 = blk[0][0]
        ncol = sum(b_[1] for b_ in blk)
        bi = g % 2
        st, bf = wb["st"][bi], wb["bf"][bi]
        stv = st[:, 0:8 * ncol].rearrange("p (kt n) -> p kt n", kt=8)
        bfv = bf[:, 0:8 * ncol].rearrange("p (kt n) -> p kt n", kt=8)
        s.cur_tag = ("wl", g)
        s.op("sp", lambda e, o=stv, i_=Wv[:, :, c0:c0 + ncol]: e.dma_start(out=o, in_=i_), writes=[("wst", bi)], dma=True)
        o_cp(c, "act", bf[:, 0:8 * ncol], st[:, 0:8 * ncol], [("wst", bi)], [("wbf", bi)])
        s.cur_tag = ("wu", g)
        off = 0
        for j, (col0, nr) in enumerate(blk):
            for kt in range(8):
                for tt in range(NTT):
                    s.op("pe", lambda e, o=c.banks[tt][0:nr, 0:512], l=bfv[:, kt, off:off + nr], r=xbf[:, kt, tt * 512:(tt + 1) * 512],
                         a=(kt == 0), z=(kt == 7): e.matmul(o, l, r, start=a, stop=z),
                         reads=[("wbf", bi), ("xbf", kt, tt)], writes=[("ps", tt)])
            s.cur_tag = None
            for tt in range(NTT):
                k = cnt[0] % 4
                cnt[0] += 1
                eng = "act" if cnt[0] % 2 else "dve"
                o_cp(c, eng, ost[k][0:nr, :], c.banks[tt][0:nr, 0:512], [("ps", tt)], [("ost", k)])
                s.op("sp", lambda e, o=out[col0:col0 + nr, tt * 512:(tt + 1) * 512], i_=ost[k][0:nr, :]:
                     e.dma_start(out=o, in_=i_), reads=[("ost", k)], dma=True)
            s.cur_tag = ("wu", g)
            off += nr
        s.cur_tag = None
    s.replay(hoist_wloads(s.end_rec()))
    return c.finish()


F32R = mybir.dt.float32r
MMK = set()


def _ro(out, w):
    for k in w:
        if k in MMK:
            return out.bitcast(F32R)
    return out


def o_tt(c, eng, out, a, b, op, r, w):
    out = _ro(out, w)
    c.s.op(eng, lambda e: e.tensor_tensor(out, a, b, op), reads=r, writes=w)


def o_ts(c, eng, out, a, s1, s2, op0, op1, r, w):
    out = _ro(out, w)
    if s2 is None:
        c.s.op(eng, lambda e: e.tensor_scalar(out, a, s1, None, op0), reads=r, writes=w)
    else:
        c.s.op(eng, lambda e: e.tensor_scalar(out, a, s1, s2, op0, op1), reads=r, writes=w)


def o_stt(c, eng, out, a, sc, b, op0, op1, r, w):
    eng = "dve"
    out = _ro(out, w)
    c.s.op(eng, lambda e: e.scalar_tensor_tensor(out, a, sc, b, op0, op1), reads=r, writes=w)


def o_act(c, out, a, func, r, w, bias=None, scale=None):
    out = _ro(out, w)
    kw = {}
    if bias is not None:
        kw["bias"] = bias
    if scale is not None:
        kw["scale"] = scale
    c.s.op("act", lambda e: e.activation(out, a, func, **kw), reads=r, writes=w)


def o_cp(c, eng, out, a, r, w):
    out = _ro(out, w)
    if eng == "act":
        c.s.op("act", lambda e: e.copy(out, a), reads=r, writes=w)
    else:
        c.s.op(eng, lambda e: e.tensor_copy(out, a), reads=r, writes=w)


def o_mm(c, out, l, rr, r, w, start=True, stop=True, f32r=False):
    if f32r:
        l, rr = l.bitcast(F32R), rr.bitcast(F32R)
    c.s.op("pe", lambda e: e.matmul(out, l, rr, start=start, stop=stop), reads=r, writes=w)


def o_tr(c, out, a, ident, r, w):
    c.s.op("pe", lambda e: e.transpose(out, a, ident), reads=r, writes=w)


def o_dma(c, out, a, r=(), w=(), q="sp"):
    c.s.op(q, lambda e: e.dma_start(out=out, in_=a), reads=r, writes=w, dma=True)


def o_scan(c, out, d0, d1, init, r, w):
    out = _ro(out, w)
    c.s.op("dve", lambda e: e.tensor_tensor_scan(out, d0, d1, init, OP.mult, OP.add), reads=r, writes=w)


def o_memset(c, eng, out, v, w):
    c.s.op(eng, lambda e: e.memset(out, v), writes=w)


TWO_PI = float(2 * np.pi)


def sin_cos(c, th_ap, r, temps, ki, kq):
    outs = []
    for idx, shift in ((0, 0.0), (1, float(np.pi / 2))):
        (a, ka), (kf, kk), (res, kr) = temps[3 * idx: 3 * idx + 3]
        o_ts(c, "dve", a, th_ap, shift, None, OP.add, None, r, [ka])
        o_ts(c, "dve", kf, a, 1.0 / TWO_PI, None, OP.mult, None, [ka], [kk])
        o_cp(c, "dve", ki, kf, [kk], [kq])
        o_cp(c, "dve", kf, ki, [kq], [kk])
        o_stt(c, "dve", a, kf, -TWO_PI, a, OP.mult, OP.add, [kk, ka], [ka])
        o_ts(c, "dve", a, a, float(np.pi), float(-np.pi), OP.min, OP.max, [ka], [ka])
        o_act(c, res, a, AF.Sin, [ka], [kr])
        outs.append((res, kr))
    return outs[0], outs[1]


TC = 512
NPC = 36
S5POOL = "pool"


def build_KB(S=SEQ, do=("lru", "s5", "dn"), dbg=False):
    c = Ctx()
    s = c.s
    MMK.clear()
    MMK.update(["s5_Bb", "s5_Cm2", "ones_r", "D_sq", "D_kc", "D_qc", "D_kb", "D_wT", "D_qd",
                "D64_U0", "D64_U1", "D64_L0", "D64_L1", "D64_X", "D64_AT",
                ("Dtm", "kbg"), ("Dtm", "vb"), ("Dtm", "kt"), ("dnS", 0), ("dnS", 1), ("D_vn", 0), ("D_vn", 1),
                ("S_uinr", 0), ("S_uinr", 1), ("S_h", 0, 0), ("S_h", 0, 1), ("S_h", 1, 0), ("S_h", 1, 1)])
    NSC = S // TC
    qkvz = c.din("qkvz", [4, 128, S])
    bd = c.din("bd", [2, S])
    s5u = c.din("s5u", [128, S])
    lrx = c.din("lrx", [128, S])
    lry = c.din("lry", [128, S])
    pcol_d = c.din("pcol", [128, NPC])
    lruw_d = c.din("lruw", [128, 2, 128])
    s5row_d = c.din("s5row", [128, 3, 512])
    s5b_d = c.din("s5b", [128, 2, 4, 128])
    s5c_d = c.din("s5c", [128, 2, 4, 128])
    ident_d = c.din("identd", [128, 128])
    cmask_d = c.din("cmask", [128, TC])
    c64_d = c.din("c64", [64, 3, 512])
    sel_d = c.din("sel", [2, 4])
    out = c.dout("o", [3, 128, S])

    pcol = c.sb("pcol", [128, NPC])
    o_dma(c, pcol[:], pcol_d, w=["pcol"])
    ident = c.sb("ident", [128, 128])
    o_dma(c, ident[:], ident_d, w=["ident"])
    ones = c.sb("ones", [128, 128])
    o_memset(c, "pool", ones[:], 1.0, ["ones"])
    ones_r = c.sb("ones_r", [128, 128])
    o_cp(c, "dve", ones_r[:], ones[:], ["ones"], ["ones_r"])

    def col(i):
        return pcol[:, i:i + 1]

    if "lru" in do:
        lruw = c.sb("lruw", [128, 2, 128])
        o_dma(c, lruw[:], lruw_d, w=["lruw"])
        c8 = c.sb("c8", [128, 1])
        o_act(c, c8[:], col(22), AF.Exp, ["pcol"], ["c8"], scale=-1.0)
        o_act(c, c8[:], c8[:], AF.Ln, ["c8"], ["c8"], bias=1.0)
        o_ts(c, "dve", c8[:], c8[:], -8.0, None, OP.mult, None, ["c8"], ["c8"])
        lru_h = c.sb("lru_h", [128, 1])
        o_memset(c, "dve", lru_h[:], 0.0, ["lru_h"])
        L_xin = [c.sb("L_xin%d" % i, [128, 3 + TC]) for i in range(2)]
        L_yin = [c.sb("L_yin%d" % i, [128, TC]) for i in range(2)]
        L_t = [c.sb("L_t%d" % i, [128, TC]) for i in range(6)]

    def lru_chunk(sc):
        t0 = sc * TC
        bi = sc % 2
        xin, yin = L_xin[bi], L_yin[bi]
        kx, ky = ("L_xin", bi), ("L_yin", bi)
        if sc == 0:
            o_memset(c, "pool", xin[:, 0:3], 0.0, [kx])
            o_dma(c, xin[:, 3:3 + TC], lrx[:, 0:TC], w=[kx])
        else:
            o_dma(c, xin[:, :], lrx[:, t0 - 3:t0 + TC], w=[kx])
        o_dma(c, yin[:, :], lry[:, t0:t0 + TC], w=[ky])
        xc, r_, i_, a_, m_, h_ = [t[:, :] for t in L_t]
        K = ["L_t%d" % i for i in range(6)]
        o_act(c, xc, xin[:, 0:TC], AF.Identity, [kx, "pcol"], [K[0]], bias=col(19), scale=col(15))
        for k in (1, 2, 3):
            o_stt(c, "pool", xc, xin[:, k:k + TC], col(15 + k), xc, OP.mult, OP.add, [kx, "pcol", K[0]], [K[0]])
        b1, b2 = c.bank(), c.bank()
        o_mm(c, c.banks[b1][:, 0:TC], lruw[:, 0, :], xc, ["lruw", K[0]], [("ps", b1)])
        o_mm(c, c.banks[b2][:, 0:TC], lruw[:, 1, :], xc, ["lruw", K[0]], [("ps", b2)])
        o_act(c, r_, c.banks[b1][:, 0:TC], AF.Sigmoid, [("ps", b1), "pcol"], [K[1]], bias=col(20))
        o_act(c, i_, c.banks[b2][:, 0:TC], AF.Sigmoid, [("ps", b2), "pcol"], [K[2]], bias=col(21))
        o_act(c, a_, r_, AF.Exp, [K[1], "c8"], [K[3]], scale=c8[:, 0:1])
        o_tt(c, "pool", m_, a_, a_, OP.mult, [K[3]], [K[4]])
        o_act(c, m_, m_, AF.Sqrt, [K[4]], [K[4]], bias=1.0, scale=-1.0)
        if sc == 0:
            o_memset(c, "pool", m_[:, 0:1], 1.0, [K[4]])
        o_tt(c, "pool", m_, m_, i_, OP.mult, [K[4], K[2]], [K[4]])
        o_tt(c, "pool", m_, m_, xc, OP.mult, [K[4], K[0]], [K[4]])
        o_scan(c, h_, a_, m_, lru_h[:, 0:1], [K[3], K[4], "lru_h"], [K[5]])
        o_cp(c, "dve", lru_h[:, 0:1], h_[:, TC - 1:TC], [K[5]], ["lru_h"])
        o_act(c, r_, yin[:, :], AF.Gelu_apprx_tanh, [ky], [K[1]])
        o_tt(c, "pool", r_, r_, h_, OP.mult, [K[1], K[5]], [K[1]])
        o_dma(c, out[2, :, t0:t0 + TC], r_, r=[K[1]])

    if "s5" in do:
        dl = c.sb("s5_dl", [128, 4])
        o_act(c, dl[:], pcol[:, 31:35], AF.Exp, ["pcol"], ["s5_dl"])
        rr = c.sb("s5_r", [128, 4])
        o_tt(c, "dve", rr[:], pcol[:, 23:27], dl[:], OP.mult, ["pcol", "s5_dl"], ["s5_r"])
        o_act(c, rr[:], rr[:], AF.Exp, ["s5_r"], ["s5_r"])
        th = c.sb("s5_th", [128, 4])
        o_tt(c, "dve", th[:], pcol[:, 27:31], dl[:], OP.mult, ["pcol", "s5_dl"], ["s5_th"])
        sct = [c.sb("s5ct%d" % i, [128, 4]) for i in range(6)]
        scki = c.sb("s5cki", [128, 4], mybir.dt.int32)
        (sn, ksn), (cs, kcs) = sin_cos(c, th[:], ["s5_th"], [(t_[:], "s5ct%d" % i) for i, t_ in enumerate(sct)], scki[:], "s5cki")
        Ct = c.sb("s5_Ct", [128, 4, TC])
        St = c.sb("s5_St", [128, 4, TC])
        o_memset(c, "dve", Ct[:, :, 0:1], 1.0, ["s5_Ct"])
        o_memset(c, "dve", St[:, :, 0:1], 0.0, ["s5_St"])
        cw = c.sb("s5_cw", [128, 4])
        sw = c.sb("s5_sw", [128, 4])
        nsw = c.sb("s5_nsw", [128, 4])
        tq = c.sb("s5_tq", [128, 4])
        o_cp(c, "dve", cw[:], cs, [kcs], ["s5_cw"])
        o_cp(c, "dve", sw[:], sn, [ksn], ["s5_sw"])
        S_w = [[c.sb("S_w%d_%d" % (p, i), [128, TC]) for i in range(8)] for p in range(2)]
        tmpT = S_w[1][7]
        w_ = 1
        while w_ < TC:
            o_ts(c, "dve", nsw[:], sw[:], -1.0, None, OP.mult, None, ["s5_sw"], ["s5_nsw"])
            for m in range(4):
                o_ts(c, "dve", tmpT[:, 0:w_], Ct[:, m, 0:w_], cw[:, m:m + 1], None, OP.mult, None, ["s5_Ct", "s5_cw"], [("S_w", 1, 7)])
                o_stt(c, "dve", Ct[:, m, w_:2 * w_], St[:, m, 0:w_], nsw[:, m:m + 1], tmpT[:, 0:w_], OP.mult, OP.add,
                      ["s5_St", "s5_nsw", ("S_w", 1, 7)], ["s5_Ct"])
                o_ts(c, "dve", tmpT[:, 0:w_], St[:, m, 0:w_], cw[:, m:m + 1], None, OP.mult, None, ["s5_St", "s5_cw"], [("S_w", 1, 7)])
                o_stt(c, "dve", St[:, m, w_:2 * w_], Ct[:, m, 0:w_], sw[:, m:m + 1], tmpT[:, 0:w_], OP.mult, OP.add,
                      ["s5_Ct", "s5_sw", ("S_w", 1, 7)], ["s5_St"])
            o_tt(c, "dve", tq[:], sw[:], sw[:], OP.mult, ["s5_sw"], ["s5_tq"])
            o_tt(c, "dve", sw[:], cw[:], sw[:], OP.mult, ["s5_cw", "s5_sw"], ["s5_sw"])
            o_ts(c, "dve", sw[:], sw[:], 2.0, None, OP.mult, None, ["s5_sw"], ["s5_sw"])
            o_tt(c, "dve", cw[:], cw[:], cw[:], OP.mult, ["s5_cw"], ["s5_cw"])
            o_tt(c, "dve", cw[:], cw[:], tq[:], OP.subtract, ["s5_cw", "s5_tq"], ["s5_cw"])
            w_ *= 2
        o_ts(c, "dve", nsw[:], sw[:], -1.0, None, OP.mult, None, ["s5_sw"], ["s5_nsw"])
        row = c.sb("s5row", [128, 3, 512])
        o_dma(c, row[:], s5row_d, w=["s5row"])
        def SW(p, i):
            return S_w[p][i][:, :], ("S_w", p, i)
        (dlr, kdlr), (er, ker), (thr, kthr), (nr, knr), (ni, kni), (den, kden), (t2, kt2), (fr, kfr) = [SW(0, i) for i in range(8)]
        (fi, kfi) = SW(1, 0)
        rki = c.sb("s5rki", [128, 512], mybir.dt.int32)
        o_act(c, dlr, row[:, 2, :], AF.Exp, ["s5row"], [kdlr])
        o_tt(c, "dve", er, row[:, 0, :], dlr, OP.mult, ["s5row", kdlr], [ker])
        o_act(c, er, er, AF.Exp, [ker], [ker])
        o_tt(c, "dve", thr, row[:, 1, :], dlr, OP.mult, ["s5row", kdlr], [kthr])
        (snr, ksnr), (csr, kcsr) = sin_cos(c, thr, [kthr], [SW(1, i) for i in range(1, 7)], rki[:], "s5rki")
        o_tt(c, "dve", nr, er, csr, OP.mult, [ker, kcsr], [knr])
        o_ts(c, "dve", nr, nr, -1.0, None, OP.add, None, [knr], [knr])
        o_tt(c, "dve", ni, er, snr, OP.mult, [ker, ksnr], [kni])
        o_tt(c, "dve", den, row[:, 0, :], row[:, 0, :], OP.mult, ["s5row"], [kden])
        o_tt(c, "dve", t2, row[:, 1, :], row[:, 1, :], OP.mult, ["s5row"], [kt2])
        o_tt(c, "dve", t2, den, t2, OP.add, [kden, kt2], [kt2])
        c.s.op("dve", lambda e: e.reciprocal(den, t2), reads=[kt2], writes=[kden])
        o_tt(c, "dve", fr, nr, row[:, 0, :], OP.mult, [knr, "s5row"], [kfr])
        o_tt(c, "dve", t2, ni, row[:, 1, :], OP.mult, [kni, "s5row"], [kt2])
        o_tt(c, "dve", fr, fr, t2, OP.add, [kfr, kt2], [kfr])
        o_tt(c, "dve", fr, fr, den, OP.mult, [kfr, kden], [kfr])
        o_tt(c, "dve", fi, ni, row[:, 0, :], OP.mult, [kni, "s5row"], [kfi])
        o_tt(c, "dve", t2, nr, row[:, 1, :], OP.mult, [knr, "s5row"], [kt2])
        o_tt(c, "dve", fi, fi, t2, OP.subtract, [kfi, kt2], [kfi])
        o_tt(c, "dve", fi, fi, den, OP.mult, [kfi, kden], [kfi])
        Bin = c.sb("s5_Bin", [128, 2, 512])
        o_dma(c, Bin[:], s5b_d.rearrange("p a m n -> p a (m n)"), w=["s5_Bin"])
        Bb = c.sb("s5_Bb", [128, 2, 512])
        o_tt(c, "dve", Bb[:, 0, :], fr, Bin[:, 0, :], OP.mult, [kfr, "s5_Bin"], ["s5_Bb"])
        o_tt(c, "dve", t2, fi, Bin[:, 1, :], OP.mult, [kfi, "s5_Bin"], [kt2])
        o_tt(c, "dve", Bb[:, 0, :], Bb[:, 0, :], t2, OP.subtract, ["s5_Bb", kt2], ["s5_Bb"])
        o_tt(c, "dve", Bb[:, 1, :], fr, Bin[:, 1, :], OP.mult, [kfr, "s5_Bin"], ["s5_Bb"])
        o_tt(c, "dve", t2, fi, Bin[:, 0, :], OP.mult, [kfi, "s5_Bin"], [kt2])
        o_tt(c, "dve", Bb[:, 1, :], Bb[:, 1, :], t2, OP.add, ["s5_Bb", kt2], ["s5_Bb"])
        Cm = row[:, 0:2, :]
        o_dma(c, Cm, s5c_d.rearrange("p a m n -> p a (m n)"), w=["s5row"])
        Cm2 = c.sb("s5_Cm2", [128, 2, 512])
        o_ts(c, "dve", Cm2[:, 0, :], Cm[:, 0, :], 1.0, None, OP.mult, None, ["s5row"], ["s5_Cm2"])
        o_ts(c, "dve", Cm2[:, 1, :], Cm[:, 1, :], -1.0, None, OP.mult, None, ["s5row"], ["s5_Cm2"])
        S_h = [[c.sb("S_h%d_%d" % (p, i), [128, TC]) for i in range(2)] for p in range(1)] * 2
        S_uinr = [c.sb("S_uinr%d" % i, [128, TC]) for i in range(1)] * 2
        s5_init = c.sb("s5_init", [128, 2, 4])
        o_memset(c, "dve", s5_init[:], 0.0, ["s5_init"])
        s5_tc = c.sb("s5_tc", [128, 4])
        if dbg:
            for nm, t_, shp, keys in (("Ct", Ct, [128, 4, TC], ["s5_Ct"]), ("St", St, [128, 4, TC], ["s5_St"]),
                                      ("Bb", Bb, [128, 2, 512], ["s5_Bb"]), ("rr", rr, [128, 4], ["s5_r"]),
                                      ("cw", cw, [128, 4], ["s5_cw"]), ("sw", sw, [128, 4], ["s5_sw"]),
                                      ("Cm", Cm2, [128, 2, 512], ["s5_Cm2"])):
                o_dma(c, c.dout("dbg_" + nm, shp), t_[:], r=keys)
        S_uin = [c.sb("S_uin%d" % i, [128, TC]) for i in range(2)]
        S_y = c.sb("S_y", [128, TC])

    def s5_chunk(sc):
        t0 = sc * TC
        bi = sc % 2
        uin = S_uin[bi]
        ku = ("S_uin", bi)
        o_dma(c, uin[:, :], s5u[:, t0:t0 + TC], w=[ku])
        uinr, kur = S_uinr[bi], ("S_uinr", 0)
        o_cp(c, "act", uinr[:, :], uin[:, :], [ku], [kur])
        by = 7
        for m in range(4):
            W = S_w[m % 2]
            KW = [("S_w", m % 2, i) for i in range(8)]
            b1, b2 = c.bank(), c.bank()
            pr, pi = c.banks[b1][:, 0:TC], c.banks[b2][:, 0:TC]
            o_mm(c, pr, Bb[:, 0, m * 128:(m + 1) * 128], uinr[:, :], ["s5_Bb", kur], [("ps", b1)], f32r=True)
            o_mm(c, pi, Bb[:, 1, m * 128:(m + 1) * 128], uinr[:, :], ["s5_Bb", kur], [("ps", b2)], f32r=True)
            Cj, Sj = Ct[:, m, :], St[:, m, :]
            t1, t2_, t3, t4, Xr, Xi = [t[:, :] for t in W[0:6]]
            hr, hi = S_h[m % 2][0][:, :], S_h[m % 2][1][:, :]
            KW[6], KW[7] = ("S_h", 0, 0), ("S_h", 0, 1)
            o_tt(c, "dve", t1, pr, Cj, OP.mult, [("ps", b1), "s5_Ct"], [KW[0]])
            o_tt(c, "dve", t2_, pi, Sj, OP.mult, [("ps", b2), "s5_St"], [KW[1]])
            o_tt(c, "dve", t3, pi, Cj, OP.mult, [("ps", b2), "s5_Ct"], [KW[2]])
            o_tt(c, "dve", t4, pr, Sj, OP.mult, [("ps", b1), "s5_St"], [KW[3]])
            o_tt(c, S5POOL, Xr, t1, t2_, OP.add, [KW[0], KW[1]], [KW[4]])
            o_tt(c, S5POOL, Xi, t3, t4, OP.subtract, [KW[2], KW[3]], [KW[5]])
            rb = rr[:, m:m + 1].to_broadcast([128, TC])
            o_scan(c, t1, rb, Xr, s5_init[:, 0, m:m + 1], ["s5_r", KW[4], "s5_init"], [KW[0]])
            o_scan(c, t3, rb, Xi, s5_init[:, 1, m:m + 1], ["s5_r", KW[5], "s5_init"], [KW[2]])
            o_ts(c, "dve", s5_tc[:, m:m + 1], t1[:, TC - 1:TC], cw[:, m:m + 1], None, OP.mult, None, [KW[0], "s5_cw"], ["s5_tc"])
            o_stt(c, "dve", s5_init[:, 0, m:m + 1], t3[:, TC - 1:TC], nsw[:, m:m + 1], s5_tc[:, m:m + 1], OP.mult, OP.add,
                  [KW[2], "s5_nsw", "s5_tc"], ["s5_init"])
            o_ts(c, "dve", s5_tc[:, m:m + 1], t3[:, TC - 1:TC], cw[:, m:m + 1], None, OP.mult, None, [KW[2], "s5_cw"], ["s5_tc"])
            o_stt(c, "dve", s5_init[:, 1, m:m + 1], t1[:, TC - 1:TC], sw[:, m:m + 1], s5_tc[:, m:m + 1], OP.mult, OP.add,
                  [KW[0], "s5_sw", "s5_tc"], ["s5_init"])
            o_tt(c, S5POOL, t2_, t1, Cj, OP.mult, [KW[0], "s5_Ct"], [KW[1]])
            o_tt(c, S5POOL, t4, t3, Sj, OP.mult, [KW[2], "s5_St"], [KW[3]])
            o_tt(c, "dve", hr, t2_, t4, OP.subtract, [KW[1], KW[3]], [KW[6]])
            o_tt(c, S5POOL, Xr, t1, Sj, OP.mult, [KW[0], "s5_St"], [KW[4]])
            o_tt(c, S5POOL, Xi, t3, Cj, OP.mult, [KW[2], "s5_Ct"], [KW[5]])
            o_tt(c, "dve", hi, Xr, Xi, OP.add, [KW[4], KW[5]], [KW[7]])
            o_mm(c, c.banks[by][:, 0:TC], Cm2[:, 0, m * 128:(m + 1) * 128], hr, ["s5_Cm2", KW[6]], [("ps", by)],
                 start=(m == 0), stop=False, f32r=True)
            o_mm(c, c.banks[by][:, 0:TC], Cm2[:, 1, m * 128:(m + 1) * 128], hi, ["s5_Cm2", KW[7]], [("ps", by)],
                 start=False, stop=(m == 3), f32r=True)
        o_stt(c, "dve", S_y[:, :], uin[:, :], col(35), c.banks[by][:, 0:TC], OP.mult, OP.add, [ku, "pcol", ("ps", by)], ["S_y"])
        o_act(c, S_y[:, :], S_y[:, :], AF.Gelu_apprx_tanh, ["S_y"], ["S_y"])
        o_dma(c, out[1, :, t0:t0 + TC], S_y[:, :], r=["S_y"])

    dn_chunk = build_dn(c, do, S, qkvz, bd, out, pcol, ident, ones, cmask_d, c64_d, sel_d, ones_r)
    LS, DP, DC = [], [], []
    for sc in range(NSC):
        s.begin_rec()
        if "lru" in do:
            lru_chunk(sc)
        if "s5" in do:
            s5_chunk(sc)
        LS.append(s.end_rec())
        s.begin_rec()
        if "dn" in do:
            dn_chunk(sc)
        d_ = s.end_rec()
        DP.append([o for o in d_ if o[5] != "chain"])
        DC.append([o for o in d_ if o[5] == "chain"])
    s.replay(LS[0])
    s.replay(DP[0])
    for sc in range(NSC):
        nxt = LS[sc + 1] if sc + 1 < NSC else []
        s.replay(merge_streams(DC[sc], nxt))
        if sc + 1 < NSC:
            s.replay(DP[sc + 1])
    return c.finish()


def build_dn(c, do, S, qkvz, bd, out, pcol, ident, ones, cmask_d, c64_d, sel_d, ones_r):
    if "dn" not in do:
        return None
    NCH = TC // 64

    def col(i):
        return pcol[:, i:i + 1]

    cmask = c.sb("cmask", [128, TC])
    o_dma(c, cmask[:], cmask_d, w=["cmask"])
    c64 = c.sb("c64", [64, 3, 512])
    o_dma(c, c64[:], c64_d, w=["c64"])
    sel = c.sb("sel", [2, 4])
    o_dma(c, sel[:], sel_d, w=["sel"])
    negmask, strict, ident8 = c64[:, 0, :], c64[:, 1, :], c64[:, 2, :]
    negA = c.sb("negA", [128, 1])
    o_act(c, negA[:], col(12), AF.Exp, ["pcol"], ["negA"])
    o_ts(c, "dve", negA[:], negA[:], -1.0, None, OP.mult, None, ["negA"], ["negA"])
    Sst = [c.sb("dnS%d" % i, [128, 128]) for i in range(2)]
    o_memset(c, "dve", Sst[1][:], 0.0, [("dnS", 1)])
    o_cp(c, "dve", Sst[0][:], Sst[1][:], [("dnS", 1)], [("dnS", 0)])
    Xin = [[c.sb("D_in%d_%d" % (w, i), [128, 3 + TC]) for i in range(2)] for w in range(3)]
    Zin = [c.sb("D_z%d" % i, [128, TC]) for i in range(2)]
    Brow = [c.sb("D_br%d" % i, [1, TC]) for i in range(2)]
    Drow = [c.sb("D_dr%d" % i, [1, TC]) for i in range(2)]
    names = ["qc", "kc", "vc", "sq", "rs", "beta", "gc", "eg", "egl", "kb", "kbg", "vb", "kt", "qd", "oT", "wT", "zs", "tmp"]
    T = {n: c.sb("D_" + n, [128, TC]) for n in names}
    T64 = {n: c.sb("D64_" + n, [64, 512]) for n in ["dm", "AT", "U0", "U1", "L0", "L1", "X"]}
    TM = {n: c.sb("Dtm_" + n, [64, NCH, 128]) for n in ["kbg", "vb", "kt", "u"]}
    Vn = [c.sb("D_vn%d" % i, [64, 128]) for i in range(2)]
    G1 = c.sb("D_G1", [2, TC])
    G2 = c.sb("D_G2", [2, TC])

    def A(n):
        return T[n][:, :]

    def K(n):
        return "D_" + n

    def A64(n):
        return T64[n][:, :]

    def K64(n):
        return "D64_" + n

    def dn_chunk(sc):
        t0 = sc * TC
        bi = sc % 2
        for w in range(3):
            xin, kx = Xin[w][bi], ("D_in", w, bi)
            if sc == 0:
                o_memset(c, "pool", xin[:, 0:3], 0.0, [kx])
                o_dma(c, xin[:, 3:3 + TC], qkvz[w, :, 0:TC], w=[kx])
            else:
                o_dma(c, xin[:, :], qkvz[w, :, t0 - 3:t0 + TC], w=[kx])
        zin, kz = Zin[bi], ("D_z", bi)
        o_dma(c, zin[:, :], qkvz[3, :, t0:t0 + TC], w=[kz])
        brow, drow = Brow[bi], Drow[bi]
        kbr, kdr = ("D_br", bi), ("D_dr", bi)
        o_dma(c, brow[:, :], bd[0:1, t0:t0 + TC], w=[kbr])
        o_dma(c, drow[:, :], bd[1:2, t0:t0 + TC], w=[kdr])
        for w, nm in enumerate(["qc", "kc", "vc"]):
            xin, kx = Xin[w][bi], ("D_in", w, bi)
            o_act(c, A(nm), xin[:, 0:TC], AF.Identity, [kx, "pcol"], [K(nm)], scale=col(w * 4 + 0))
            for k in (1, 2, 3):
                o_stt(c, "dve", A(nm), xin[:, k:k + TC], col(w * 4 + k), A(nm), OP.mult, OP.add, [kx, "pcol", K(nm)], [K(nm)])
            o_act(c, A(nm), A(nm), AF.Silu, [K(nm)], [K(nm)])
        for nm, scale in (("qc", 128.0 ** -0.5), ("kc", 1.0)):
            o_act(c, A("sq"), A(nm), AF.Square, [K(nm)], [K("sq")])
            b = c.bank()
            o_mm(c, c.banks[b][:, 0:TC], ones_r[:, :], A("sq"), ["ones_r", K("sq")], [("ps", b)], f32r=True)
            o_act(c, A("rs"), c.banks[b][:, 0:TC], AF.Sqrt, [("ps", b)], [K("rs")], bias=1e-6)
            c.s.op("dve", lambda e: e.reciprocal(A("rs"), A("rs")), reads=[K("rs")], writes=[K("rs")])
            o_stt(c, "dve", A(nm), A(nm), scale, A("rs"), OP.mult, OP.mult, [K(nm), K("rs")], [K(nm)])
        b = c.bank()
        o_mm(c, c.banks[b][:, 0:TC], ones[0:1, :], brow[:, :], ["ones", kbr], [("ps", b)])
        o_act(c, A("beta"), c.banks[b][:, 0:TC], AF.Sigmoid, [("ps", b)], [K("beta")])
        b = c.bank()
        o_mm(c, c.banks[b][:, 0:TC], ones[0:1, :], drow[:, :], ["ones", kdr], [("ps", b)])
        o_act(c, A("tmp"), c.banks[b][:, 0:TC], AF.Exp, [("ps", b), "pcol"], [K("tmp")], bias=col(13))
        o_act(c, A("tmp"), A("tmp"), AF.Ln, [K("tmp")], [K("tmp")], bias=1.0)
        o_ts(c, "dve", A("tmp"), A("tmp"), negA[:, 0:1], None, OP.mult, None, [K("tmp"), "negA"], [K("tmp")])
        o_scan(c, A("gc"), cmask[:, :], A("tmp"), 0.0, ["cmask", K("tmp")], [K("gc")])
        o_act(c, A("eg"), A("gc"), AF.Exp, [K("gc")], [K("eg")])
        gc3 = A("gc").rearrange("p (n c) -> p n c", c=64)
        o_tt(c, "dve", A("egl").rearrange("p (n c) -> p n c", c=64), gc3[:, :, 63:64].to_broadcast([128, NCH, 64]), gc3,
             OP.subtract, [K("gc")], [K("egl")])
        o_act(c, A("egl"), A("egl"), AF.Exp, [K("egl")], [K("egl")])
        o_tt(c, "dve", A("kb"), A("kc"), A("beta"), OP.mult, [K("kc"), K("beta")], [K("kb")])
        o_tt(c, "pool", A("kbg"), A("kb"), A("eg"), OP.mult, [K("kb"), K("eg")], [K("kbg")])
        o_tt(c, "pool", A("vb"), A("vc"), A("beta"), OP.mult, [K("vc"), K("beta")], [K("vb")])
        o_tt(c, "pool", A("kt"), A("kc"), A("egl"), OP.mult, [K("kc"), K("egl")], [K("kt")])
        o_tt(c, "pool", A("qd"), A("qc"), A("eg"), OP.mult, [K("qc"), K("eg")], [K("qd")])
        o_ts(c, "dve", G1[:, :], A("gc")[0:2, :], sel[:, 0:1], sel[:, 1:2], OP.mult, OP.add, [K("gc"), "sel"], ["D_G1"])
        o_ts(c, "dve", G2[:, :], A("gc")[0:2, :], sel[:, 2:3], sel[:, 3:4], OP.mult, OP.add, [K("gc"), "sel"], ["D_G2"])
        b = c.bank()
        for n in range(NCH):
            cs_ = slice(n * 64, (n + 1) * 64)
            o_mm(c, c.banks[b][0:64, cs_], G1[:, cs_], G2[:, cs_], ["D_G1", "D_G2"], [("ps", b)])
        o_stt(c, "dve", A64("dm"), c.banks[b][0:64, 0:512], 0.0, negmask, OP.min, OP.add, [("ps", b), "c64"], [K64("dm")])
        o_act(c, A64("dm"), A64("dm"), AF.Exp, [K64("dm")], [K64("dm")])
        b = c.bank()
        for n in range(NCH):
            cs_ = slice(n * 64, (n + 1) * 64)
            o_mm(c, c.banks[b][0:64, cs_], A("kc")[:, cs_], A("qc")[:, cs_], [K("kc"), K("qc")], [("ps", b)], f32r=True)
        o_tt(c, "dve", A64("AT"), c.banks[b][0:64, 0:512], A64("dm"), OP.mult, [("ps", b), K64("dm")], [K64("AT")])
        b = c.bank()
        for n in range(NCH):
            cs_ = slice(n * 64, (n + 1) * 64)
            o_mm(c, c.banks[b][0:64, cs_], A("kc")[:, cs_], A("kb")[:, cs_], [K("kc"), K("kb")], [("ps", b)], f32r=True)
        o_tt(c, "dve", A64("U0"), c.banks[b][0:64, 0:512], A64("dm"), OP.mult, [("ps", b), K64("dm")], [K64("U0")])
        o_tt(c, "pool", A64("U0"), A64("U0"), strict, OP.mult, [K64("U0"), "c64"], [K64("U0")])
        b = c.bank()
        for n in range(NCH):
            cs_ = slice(n * 64, (n + 1) * 64)
            o_tr(c, c.banks[b][0:64, cs_], A64("U0")[:, cs_], ident[0:64, 0:64], [K64("U0"), "ident"], [("ps", b)])
        o_cp(c, "act", A64("L0"), c.banks[b][0:64, 0:512], [("ps", b)], [K64("L0")])
        o_tt(c, "dve", A64("X"), ident8, A64("U0"), OP.subtract, ["c64", K64("U0")], [K64("X")])
        cu, cl = "U0", "L0"
        for lvl in range(5):
            nu, nl = ("U1", "L1") if cu == "U0" else ("U0", "L0")
            b = c.bank()
            for n in range(NCH):
                cs_ = slice(n * 64, (n + 1) * 64)
                o_mm(c, c.banks[b][0:64, cs_], A64(cl)[:, cs_], A64(cu)[:, cs_], [K64(cl), K64(cu)], [("ps", b)], f32r=True)
            o_cp(c, "act", A64(nu), c.banks[b][0:64, 0:512], [("ps", b)], [K64(nu)])
            b = c.bank()
            for n in range(NCH):
                cs_ = slice(n * 64, (n + 1) * 64)
                o_mm(c, c.banks[b][0:64, cs_], A64(cu)[:, cs_], A64(cl)[:, cs_], [K64(cl), K64(cu)], [("ps", b)], f32r=True)
            o_cp(c, "dve", A64(nl), c.banks[b][0:64, 0:512], [("ps", b)], [K64(nl)])
            b = c.bank()
            for n in range(NCH):
                cs_ = slice(n * 64, (n + 1) * 64)
                o_mm(c, c.banks[b][0:64, cs_], A64(nl)[:, cs_], A64("X")[:, cs_], [K64(nl), K64("X")], [("ps", b)], f32r=True)
            o_tt(c, "dve", A64("X"), A64("X"), c.banks[b][0:64, 0:512], OP.add, [K64("X"), ("ps", b)], [K64("X")])
            cu, cl = nu, nl
        ei = 0
        for nm in ("kbg", "vb", "kt"):
            for half in range(2):
                b = c.bank()
                for q in range(4):
                    n = half * 4 + q
                    o_tr(c, c.banks[b][0:64, q * 128:(q + 1) * 128], A(nm)[:, n * 64:(n + 1) * 64], ident[:, :],
                         [K(nm), "ident"], [("ps", b)])
                o_cp(c, "act" if ei % 2 == 0 else "dve", TM[nm][:, half * 4:(half + 1) * 4, :].rearrange("p a b -> p (a b)"),
                     c.banks[b][0:64, 0:512], [("ps", b)], [("Dtm", nm)])
                ei += 1
        for half in range(2):
            b = c.bank()
            for q in range(4):
                n = half * 4 + q
                o_mm(c, c.banks[b][0:64, q * 128:(q + 1) * 128], A64("X")[:, n * 64:(n + 1) * 64], TM["vb"][:, n, :],
                     [K64("X"), ("Dtm", "vb")], [("ps", b)], f32r=True)
            o_cp(c, "act", TM["u"][:, half * 4:(half + 1) * 4, :].rearrange("p a b -> p (a b)"), c.banks[b][0:64, 0:512],
                 [("ps", b)], [("Dtm", "u")])
        b = c.bank()
        for n in range(NCH):
            cs_ = slice(n * 64, (n + 1) * 64)
            o_mm(c, c.banks[b][:, cs_], TM["kbg"][:, n, :], A64("X")[:, cs_], [("Dtm", "kbg"), K64("X")], [("ps", b)], f32r=True)
        o_cp(c, "act", A("wT"), c.banks[b][:, 0:512], [("ps", b)], [K("wT")])
        c.s.cur_tag = "chain"
        bo = 6
        for n in range(NCH):
            gi = sc * NCH + n
            cur, nxt = Sst[gi % 2], Sst[(gi + 1) % 2]
            kcur, knxt = ("dnS", gi % 2), ("dnS", (gi + 1) % 2)
            vn, kvn = Vn[gi % 2], ("D_vn", gi % 2)
            cs_ = slice(n * 64, (n + 1) * 64)
            b1 = c.bank(4, 6)
            o_mm(c, c.banks[b1][0:64, 0:128], A("wT")[:, cs_], cur[:, :], [K("wT"), kcur], [("ps", b1)], f32r=True)
            o_tt(c, "dve", vn[:, :], TM["u"][:, n, :], c.banks[b1][0:64, 0:128], OP.subtract, [("Dtm", "u"), ("ps", b1)], [kvn])
            o_mm(c, c.banks[bo][:, cs_], cur[:, :], A("qd")[:, cs_], [kcur, K("qd")], [("ps", bo)], start=True, stop=False, f32r=True)
            o_mm(c, c.banks[bo][:, cs_], vn[:, :], A64("AT")[:, cs_], [kvn, K64("AT")], [("ps", bo)], start=False, stop=True, f32r=True)
            b2 = c.bank(4, 6)
            o_mm(c, c.banks[b2][:, 0:128], TM["kt"][:, n, :], vn[:, :], [("Dtm", "kt"), kvn], [("ps", b2)], f32r=True)
            o_stt(c, "dve", nxt[:, :], cur[:, :], A("eg")[:, n * 64 + 63:n * 64 + 64], c.banks[b2][:, 0:128], OP.mult, OP.add,
                  [kcur, K("eg"), ("ps", b2)], [knxt])
        o_cp(c, "act", A("oT"), c.banks[bo][:, 0:512], [("ps", bo)], [K("oT")])
        o_act(c, A("sq"), A("oT"), AF.Square, [K("oT")], [K("sq")])
        b = c.bank(4, 6)
        o_mm(c, c.banks[b][:, 0:TC], ones_r[:, :], A("sq"), ["ones_r", K("sq")], [("ps", b)], f32r=True)
        o_act(c, A("rs"), c.banks[b][:, 0:TC], AF.Sqrt, [("ps", b)], [K("rs")], bias=1e-6, scale=1.0 / 128.0)
        c.s.op("dve", lambda e: e.reciprocal(A("rs"), A("rs")), reads=[K("rs")], writes=[K("rs")])
        o_tt(c, "dve", A("oT"), A("oT"), A("rs"), OP.mult, [K("oT"), K("rs")], [K("oT")])
        o_act(c, A("zs"), zin[:, :], AF.Silu, [kz], [K("zs")])
        o_stt(c, "dve", A("oT"), A("oT"), col(14), A("zs"), OP.mult, OP.mult, [K("oT"), "pcol", K("zs")], [K("oT")])
        o_dma(c, out[0, :, t0:t0 + TC], A("oT"), r=[K("oT")])
        c.s.cur_tag = None

    return dn_chunk


HW = 16
NPC2 = 4 + 24 + 8 + 8 + 144 + 48 + 8 + 8
PC_BGLU, PC_BGATE, PC_L1G, PC_L1B, PC_CW, PC_CB, PC_L2G, PC_L2B = 0, 4, 28, 36, 44, 188, 236, 244


def build_KC(TOKC=4096):
    c = Ctx()
    s = c.s
    NT = TOKC // 512
    xT = c.din("xT", [D, HW + TOKC])
    brT = c.din("brT", [1536, HW + TOKC])
    flag_d = c.din("flag", [128, 1])
    w_gate = c.din("w_gate", [D, 3072])
    w_glu = c.din("w_glu", [512, 512])
    w_br = c.din("w_br", [1536, D])
    w_out = c.din("w_out", [D, D])
    w_up = c.din("w_up", [D, 6144])
    w_down = c.din("w_down", [3072, D])
    pc_d = c.din("pc", [128, NPC2])
    out = c.dout("x2T", [D, TOKC])

    pc = c.sb("pc", [128, NPC2])
    o_dma(c, pc[:], pc_d, w=["pc"])
    flag = c.sb("flag", [128, 1])
    o_dma(c, flag[:], flag_d, w=["flag"])
    onesm = c.sb("onesm", [128, 128])
    o_memset(c, "pool", onesm[:], 1.0 / D, ["onesm"])
    wb = make_wbufs(c)
    x_f = c.sb("x_f", [128, 8, 512])
    x_b = c.sb("x_b", [128, 8, 512], BF16)
    brst = c.sb("brst", [128, 4, 512])
    y_f = c.sb("y_f", [128, 4, 512])
    br_b = c.sb("br_b", [128, 12, 512], BF16)
    s5o = c.sb("s5o", [128, 4, 512], BF16)
    GA = c.sb("GA", [128, 24, 512], BF16)
    acc = c.sb("acc", [128, 8, 512])
    mx_b = c.sb("mx_b", [128, 8, 512], BF16)
    sq = [c.sb("sq%d" % i, [128, 512]) for i in range(2)]
    mean = c.sb("mean", [128, 512])
    var = c.sb("var", [128, 512])
    xc = [c.sb("xc%d" % i, [128, 512]) for i in range(2)]
    tmpf = [c.sb("tmpf%d" % i, [128, 512]) for i in range(2)]
    hst = [c.sb("hst%d" % i, [128, 2 + 512]) for i in range(2)]
    cv = [c.sb("cv%d" % i, [128, 512]) for i in range(2)]
    carry = c.sb("carry", [128, 48, 2])
    o_memset(c, "pool", carry[:], 0.0, ["carry"])
    xv = xT.rearrange("(kt p) t -> p kt t", p=128)
    bv = brT.rearrange("(kt p) t -> p kt t", p=128)
    ov = out.rearrange("(kt p) t -> p kt t", p=128)
    cnt = {"e": 0}

    def pcol(i):
        return pc[:, i:i + 1]

    def layer_norm(n, gi, bi_, make_bf):
        bA, bB = 6, 7
        for i in range(8):
            o_mm(c, c.banks[bA][:, 0:n], onesm[:, :], x_f[:, i, 0:n], ["onesm", ("x_f", i)], [("ps", bA)], start=(i == 0), stop=(i == 7))
        for i in range(8):
            q = sq[i % 2]
            o_act(c, q[:, 0:n], x_f[:, i, 0:n], AF.Square, [("x_f", i)], [("sq", i % 2)])
            o_mm(c, c.banks[bB][:, 0:n], onesm[:, :], q[:, 0:n], ["onesm", ("sq", i % 2)], [("ps", bB)], start=(i == 0), stop=(i == 7))
        o_cp(c, "act", mean[:, 0:n], c.banks[bA][:, 0:n], [("ps", bA)], ["mean"])
        o_tt(c, "dve", var[:, 0:n], mean[:, 0:n], mean[:, 0:n], OP.mult, ["mean"], ["var"])
        o_tt(c, "dve", var[:, 0:n], c.banks[bB][:, 0:n], var[:, 0:n], OP.subtract, [("ps", bB), "var"], ["var"])
        o_act(c, var[:, 0:n], var[:, 0:n], AF.Sqrt, ["var"], ["var"], bias=1e-5)
        c.s.op("dve", lambda e: e.reciprocal(var[:, 0:n], var[:, 0:n]), reads=["var"], writes=["var"])
        for i in range(8):
            t_ = xc[i % 2]
            o_tt(c, "dve", t_[:, 0:n], x_f[:, i, 0:n], mean[:, 0:n], OP.subtract, [("x_f", i), "mean"], [("xc", i % 2)])
            o_tt(c, "pool", t_[:, 0:n], t_[:, 0:n], var[:, 0:n], OP.mult, [("xc", i % 2), "var"], [("xc", i % 2)])
            o_act(c, x_f[:, i, 0:n], t_[:, 0:n], AF.Identity, [("xc", i % 2), "pc"], [("x_f", i)], bias=pcol(bi_ + i), scale=pcol(gi + i))
            if make_bf:
                o_cp(c, "act", x_b[:, i, 0:n], x_f[:, i, 0:n], [("x_f", i)], [("x_b", i)])

    def tile(ti):
        halo = ti < 0
        n = HW if halo else 512
        tok0 = 0 if halo else HW + ti * 512
        o_dma(c, x_f[:, :, 0:n], xv[:, :, tok0:tok0 + n], w=[("x_f", i) for i in range(8)])
        for i in range(8):
            o_cp(c, "act" if i % 2 else "dve", x_b[:, i, 0:n], x_f[:, i, 0:n], [("x_f", i)], [("x_b", i)])
        for g in range(3):
            dst = y_f if g == 1 else brst
            kd = "y_f" if g == 1 else "brst"
            o_dma(c, dst[:, :, 0:n], bv[:, g * 4:(g + 1) * 4, tok0:tok0 + n], w=[kd])
            for i in range(4):
                o_cp(c, "dve" if i % 2 else "act", br_b[:, g * 4 + i, 0:n], dst[:, i, 0:n], [kd], [("br_b", g * 4 + i)])
        def epi_glu(i, ps, pk):
            t_ = tmpf[i % 2]
            o_act(c, t_[:, 0:n], ps, AF.Sigmoid, [pk, "pc"], [("tmpf", i % 2)], bias=pcol(PC_BGLU + i))
            o_tt(c, "dve", s5o[:, i, 0:n], y_f[:, i, 0:n], t_[:, 0:n], OP.mult, ["y_f", ("tmpf", i % 2)], [("s5o", i)])
        stream_linear(c, w_glu, 512, [(i * 128, 128) for i in range(4)], lambda kt: (br_b[:, 4 + kt, 0:n], ("br_b", 4 + kt)), n, epi_glu, wb)
        def epi_gate(i, ps, pk):
            o_act(c, GA[:, i, 0:n], ps, AF.Sigmoid, [pk, "pc"], [("GA", i)], bias=pcol(PC_BGATE + i))
        stream_linear(c, w_gate, D, [(i * 128, 128) for i in range(24)], lambda kt: (x_b[:, kt, 0:n], ("x_b", kt)), n, epi_gate, wb)
        for g in range(3):
            def rhs(kt, g=g):
                if g == 1:
                    return s5o[:, kt, 0:n], ("s5o", kt)
                return br_b[:, g * 4 + kt, 0:n], ("br_b", g * 4 + kt)

            def epi_br(i, ps, pk, g=g):
                if g == 0:
                    o_tt(c, "dve", acc[:, i, 0:n], ps, GA[:, i, 0:n], OP.mult, [pk, ("GA", i)], [("acc", i)])
                else:
                    t_ = tmpf[i % 2]
                    o_tt(c, "dve", t_[:, 0:n], ps, GA[:, g * 8 + i, 0:n], OP.mult, [pk, ("GA", g * 8 + i)], [("tmpf", i % 2)])
                    if g == 1:
                        o_tt(c, "pool", acc[:, i, 0:n], acc[:, i, 0:n], t_[:, 0:n], OP.add, [("acc", i), ("tmpf", i % 2)], [("acc", i)])
                    else:
                        o_tt(c, "pool", mx_b[:, i, 0:n], acc[:, i, 0:n], t_[:, 0:n], OP.add, [("acc", i), ("tmpf", i % 2)], [("mx_b", i)])
            stream_linear(c, w_br[g * 512:(g + 1) * 512, :], 512, [(i * 128, 128) for i in range(8)], rhs, n, epi_br, wb)
        def epi_out(i, ps, pk):
            o_stt(c, "dve", x_f[:, i, 0:n], x_f[:, i, 0:n], ALPHA, ps, OP.mult, OP.add, [("x_f", i), pk], [("x_f", i)])
        stream_linear(c, w_out, D, [(i * 128, 128) for i in range(8)], lambda kt: (mx_b[:, kt, 0:n], ("mx_b", kt)), n, epi_out, wb)
        layer_norm(n, PC_L1G, PC_L1B, True)
        order = []
        for b_ in range(6):
            order += [4 * b_ + q for q in range(4)] + [24 + 4 * b_ + q for q in range(4)]
        cols = [(t_ * 128, 128) for t_ in order]

        def epi_up(idx, ps, pk):
            t_ = order[idx]
            h = hst[idx % 2]
            kh = ("hst", idx % 2)
            cvt = cv[idx % 2]
            kc = ("cv", idx % 2)
            o_cp(c, "pool", h[:, 0:2], carry[:, t_, :], ["carry"], [kh])
            o_cp(c, "act", h[:, 2:2 + n], ps, [pk], [kh])
            o_ts(c, "dve", cvt[:, 0:n], h[:, 0:n], pcol(PC_CW + t_), pcol(PC_CB + t_), OP.mult, OP.add, [kh, "pc"], [kc])
            o_stt(c, "dve", cvt[:, 0:n], h[:, 1:1 + n], pcol(PC_CW + 48 + t_), cvt[:, 0:n], OP.mult, OP.add, [kh, "pc", kc], [kc])
            o_stt(c, "dve", cvt[:, 0:n], h[:, 2:2 + n], pcol(PC_CW + 96 + t_), cvt[:, 0:n], OP.mult, OP.add, [kh, "pc", kc], [kc])
            if halo:
                o_ts(c, "pool", carry[:, t_, :], h[:, n:n + 2], flag[:, 0:1], None, OP.mult, None, [kh, "flag"], ["carry"])
            else:
                o_cp(c, "pool", carry[:, t_, :], h[:, n:n + 2], [kh], ["carry"])
                if t_ < 24:
                    o_act(c, GA[:, t_, 0:n], cvt[:, 0:n], AF.Gelu_apprx_tanh, [kc], [("GA", t_)])
                else:
                    o_tt(c, "pool", GA[:, t_ - 24, 0:n], GA[:, t_ - 24, 0:n], cvt[:, 0:n], OP.mult, [("GA", t_ - 24), kc], [("GA", t_ - 24)])
        stream_linear(c, w_up, D, cols, lambda kt: (x_b[:, kt, 0:n], ("x_b", kt)), n, epi_up, wb)
        if halo:
            return
        stream_linear(c, w_down, 3072, [(i * 128, 128) for i in range(8)], lambda kt: (GA[:, kt, 0:n], ("GA", kt)), n, epi_out, wb)
        layer_norm(n, PC_L2G, PC_L2B, False)
        o_dma(c, ov[:, :, ti * 512:(ti + 1) * 512], x_f[:, :, 0:n], r=[("x_f", i) for i in range(8)])

    s.begin_rec()
    tile(-1)
    for ti in range(NT):
        tile(ti)
    s.replay(hoist_wloads(s.end_rec()))
    return c.finish()


def kc_params(inp, l):
    def cols(v):
        return np.ascontiguousarray(v.reshape(-1, 128).T)
    pc = np.concatenate([cols(inp["s5_b_glu"][l]), cols(inp["b_gate"][l].reshape(-1)), cols(inp["ln1_g"][l]), cols(inp["ln1_b"][l]),
                         cols(inp["ffn_conv_w"][l][0]), cols(inp["ffn_conv_w"][l][1]), cols(inp["ffn_conv_w"][l][2]),
                         cols(inp["ffn_conv_b"][l]), cols(inp["ln2_g"][l]), cols(inp["ln2_b"][l])], axis=1).astype(np.float32)
    return {"pc": np.ascontiguousarray(pc), "w_gate": np.ascontiguousarray(inp["w_in"][l][:, NMIX:]),
            "w_glu": inp["s5_w_glu"][l], "w_br": np.ascontiguousarray(inp["w_branch"][l].reshape(1536, D)),
            "w_out": inp["w_out"][l], "w_up": inp["ffn_w_up"][l], "w_down": inp["ffn_w_down"][l]}


def kb_consts():
    cm = np.ones((128, TC), np.float32)
    cm[:, ::64] = 0.0
    jj = np.arange(64)[:, None]
    cc = np.arange(64)[None, :]
    negmask = np.where(jj <= cc, 0.0, -30000.0).astype(np.float32)
    strict = (jj < cc).astype(np.float32)
    eye = np.eye(64, dtype=np.float32)
    c64 = np.stack([np.tile(negmask, (1, 8)), np.tile(strict, (1, 8)), np.tile(eye, (1, 8))], axis=1)
    sel = np.array([[0.0, 1.0, 1.0, 0.0], [-1.0, 0.0, 0.0, 1.0]], np.float32)
    return {"identd": np.eye(128, dtype=np.float32), "cmask": cm, "c64": np.ascontiguousarray(c64), "sel": sel}


def kb_params(inp, l, j):
    pcol = np.zeros((128, NPC), np.float32)
    sl = slice(j * 128, (j + 1) * 128)
    for which in range(3):
        for tap in range(4):
            pcol[:, which * 4 + tap] = inp["dn_conv_w"][l][tap, which * 512 + j * 128: which * 512 + (j + 1) * 128]
    pcol[:, 12] = inp["dn_a_log"][l][j]
    pcol[:, 13] = inp["dn_dt_bias"][l][j]
    pcol[:, 14] = inp["dn_norm_w"][l]
    for tap in range(4):
        pcol[:, 15 + tap] = inp["lru_conv_w"][l][tap, sl]
    pcol[:, 19] = inp["lru_conv_b"][l][sl]
    pcol[:, 20] = inp["lru_b_a"][l].reshape(512)[sl]
    pcol[:, 21] = inp["lru_b_x"][l].reshape(512)[sl]
    pcol[:, 22] = inp["lru_lam"][l][sl]
    pcol[:, 35] = inp["s5_d"][l].reshape(512)[sl]
    lruw = np.zeros((128, 2, 128), np.float32)
    for n in range(2):
        lruw[n * 64:(n + 1) * 64, 0, n * 64:(n + 1) * 64] = inp["lru_w_a"][l][2 * j + n]
        lruw[n * 64:(n + 1) * 64, 1, n * 64:(n + 1) * 64] = inp["lru_w_x"][l][2 * j + n]
    s5row = np.zeros((128, 3, 512), np.float32)
    s5b = np.zeros((128, 2, 4, 128), np.float32)
    s5c = np.zeros((128, 2, 4, 128), np.float32)
    for m in range(4):
        for gl in range(2):
            gloc = 2 * m + gl
            g = 8 * j + gloc
            rs = slice(gl * 64, (gl + 1) * 64)
            pcol[rs, 23 + m] = inp["s5_lam_re"][l][g]
            pcol[rs, 27 + m] = inp["s5_lam_im"][l][g]
            pcol[rs, 31 + m] = inp["s5_log_step"][l][g]
            cs_ = slice(m * 128 + gl * 64, m * 128 + (gl + 1) * 64)
            s5row[:, 0, cs_] = inp["s5_lam_re"][l][g][None, :]
            s5row[:, 1, cs_] = inp["s5_lam_im"][l][g][None, :]
            s5row[:, 2, cs_] = inp["s5_log_step"][l][g]
            ch = slice(gloc * 16, (gloc + 1) * 16)
            s5b[ch, 0, m, rs] = inp["s5_b_re"][l][g].T
            s5b[ch, 1, m, rs] = inp["s5_b_im"][l][g].T
            s5c[rs, 0, m, ch] = inp["s5_c_re"][l][g].T
            s5c[rs, 1, m, ch] = inp["s5_c_im"][l][g].T
    return {"pcol": pcol, "lruw": lruw, "s5row": s5row, "s5b": s5b, "s5c": s5c}


def kb_acts(projT, j):
    q = projT[0 + j * 128: 0 + (j + 1) * 128]
    k = projT[512 + j * 128: 512 + (j + 1) * 128]
    v = projT[1024 + j * 128: 1024 + (j + 1) * 128]
    z = projT[1536 + j * 128: 1536 + (j + 1) * 128]
    return {"qkvz": np.ascontiguousarray(np.stack([q, k, v, z])),
            "bd": np.ascontiguousarray(np.stack([projT[2048 + j], projT[2052 + j]])),
            "s5u": np.ascontiguousarray(projT[2056 + j * 128: 2056 + (j + 1) * 128]),
            "lrx": np.ascontiguousarray(projT[2568 + j * 128: 2568 + (j + 1) * 128]),
            "lry": np.ascontiguousarray(projT[3080 + j * 128: 3080 + (j + 1) * 128])}


_PROGS = {}


def _prog(name, fn):
    if name not in _PROGS:
        _PROGS[name] = fn()
    return _PROGS[name]


def _run(nc, in_maps):
    return run_bass_kernel_spmd(nc, in_maps, core_ids=list(range(NCORE))).results


def kernel(**inp):
    inp = {k: np.asarray(v) for k, v in inp.items()}
    x = inp["x"].astype(np.float32, copy=False)
    TOKC = SEQ // 4
    xT = [np.ascontiguousarray(x[b].T) for b in range(BATCH)]
    cst = kb_consts()
    for l in range(DEPTH):
        w = np.ascontiguousarray(inp["w_in"][l][:, :NMIX])
        ins = []
        for core in range(NCORE):
            b, q = core // 4, core % 4
            ins.append({"xT": np.ascontiguousarray(xT[b][:, q * TOKC:(q + 1) * TOKC]), "w": w})
        res = _run(_prog("KA", lambda: build_KA(TOKC, NMIX)), ins)
        projT = [np.concatenate([res[b * 4 + q]["pT"] for q in range(4)], axis=1) for b in range(BATCH)]
        ins = []
        for core in range(NCORE):
            b, j = core // 4, core % 4
            d = dict(cst)
            d.update(kb_params(inp, l, j))
            d.update(kb_acts(projT[b], j))
            ins.append(d)
        res = _run(_prog("KB", lambda: build_KB(SEQ)), ins)
        brT = []
        for b in range(BATCH):
            parts = [[res[b * 4 + j]["o"][g] for j in range(4)] for g in range(3)]
            brT.append(np.concatenate([np.concatenate(p, axis=0) for p in parts], axis=0))
        del projT
        par = kc_params(inp, l)
        ins = []
        for core in range(NCORE):
            b, q = core // 4, core % 4
            t0 = q * TOKC
            d = dict(par)
            xs = np.zeros((D, HW + TOKC), np.float32)
            bs = np.zeros((1536, HW + TOKC), np.float32)
            lo = max(0, t0 - HW)
            xs[:, HW - (t0 - lo):] = xT[b][:, lo:t0 + TOKC]
            bs[:, HW - (t0 - lo):] = brT[b][:, lo:t0 + TOKC]
            d["xT"], d["brT"] = xs, bs
            d["flag"] = np.full((128, 1), 0.0 if t0 == 0 else 1.0, np.float32)
            ins.append(d)
        res = _run(_prog("KC", lambda: build_KC(TOKC)), ins)
        xT = [np.concatenate([res[b * 4 + q]["x2T"] for q in range(4)], axis=1) for b in range(BATCH)]
    return np.ascontiguousarray(np.stack([xT[b].T for b in range(BATCH)])).astype(np.float32)
```
